# Optimizing a Trainium2 kernel written in Bass

```python
import math
import jax
import jax.numpy as jnp
from jax import lax
import numpy as np

D_MODEL = 1024
BATCH = 8
SEQ = 4096
DEPTH = 2

MEM_LEN = 256
N_EVEN = (DEPTH + 1) // 2
N_ODD = DEPTH // 2
NORM_EPS = 1e-6
CONV_W = 4

W_A = D_MODEL
A_HEADS = 8
A_HEAD_DIM = W_A // A_HEADS
LRU_C = 8.0
W_B = D_MODEL
B_HEADS = 8
B_HEAD_DIM = W_B // B_HEADS
B_CHUNK = 32
E_SPLITS = [W_A, 2 * W_A, 2 * W_A + W_B, 2 * W_A + 2 * W_B, 2 * W_A + 3 * W_B]
E_IN = 2 * W_A + 4 * W_B

D_INNER = 2 * D_MODEL
M_HEAD_DIM = 64
M_HEADS = D_INNER // M_HEAD_DIM
M_GROUPS = 8
M_HPG = M_HEADS // M_GROUPS
D_STATE = 128
M_CHUNK = 128
CONV_DIM = D_INNER + 2 * M_GROUPS * D_STATE
O_SPLITS = [D_INNER, D_INNER + CONV_DIM]
O_IN = D_INNER + CONV_DIM + M_HEADS

X_HEADS = 4
X_HEAD_DIM = D_MODEL // X_HEADS
D_FF = 4 * D_MODEL

kernel_name = 'hybrid_rglru_hgrn2_ssd_memxattn'


def rmsnorm(x, g):
    xf = x.astype(jnp.float32)
    y = xf * lax.rsqrt(jnp.mean(xf * xf, axis=-1, keepdims=True) + NORM_EPS)
    return (y * g.astype(jnp.float32)).astype(x.dtype)


def group_rmsnorm(xf, g, n_groups):
    shp = xf.shape
    xg = xf.reshape(shp[:-1] + (n_groups, shp[-1] // n_groups))
    xg = xg * lax.rsqrt(jnp.mean(xg * xg, axis=-1, keepdims=True) + NORM_EPS)
    return xg.reshape(shp) * g.astype(jnp.float32)


def causal_dwconv(x, w, b):
    k, c = w.shape
    y = lax.conv_general_dilated(x, w[:, None, :].astype(x.dtype), window_strides=(1,),
                                 padding=[(k - 1, 0)], dimension_numbers=('NWC', 'WIO', 'NWC'),
                                 feature_group_count=c)
    return y + b.astype(x.dtype)


def _lin_combine(c1, c2):
    a1, b1 = c1
    a2, b2 = c2
    return a1 * a2, a2 * b1 + b2


def rg_lru(x, w_r, b_r, w_i, b_i, lam):
    bsz, s, _ = x.shape
    xf = x.astype(jnp.float32)
    xh = xf.reshape(bsz, s, A_HEADS, A_HEAD_DIM)
    r = jax.nn.sigmoid(jnp.einsum('bshi,hij->bshj', xh, w_r.astype(jnp.float32)).reshape(bsz, s, W_A)
                       + b_r.astype(jnp.float32))
    i = jax.nn.sigmoid(jnp.einsum('bshi,hij->bshj', xh, w_i.astype(jnp.float32)).reshape(bsz, s, W_A)
                       + b_i.astype(jnp.float32))
    log_a = -LRU_C * r * jax.nn.softplus(-lam.astype(jnp.float32))
    a = jnp.exp(log_a)
    u = jnp.sqrt(-jnp.expm1(2.0 * log_a)) * (i * xf)
    _, h = lax.associative_scan(_lin_combine, (a, u), axis=1)
    return h


def hgrn2_mix(q, f_logit, v, g, lb, norm_g):
    bsz, s, _ = q.shape
    nc = s // B_CHUNK
    qf = jax.nn.silu(q.astype(jnp.float32))
    f = lb + (1.0 - lb) * jax.nn.sigmoid(f_logit.astype(jnp.float32))
    k = 1.0 - f
    logf = jnp.log(f)

    def to_chunks(t):
        return t.reshape(bsz, nc, B_CHUNK, B_HEADS, B_HEAD_DIM).transpose(1, 0, 3, 2, 4)

    mask = jnp.tril(jnp.ones((B_CHUNK, B_CHUNK), dtype=bool))[None, None, :, :, None]

    def step(state, inp):
        qc, kc, vc, gc = inp
        bc = jnp.cumsum(gc, axis=2)
        o_inter = jnp.einsum('bhtk,bhkv->bhtv', qc * jnp.exp(bc), state)
        diff = bc[:, :, :, None, :] - bc[:, :, None, :, :]
        decay = jnp.where(mask, jnp.exp(jnp.where(mask, diff, 0.0)), 0.0)
        scores = jnp.einsum('bhtk,bhtsk,bhsk->bhts', qc, decay, kc)
        o_intra = jnp.einsum('bhts,bhsv->bhtv', scores, vc)
        b_last = bc[:, :, -1]
        k_dec = kc * jnp.exp(b_last[:, :, None, :] - bc)
        new_state = jnp.exp(b_last)[..., None] * state + jnp.einsum('bhsk,bhsv->bhkv', k_dec, vc)
        return new_state, o_inter + o_intra

    state0 = jnp.zeros((bsz, B_HEADS, B_HEAD_DIM, B_HEAD_DIM), jnp.float32)
    _, o = lax.scan(step, state0, (to_chunks(qf), to_chunks(k), to_chunks(v.astype(jnp.float32)),
                                   to_chunks(logf)))
    o = o.transpose(1, 0, 3, 2, 4).reshape(bsz, s, B_HEADS, B_HEAD_DIM)
    o = o * lax.rsqrt(jnp.mean(o * o, axis=-1, keepdims=True) + NORM_EPS)
    o = o.reshape(bsz, s, W_B) * norm_g.astype(jnp.float32) * jax.nn.silu(g.astype(jnp.float32))
    return o


def ssd_chunked(xdt, da, bm, cm):
    bsz, s = xdt.shape[:2]
    nc = s // M_CHUNK
    xc_all = jnp.moveaxis(xdt.reshape(bsz, nc, M_CHUNK, M_GROUPS, M_HPG, M_HEAD_DIM), 1, 0)
    ac_all = jnp.moveaxis(da.reshape(bsz, nc, M_CHUNK, M_GROUPS, M_HPG), 1, 0)
    bc_all = jnp.moveaxis(bm.reshape(bsz, nc, M_CHUNK, M_GROUPS, D_STATE), 1, 0)
    cc_all = jnp.moveaxis(cm.reshape(bsz, nc, M_CHUNK, M_GROUPS, D_STATE), 1, 0)
    mask = jnp.tril(jnp.ones((M_CHUNK, M_CHUNK), dtype=bool))[None, :, :, None, None]

    def step(state, inp):
        xc, ac, bc, cc = inp
        acum = jnp.cumsum(ac, axis=1)
        seg = acum[:, :, None] - acum[:, None, :]
        lmat = jnp.where(mask, jnp.exp(jnp.where(mask, seg, 0.0)), 0.0)
        cb = jnp.einsum('btgn,bsgn->btsg', cc, bc)
        y_intra = jnp.einsum('btsg,btsgr,bsgrp->btgrp', cb, lmat, xc)
        y_inter = jnp.einsum('btgn,btgr,bgrpn->btgrp', cc, jnp.exp(acum), state)
        a_last = acum[:, -1]
        w_s = jnp.exp(a_last[:, None] - acum)
        new_state = (jnp.exp(a_last)[..., None, None] * state
                     + jnp.einsum('bsgn,bsgr,bsgrp->bgrpn', bc, w_s, xc))
        return new_state, y_intra + y_inter

    state0 = jnp.zeros((bsz, M_GROUPS, M_HPG, M_HEAD_DIM, D_STATE), jnp.float32)
    _, y = lax.scan(step, state0, (xc_all, ac_all, bc_all, cc_all))
    return jnp.moveaxis(y, 0, 1).reshape(bsz, s, M_HEADS, M_HEAD_DIM)


def mamba2_mix(z, xm, bm, cm, dt_raw, dt_bias, a_log, d_skip, norm_g):
    bsz, s, _ = xm.shape
    dt = jax.nn.softplus(dt_raw.astype(jnp.float32) + dt_bias.astype(jnp.float32))
    a = -jnp.exp(a_log.astype(jnp.float32))
    xh = xm.astype(jnp.float32).reshape(bsz, s, M_HEADS, M_HEAD_DIM)
    y = ssd_chunked(xh * dt[..., None], dt * a,
                    bm.astype(jnp.float32).reshape(bsz, s, M_GROUPS, D_STATE),
                    cm.astype(jnp.float32).reshape(bsz, s, M_GROUPS, D_STATE))
    y = y + d_skip.astype(jnp.float32)[:, None] * xh
    y = y.reshape(bsz, s, D_INNER) * jax.nn.silu(z.astype(jnp.float32))
    return group_rmsnorm(y, norm_g, M_GROUPS)


def mem_cross_attn(h, mem_n, wq, wk, wv, wo):
    bsz, s, _ = h.shape
    m = mem_n.shape[1]
    q = (h @ wq).reshape(bsz, s, X_HEADS, X_HEAD_DIM)
    k = (mem_n @ wk).reshape(bsz, m, X_HEADS, X_HEAD_DIM)
    v = (mem_n @ wv).reshape(bsz, m, X_HEADS, X_HEAD_DIM)
    sc = jnp.einsum('bshd,bmhd->bhsm', q, k).astype(jnp.float32) * (X_HEAD_DIM ** -0.5)
    p = jax.nn.softmax(sc, axis=-1).astype(v.dtype)
    o = jnp.einsum('bhsm,bmhd->bshd', p, v).reshape(bsz, s, D_MODEL)
    return o @ wo


def setup_inputs(seed: int = 0) -> dict:
    key = jax.random.key(seed)
    ks = iter(jax.random.split(key, 48))

    def nrm(shape, scale):
        return jax.random.normal(next(ks), shape, jnp.float32) * scale

    def gain(shape):
        return 1.0 + nrm(shape, 0.01)

    d = D_MODEL
    inp = {}
    inp['x'] = nrm((BATCH, SEQ, d), 1.0)
    inp['mem'] = nrm((BATCH, MEM_LEN, d), 1.0)
    inp['norm_mix_g'] = gain((DEPTH, d))
    inp['norm_mem_q_g'] = gain((DEPTH, d))
    inp['norm_mem_kv_g'] = gain((DEPTH, d))
    inp['norm_ffn_g'] = gain((DEPTH, d))
    inp['final_norm_g'] = gain((d,))
    inp['e_w_in'] = nrm((N_EVEN, d, E_IN), d ** -0.5)
    inp['a_conv_w'] = nrm((N_EVEN, CONV_W, W_A), CONV_W ** -0.5)
    inp['a_conv_b'] = nrm((N_EVEN, W_A), 0.01)
    inp['a_gate_r_w'] = nrm((N_EVEN, A_HEADS, A_HEAD_DIM, A_HEAD_DIM), A_HEAD_DIM ** -0.5)
    inp['a_gate_r_b'] = nrm((N_EVEN, W_A), 0.01)
    inp['a_gate_i_w'] = nrm((N_EVEN, A_HEADS, A_HEAD_DIM, A_HEAD_DIM), A_HEAD_DIM ** -0.5)
    inp['a_gate_i_b'] = nrm((N_EVEN, W_A), 0.01)
    a8 = jax.random.uniform(next(ks), (N_EVEN, W_A), jnp.float32, 0.9, 0.999)
    a_base = a8 ** (1.0 / LRU_C)
    inp['a_lambda'] = jnp.log(a_base) - jnp.log1p(-a_base)
    inp['b_lb_logits'] = nrm((DEPTH + 1, W_B), 0.1)
    inp['b_norm_g'] = gain((N_EVEN, W_B))
    inp['e_w_out'] = nrm((N_EVEN, W_A + W_B, d), (W_A + W_B) ** -0.5)
    inp['o_w_in'] = nrm((N_ODD, d, O_IN), d ** -0.5)
    inp['m_conv_w'] = nrm((N_ODD, CONV_W, CONV_DIM), CONV_W ** -0.5)
    inp['m_conv_b'] = nrm((N_ODD, CONV_DIM), 0.01)
    u = jax.random.uniform(next(ks), (N_ODD, M_HEADS), jnp.float32)
    dt0 = jnp.exp(u * (math.log(0.1) - math.log(0.001)) + math.log(0.001))
    inp['m_dt_bias'] = dt0 + jnp.log(-jnp.expm1(-dt0))
    inp['m_a_log'] = jnp.log(jax.random.uniform(next(ks), (N_ODD, M_HEADS), jnp.float32, 1.0, 16.0))
    inp['m_d'] = 1.0 + nrm((N_ODD, M_HEADS), 0.1)
    inp['m_norm_g'] = gain((N_ODD, D_INNER))
    inp['o_w_out'] = nrm((N_ODD, D_INNER, d), D_INNER ** -0.5)
    inp['xq_w'] = nrm((DEPTH, d, d), d ** -0.5)
    inp['xk_w'] = nrm((DEPTH, d, d), d ** -0.5)
    inp['xv_w'] = nrm((DEPTH, d, d), d ** -0.5)
    inp['xo_w'] = nrm((DEPTH, d, d), d ** -0.5)
    inp['ffn_w1'] = nrm((DEPTH, d, D_FF), d ** -0.5)
    inp['ffn_w2'] = nrm((DEPTH, D_FF, d), D_FF ** -0.5)
    return inp


def reference(x, mem, norm_mix_g, norm_mem_q_g, norm_mem_kv_g, norm_ffn_g, final_norm_g,
              e_w_in, a_conv_w, a_conv_b, a_gate_r_w, a_gate_r_b, a_gate_i_w, a_gate_i_b,
              a_lambda, b_lb_logits, b_norm_g, e_w_out,
              o_w_in, m_conv_w, m_conv_b, m_dt_bias, m_a_log, m_d, m_norm_g, o_w_out,
              xq_w, xk_w, xv_w, xo_w, ffn_w1, ffn_w2):
    lb_table = jnp.cumsum(jax.nn.softmax(b_lb_logits.astype(jnp.float32), axis=0), axis=0)
    for l in range(DEPTH):
        h = rmsnorm(x, norm_mix_g[l])
        if l % 2 == 0:
            e = l // 2
            proj = h @ e_w_in[e]
            xa, ga, qb, fb, ib, gb = jnp.split(proj, E_SPLITS, axis=-1)
            xa = causal_dwconv(xa, a_conv_w[e], a_conv_b[e])
            ya = rg_lru(xa, a_gate_r_w[e], a_gate_r_b[e], a_gate_i_w[e], a_gate_i_b[e], a_lambda[e])
            ya = ya * jax.nn.gelu(ga.astype(jnp.float32))
            yb = hgrn2_mix(qb, fb, ib, gb, lb_table[l], b_norm_g[e])
            mix = jnp.concatenate([ya, yb], axis=-1).astype(x.dtype) @ e_w_out[e]
        else:
            o = l // 2
            proj = h @ o_w_in[o]
            z, xbc, dt_raw = jnp.split(proj, O_SPLITS, axis=-1)
            xbc = jax.nn.silu(causal_dwconv(xbc, m_conv_w[o], m_conv_b[o]))
            xm, bm, cm = jnp.split(xbc, [D_INNER, D_INNER + M_GROUPS * D_STATE], axis=-1)
            ym = mamba2_mix(z, xm, bm, cm, dt_raw, m_dt_bias[o], m_a_log[o], m_d[o], m_norm_g[o])
            mix = ym.astype(x.dtype) @ o_w_out[o]
        x = x + mix
        x = x + mem_cross_attn(rmsnorm(x, norm_mem_q_g[l]), rmsnorm(mem, norm_mem_kv_g[l]),
                               xq_w[l], xk_w[l], xv_w[l], xo_w[l])
        hf = rmsnorm(x, norm_ffn_g[l])
        x = x + jnp.square(jax.nn.relu(hf @ ffn_w1[l])) @ ffn_w2[l]
    return rmsnorm(x, final_norm_g)
```

```python
import numpy as np
from contextlib import ExitStack
import concourse.bass as bass
import concourse.mybir as mybir
from concourse.bass_utils import run_bass_kernel_spmd

F32 = mybir.dt.float32
BF16 = mybir.dt.bfloat16
AF = mybir.ActivationFunctionType
ALU = mybir.AluOpType
D = 1024
TT = 512
EPS = 1e-6
MEM = 256

PCI = {}
_n = 0
for _nm, _c in [("gmix", 16), ("gmq", 16), ("gmkv", 16), ("gffn", 16), ("gfin", 8), ("acw", 32), ("acb", 8),
                ("arb", 8), ("aib", 8), ("alam", 8), ("lbl", 24), ("bng", 8), ("mcw", 128), ("mcb", 32),
                ("dtb", 1), ("alog", 1)]:
    PCI[_nm] = _n
    _n += _c
NPC = _n
C_ID, C_MH, C_MC, C_R64, C_R128 = 0, 128, 256, 384, 896
NCST = 1408

WSHAPES = {"e_w_in": (1024, 6144), "e_w_out": (2048, 1024), "o_w_in": (1024, 6176), "o_w_out": (2048, 1024),
           "xq0": (1024, 1024), "xk0": (1024, 1024), "xv0": (1024, 1024), "xo0": (1024, 1024),
           "xq1": (1024, 1024), "xk1": (1024, 1024), "xv1": (1024, 1024), "xo1": (1024, 1024),
           "w1_0": (1024, 4096), "w2_0": (4096, 1024), "w1_1": (1024, 4096), "w2_1": (4096, 1024)}
WORDER = ["xk0", "xv0", "xk1", "xv1", "e_w_in", "e_w_out", "xq0", "xo0", "w1_0", "w2_0",
          "o_w_in", "o_w_out", "xq1", "xo1", "w1_1", "w2_1"]


class Reg:
    __slots__ = ("w", "r")

    def __init__(self):
        self.w = None
        self.r = {}


class Trk:
    def __init__(self, sem, inc):
        self.sem = sem
        self.inc = inc
        self.n = 0


class Eng:
    def __init__(self, eng, trk, is_pe=False):
        self.eng = eng
        self.trk = trk
        self.seen = {}
        self.is_pe = is_pe


class View:
    def __init__(self, kb, b0, dtype, shape):
        self.kb = kb
        self.b0 = b0
        self.es = 4 if dtype == F32 else 2
        self.shape = tuple(shape)
        n = int(np.prod(shape))
        self.nbytes = n * self.es
        base = kb.arena[:, b0 // 4:(b0 + self.nbytes) // 4]
        ap = base if dtype == F32 else base.bitcast(BF16)
        if len(shape) == 2:
            ap = ap.rearrange("p (a b) -> p a b", a=shape[0])
        elif len(shape) == 3:
            ap = ap.rearrange("p (a b c) -> p a b c", a=shape[0], b=shape[1])
        self.ap = ap
        self.slot = self.nbytes // shape[0] if len(shape) > 1 else self.nbytes

    def r(self, i=None, j=None):
        G = self.kb.G
        if i is None:
            lo, hi = self.b0, self.b0 + self.nbytes
        else:
            if j is None:
                j = i + 1
            lo, hi = self.b0 + i * self.slot, self.b0 + j * self.slot
        return self.kb.regs[lo // G:(hi + G - 1) // G]

    def rb(self, lo_el, hi_el):
        G = self.kb.G
        lo, hi = self.b0 + lo_el * self.es, self.b0 + hi_el * self.es
        return self.kb.regs[lo // G:(hi + G - 1) // G]


def build(S, stop_after=None, dbg=None):
    NT = S // TT
    nc = bass.Bass("TRN2", target_bir_lowering=False)
    kb = type("KB", (), {})()
    es = ExitStack()
    E_ = es.enter_context

    def din(name, shape, dt=F32):
        return nc.dram_tensor(name, list(shape), dt, kind="ExternalInput").ap()

    x_d = din("x", [S, D])
    mem_d = din("mem", [MEM, D])
    pc_d = din("pc", [128, NPC])
    cst_d = din("cst", [128, NCST])
    gr_d = din("gr", [128, 1024])
    gi_d = din("gi", [128, 1024])
    mng_d = din("mng", [1, 2048])
    md_d = din("md", [1, 32])
    _stage_w = {"load": [], "rglru": ["e_w_in"], "hgA": [], "hgB": [], "l0": ["e_w_in", "e_w_out"], "a0": ["xq0", "xo0"], "f0": ["w1_0", "w2_0"],
                "l1": ["o_w_in", "o_w_out"], "a1": ["xq1", "xo1"], "f1": ["w1_1", "w2_1"]}
    wneed = ["xk0", "xv0", "xk1", "xv1"]
    for _st in ["load", "rglru", "hgA", "hgB", "l0", "a0", "f0", "l1", "a1", "f1"]:
        wneed += _stage_w[_st]
        if stop_after == _st:
            break
    wneed = [k for k in WORDER if k in set(wneed)]
    wf = {k: din(k, WSHAPES[k]) for k in wneed}
    wb = {k: nc.dram_tensor(k + "_b", list(WSHAPES[k]), BF16, kind="Internal").ap() for k in wneed}
    out_d = nc.dram_tensor("out", [S, D], F32, kind="ExternalOutput").ap()
    dbg_d = {}
    if dbg:
        for nm, shp in dbg.items():
            dbg_d[nm] = nc.dram_tensor("dbg_" + nm, list(shp), F32, kind="ExternalOutput").ap()

    ARENA_BYTES = 206 * 1024
    kb.G = 512
    kb.arena = E_(nc.sbuf_tensor("arena", [128, ARENA_BYTES // 4], F32))[:]
    kb.regs = [Reg() for _ in range(ARENA_BYTES // kb.G)]
    kb.off = 0
    kb.dbgt = []

    def alloc(shape, dtype=F32):
        v = View(kb, kb.off, dtype, shape)
        kb.off += (v.nbytes + kb.G - 1) // kb.G * kb.G
        assert kb.off <= ARENA_BYTES, f"SBUF arena overflow {kb.off}"
        return v

    def mksem(name):
        return E_(nc.semaphore(name))

    PE = Eng(nc.tensor, Trk(mksem("s_pe"), 1), is_pe=True)
    ACT = Eng(nc.scalar, Trk(mksem("s_act"), 1))
    DVE = Eng(nc.vector, Trk(mksem("s_dve"), 1))
    POOL = Eng(nc.gpsimd, Trk(mksem("s_pool"), 1))
    SP = Eng(nc.sync, None)
    kb.nsem = 0

    def newtrk():
        kb.nsem += 1
        return Trk(mksem(f"s_dma{kb.nsem}"), 16)
    NCAST = 6
    T_CAST = [newtrk() for _ in range(NCAST)]
    T_XIN = newtrk()
    T_OUT = newtrk()
    counts = {"wait": 0, "ins": 0}

    def op(E, fn, r=(), w=(), trk=None):
        trk = trk or E.trk
        need = {}
        for g in r:
            if g.w is not None:
                t, c = g.w
                if need.get(t, 0) < c:
                    need[t] = c
        for g in w:
            if g.w is not None:
                t, c = g.w
                if need.get(t, 0) < c:
                    need[t] = c
            for t, c in g.r.items():
                if need.get(t, 0) < c:
                    need[t] = c
        for t, c in need.items():
            if E.is_pe and t is E.trk:
                continue
            if E.seen.get(t, 0) < c:
                E.eng.wait_ge(t.sem, c * t.inc)
                E.seen[t] = c
                counts["wait"] += 1
        ins = fn()
        trk.n += 1
        ins.then_inc(trk.sem, trk.inc)
        counts["ins"] += 1
        n = trk.n
        for g in r:
            if g.r.get(trk, 0) < n:
                g.r[trk] = n
        for g in w:
            g.w = (trk, n)
            g.r = {}

    PSt = [E_(nc.psum_tensor(f"ps{i}", [128, 512], F32)) for i in range(8)]
    PS = [t[:] for t in PSt]
    PSB = [t[:].bitcast(BF16) for t in PSt]
    PR = [[Reg()] for _ in range(8)]
    kb.pb = 0

    def bank(pool=(0, 1, 2, 3, 4, 5, 6, 7)):
        kb.pb = (kb.pb + 1) % len(pool)
        return pool[kb.pb]

    def mm(out, lhsT, rhs, start, stop, r, w):
        op(PE, lambda: nc.tensor.matmul(out, lhsT=lhsT, rhs=rhs, start=start, stop=stop), r, w)

    def tr(out, in_, ident, r, w):
        op(PE, lambda: nc.tensor.transpose(out=out, in_=in_, identity=ident), r, w)

    def act(out, in_, func, r, w, bias=None, scale=None):
        kw = {}
        if bias is not None:
            kw["bias"] = bias
        if scale is not None:
            kw["scale"] = scale
        op(ACT, lambda: nc.scalar.activation(out=out, in_=in_, func=func, **kw), r, w)

    def ts(E, out, in0, s1, s2, op0, op1, r, w):
        if s2 is None:
            op(E, lambda: E.eng.tensor_scalar(out=out, in0=in0, scalar1=s1, scalar2=None, op0=op0), r, w)
        else:
            op(E, lambda: E.eng.tensor_scalar(out=out, in0=in0, scalar1=s1, scalar2=s2, op0=op0, op1=op1), r, w)

    def tt(E, out, in0, in1, o, r, w):
        op(E, lambda: E.eng.tensor_tensor(out=out, in0=in0, in1=in1, op=o), r, w)

    def stt(out, in0, scalar, in1, op0, op1, r, w):
        op(DVE, lambda: nc.vector.scalar_tensor_tensor(out=out, in0=in0, scalar=scalar, in1=in1, op0=op0, op1=op1), r, w)

    def scan(out, d0, d1, initial, r, w):
        op(DVE, lambda: nc.vector.tensor_tensor_scan(out=out, data0=d0, data1=d1, initial=initial,
                                                     op0=ALU.mult, op1=ALU.add), r, w)

    def cp(E, out, in_, r, w):
        if E is ACT:
            op(ACT, lambda: nc.scalar.activation(out=out, in_=in_, func=AF.Copy), r, w)
        else:
            op(E, lambda: E.eng.tensor_copy(out=out, in_=in_), r, w)

    def recip(out, in_, r, w):
        op(DVE, lambda: nc.vector.reciprocal(out=out, in_=in_), r, w)

    def dma(E, trk, out, in_, r, w):
        op(E, lambda: E.eng.dma_start(out=out, in_=in_), r, w, trk=trk)

    WBR = {k: [Reg() for _ in range(v[0] // 128)] for k, v in WSHAPES.items()}

    PC = alloc([NPC])
    CST = alloc([NCST])
    IDB = alloc([128], BF16)
    ONES = alloc([128], BF16)
    EPSC = alloc([1])
    NG = alloc([2048])
    DBC = alloc([32])
    GR = alloc([8, 128], BF16)
    GI = alloc([8, 128], BF16)
    CL = alloc([8])
    CL2 = alloc([8])
    LB = alloc([8])
    OML = alloc([8])
    ANEG = alloc([1])
    KF = [alloc([8, MEM], BF16) for _ in range(2)]
    VM = [alloc([2, 1024], BF16) for _ in range(2)]
    X = alloc([8, TT])
    H = alloc([8, TT], BF16)
    Y = alloc([16, TT], BF16)
    RS = alloc([TT])
    CAR0 = alloc([8, 4])
    HST = alloc([8])
    ST0 = alloc([8, 128])
    CAR1 = alloc([32, 4])
    ST1 = alloc([2048])
    NRING = 5
    RING = [alloc([8 * 512], BF16) for _ in range(NRING)]
    T_RING = [newtrk() for _ in range(NRING)]
    kb.ring = 0
    pers_off = kb.off

    def pc(name, i):
        c = PCI[name] + i
        return PC.ap[:, c:c + 1]

    IDF = CST.ap[:, C_ID:C_ID + 128]
    MASKH = CST.ap[:, C_MH:C_MH + 128]
    MASKC = CST.ap[:, C_MC:C_MC + 128]
    R64 = CST.ap[:, C_R64:C_R64 + 512]
    R128 = CST.ap[:, C_R128:C_R128 + 512]

    class Slab:
        pass

    def wload(name, k0, kc, c0, ncols):
        assert kc * ncols * 2 <= 8192
        v = RING[kb.ring]
        trk_ = T_RING[kb.ring]
        kb.ring = (kb.ring + 1) % NRING
        s = Slab()
        base = v.ap[:, 0:kc * ncols]
        s.ap = base.rearrange("p (k n) -> p k n", k=kc)
        s.regs = v.rb(0, kc * ncols)
        src = wb[name][k0 * 128:(k0 + kc) * 128, c0:c0 + ncols].rearrange("(k p) n -> p k n", p=128)
        dma(SP, trk_, s.ap, src, r=WBR[name][k0:k0 + kc], w=s.regs)
        return s

    dma(SP, newtrk(), PC.ap, pc_d[:, :], [], PC.r())
    dma(SP, newtrk(), CST.ap, cst_d[:, :], [], CST.r())
    dma(SP, newtrk(), NG.ap, mng_d.partition_broadcast(128), [], NG.r())
    dma(SP, newtrk(), DBC.ap, md_d.partition_broadcast(128), [], DBC.r())
    kb.ncast = 0
    for name in wneed:
        K_, N_ = WSHAPES[name]
        for kb_ in range(K_ // 128):
            tc_ = T_CAST[kb.ncast % NCAST]
            kb.ncast += 1
            if tc_.n > 0:
                nc.gpsimd.wait_ge(tc_.sem, tc_.n * 16)
            dma(POOL, tc_, wb[name][kb_ * 128:(kb_ + 1) * 128, :], wf[name][kb_ * 128:(kb_ + 1) * 128, :],
                [], [WBR[name][kb_]])

    m0 = kb.off
    TMPF = alloc([1024])
    dma(SP, newtrk(), TMPF.ap, gr_d[:, :], [], TMPF.r())
    cp(DVE, GR.ap, TMPF.ap.rearrange("p (a b) -> p a b", a=8), TMPF.r(), GR.r())
    TMPG = alloc([1024])
    dma(SP, newtrk(), TMPG.ap, gi_d[:, :], [], TMPG.r())
    cp(DVE, GI.ap, TMPG.ap.rearrange("p (a b) -> p a b", a=8), TMPG.r(), GI.r())
    cp(DVE, IDB.ap, IDF, CST.r(), IDB.r())
    op(DVE, lambda: nc.vector.memset(ONES.ap, 1.0), [], ONES.r())
    op(DVE, lambda: nc.vector.memset(EPSC.ap, EPS), [], EPSC.r())
    for v in (CAR0, HST, ST0, CAR1, ST1):
        op(DVE, lambda v=v: nc.vector.memset(v.ap, 0.0), [], v.r())
    T8 = alloc([24])
    act(T8.ap[:, 0:8], PC.ap[:, PCI["alam"]:PCI["alam"] + 8], AF.Exp, PC.r(), T8.r(), scale=-1.0)
    act(T8.ap[:, 0:8], T8.ap[:, 0:8], AF.Ln, T8.r(), T8.r(), bias=1.0)
    ts(DVE, CL.ap, T8.ap[:, 0:8], -8.0, None, ALU.mult, None, T8.r(), CL.r())
    ts(DVE, CL2.ap, T8.ap[:, 0:8], -16.0, None, ALU.mult, None, T8.r(), CL2.r())
    act(T8.ap, PC.ap[:, PCI["lbl"]:PCI["lbl"] + 24], AF.Exp, PC.r(), T8.r())
    tt(DVE, LB.ap, T8.ap[:, 0:8], T8.ap[:, 8:16], ALU.add, T8.r(), LB.r())
    tt(DVE, LB.ap, LB.ap, T8.ap[:, 16:24], ALU.add, T8.r() + LB.r(), LB.r())
    recip(LB.ap, LB.ap, LB.r(), LB.r())
    tt(DVE, LB.ap, LB.ap, T8.ap[:, 0:8], ALU.mult, LB.r() + T8.r(), LB.r())
    ts(DVE, OML.ap, LB.ap, -1.0, 1.0, ALU.mult, ALU.add, LB.r(), OML.r())
    act(ANEG.ap, pc("alog", 0), AF.Exp, PC.r(), ANEG.r())
    ts(DVE, ANEG.ap, ANEG.ap, -1.0, None, ALU.mult, None, ANEG.r(), ANEG.r())

    def norm_fm(Xv, n, gname, goff, out, tmp_sq):
        for c in range(8):
            act(tmp_sq.ap[:, c, :n], Xv.ap[:, c, :n], AF.Square, Xv.r(c), tmp_sq.r(c))
        pb = bank()
        for c in range(8):
            mm(PS[pb][:, :n], ONES.ap, tmp_sq.ap[:, c, :n], c == 0, c == 7, tmp_sq.r(c) + ONES.r(), PR[pb])
        act(RS.ap[:, :n], PS[pb][:, :n], AF.Sqrt, PR[pb] + EPSC.r(), RS.r(), bias=EPSC.ap[:, 0:1], scale=1.0 / D)
        recip(RS.ap[:, :n], RS.ap[:, :n], RS.r(), RS.r())
        for c in range(8):
            stt(out.ap[:, c, :n], Xv.ap[:, c, :n], pc(gname, goff + c), RS.ap[:, :n], ALU.mult, ALU.mult,
                Xv.r(c) + RS.r() + PC.r(), out.r(c))

    def proj_fm(name, c0, nchunks, Hv, n, consumer, kc=8, k0=0):
        done = 0
        while done < nchunks:
            g = min(4, nchunks - done)
            ws = wload(name, k0, kc, c0 + done * 128, g * 128)
            for cc in range(g):
                pb = bank()
                for k in range(kc):
                    mm(PS[pb][:, :n], ws.ap[:, k, cc * 128:(cc + 1) * 128], Hv.ap[:, k, :n], k == 0, k == kc - 1,
                       ws.regs + Hv.r(k), PR[pb])
                consumer(done + cc, pb)
            done += g

    def outproj_add(name, Yv, kc):
        ncol = 8192 // (kc * 2)
        per = ncol // 128
        for g0 in range(0, 8, per):
            ws = wload(name, 0, kc, g0 * 128, ncol)
            for cc in range(per):
                c = g0 + cc
                pb = bank()
                for k in range(kc):
                    mm(PS[pb], ws.ap[:, k, cc * 128:(cc + 1) * 128], Yv.ap[:, k, :], k == 0, k == kc - 1,
                       ws.regs + Yv.r(k), PR[pb])
                tt(DVE, X.ap[:, c, :], X.ap[:, c, :], PS[pb], ALU.add, X.r(c) + PR[pb], X.r(c))

    def dump(nm, view_ap, regs):
        if nm in dbg_d:
            kb.dbgt.append(newtrk())
            dma(POOL, kb.dbgt[-1], dbg_d[nm], view_ap, regs, [])

    kb.off = m0
    MIN = alloc([2, 1024])
    MX = alloc([8, MEM])
    MH = alloc([8, MEM], BF16)
    MSQ = alloc([8, MEM], BF16)
    dma(SP, newtrk(), MIN.ap, mem_d.rearrange("(s p) f -> p s f", p=128), [], MIN.r())
    for c in range(8):
        pb = bank()
        for s in range(2):
            tr(PS[pb][:, s * 128:(s + 1) * 128], MIN.ap[:, s, c * 128:(c + 1) * 128], IDF, MIN.r(s) + CST.r(), PR[pb])
        cp(ACT, MX.ap[:, c, :], PS[pb][:, 0:MEM], PR[pb], MX.r(c))
    for l in range(2):
        norm_fm(MX, MEM, "gmkv", l * 8, MH, MSQ)

        def kcons(ci, pb, l=l):
            cp(ACT, KF[l].ap[:, ci, :], PS[pb][:, :MEM], PR[pb], KF[l].r(ci))
        proj_fm(f"xk{l}", 0, 8, MH, MEM, kcons)
        for sl in range(2):
            ws = wload(f"xv{l}", 0, 8, sl * 512, 512)
            for mh in range(2):
                pb = bank()
                for k in range(8):
                    mm(PS[pb], MH.ap[:, k, mh * 128:(mh + 1) * 128], ws.ap[:, k, :], k == 0, k == 7,
                       ws.regs + MH.r(k), PR[pb])
                cp(DVE, VM[l].ap[:, mh, sl * 512:(sl + 1) * 512], PS[pb], PR[pb], VM[l].rb(mh * 1024 + sl * 512, mh * 1024 + sl * 512 + 512))
    kb.off = pers_off

    def load_x(t):
        kb.off = pers_off
        IO = alloc([4, 1024])
        dma(SP, T_XIN, IO.ap, x_d[t * TT:(t + 1) * TT, :].rearrange("(s p) f -> p s f", p=128), [], IO.r())
        for c in range(8):
            pb = bank()
            for s in range(4):
                tr(PS[pb][:, s * 128:(s + 1) * 128], IO.ap[:, s, c * 128:(c + 1) * 128], IDF, IO.r(s) + CST.r(), PR[pb])
            cp(ACT if c % 2 else DVE, X.ap[:, c, :], PS[pb], PR[pb], X.r(c))

    def conv_chunk(pb, XAv, CARv, j, wname, wstride, bname, XCv_ap, XC_regs):
        cp(POOL, XAv.ap[:, 0:3], CARv.ap[:, j, 0:3], CARv.r(j), XAv.r())
        cp(ACT, XAv.ap[:, 3:515], PS[pb], PR[pb], XAv.r())
        cp(POOL, CARv.ap[:, j, 0:3], XAv.ap[:, 512:515], XAv.r(), CARv.r(j))
        ts(DVE, XCv_ap, XAv.ap[:, 3:515], pc(wname, 3 * wstride + j), pc(bname, j), ALU.mult, ALU.add,
           XAv.r() + PC.r(), XC_regs)
        for k in range(3):
            stt(XCv_ap, XAv.ap[:, k:k + 512], pc(wname, k * wstride + j), XCv_ap, ALU.mult, ALU.add,
                XAv.r() + XC_regs + PC.r(), XC_regs)

    def l0_mixer(t):
        kb.off = pers_off
        SQ = alloc([8, TT], BF16)
        norm_fm(X, TT, "gmix", 0, H, SQ)
        kb.off = pers_off
        F_ = [alloc([TT]) for _ in range(7)]
        GATE = alloc([4, TT])
        QS = alloc([4, TT])
        QT = alloc([4, TT], BF16)
        KT = alloc([4, TT], BF16)
        KDT = alloc([4, TT], BF16)
        VT = alloc([4, 512], BF16)
        XA = [alloc([516]) for _ in range(2)]
        XCB = alloc([TT], BF16)
        KD = alloc([TT], BF16)
        PT = alloc([4, 128], BF16)
        SB = [alloc([8, 128], BF16) for _ in range(2)]
        EL = alloc([4, 8])
        OSQ = alloc([TT], BF16)
        W = "e_w_in"

        for hf in range(2):
            def c_ga(ci, pb):
                act(GATE.ap[:, ci, :], PS[pb], AF.Gelu, PR[pb], GATE.r(ci))
            proj_fm(W, 1024 + hf * 512, 4, H, TT, c_ga)

            def c_xa(ci, pb, hf=hf):
                j = hf * 4 + ci
                XAb = XA[j % 2]
                XC, RR, II, AA, A2, HS = F_[0], F_[1], F_[2], F_[3], F_[4], F_[5]
                conv_chunk(pb, XAb, CAR0, j, "acw", 8, "acb", XC.ap, XC.r())
                cp(POOL, XCB.ap, XC.ap, XC.r(), XCB.r())
                p1 = bank()
                mm(PS[p1], GR.ap[:, j, :], XCB.ap, True, True, GR.r() + XCB.r(), PR[p1])
                act(RR.ap, PS[p1], AF.Sigmoid, PR[p1] + PC.r(), RR.r(), bias=pc("arb", j))
                p2 = bank()
                mm(PS[p2], GI.ap[:, j, :], XCB.ap, True, True, GI.r() + XCB.r(), PR[p2])
                act(II.ap, PS[p2], AF.Sigmoid, PR[p2] + PC.r(), II.r(), bias=pc("aib", j))
                act(AA.ap, RR.ap, AF.Exp, RR.r() + CL.r(), AA.r(), scale=CL.ap[:, j:j + 1])
                act(A2.ap, RR.ap, AF.Exp, RR.r() + CL2.r(), A2.r(), scale=CL2.ap[:, j:j + 1])
                act(A2.ap, A2.ap, AF.Sqrt, A2.r(), A2.r(), bias=1.0, scale=-1.0)
                tt(DVE, II.ap, II.ap, XC.ap, ALU.mult, II.r() + XC.r(), II.r())
                tt(DVE, II.ap, II.ap, A2.ap, ALU.mult, II.r() + A2.r(), II.r())
                scan(HS.ap, AA.ap, II.ap, HST.ap[:, j:j + 1], AA.r() + II.r() + HST.r(), HS.r())
                cp(POOL, HST.ap[:, j:j + 1], HS.ap[:, 511:512], HS.r(), HST.r())
                tt(POOL, Y.ap[:, j, :], HS.ap, GATE.ap[:, ci, :], ALU.mult, HS.r() + GATE.r(ci), Y.r(j))
            proj_fm(W, hf * 512, 4, H, TT, c_xa)
        dump("ya", Y.ap[:, 0:8, :], Y.r(0, 8))
        if stop_after == "rglru":
            return

        for hf in range(2):
            def c_g(ci, pb):
                act(GATE.ap[:, ci, :], PS[pb], AF.Silu, PR[pb], GATE.r(ci))
            proj_fm(W, 5120 + hf * 512, 4, H, TT, c_g)

            def c_q(ci, pb):
                act(QS.ap[:, ci, :], PS[pb], AF.Silu, PR[pb], QS.r(ci))
            proj_fm(W, 2048 + hf * 512, 4, H, TT, c_q)
            ws = wload(W, 0, 8, 4096 + hf * 512, 512)
            for s in range(4):
                pb = bank()
                for k in range(8):
                    mm(PS[pb], H.ap[:, k, s * 128:(s + 1) * 128], ws.ap[:, k, :], k == 0, k == 7, ws.regs + H.r(k), PR[pb])
                cp(ACT if s % 2 else DVE, VT.ap[:, s, :], PS[pb], PR[pb], VT.r(s))
            if stop_after == "hgA":
                return

            def c_f(ci, pb, hf=hf):
                j = hf * 4 + ci
                FF, LF, BB, E1, E2 = F_[0], F_[1], F_[2], F_[3], F_[4]
                act(FF.ap, PS[pb], AF.Sigmoid, PR[pb], FF.r())
                ts(DVE, FF.ap, FF.ap, OML.ap[:, j:j + 1], LB.ap[:, j:j + 1], ALU.mult, ALU.add, FF.r() + OML.r() + LB.r(), FF.r())
                act(LF.ap, FF.ap, AF.Ln, FF.r(), LF.r())
                scan(BB.ap, R64, LF.ap, 0.0, LF.r() + CST.r(), BB.r())
                act(E1.ap, BB.ap, AF.Exp, BB.r(), E1.r())
                act(E2.ap, BB.ap, AF.Exp, BB.r(), E2.r(), scale=-1.0)
                cp(POOL, EL.ap[:, ci, :], E1.ap.rearrange("p (c u) -> p c u", u=64)[:, :, 63], E1.r(), EL.r(ci))
                tt(DVE, QT.ap[:, ci, :], QS.ap[:, ci, :], E1.ap, ALU.mult, QS.r(ci) + E1.r(), QT.r(ci))
                ts(DVE, FF.ap, FF.ap, -1.0, 1.0, ALU.mult, ALU.add, FF.r(), FF.r())
                tt(DVE, KT.ap[:, ci, :], FF.ap, E2.ap, ALU.mult, FF.r() + E2.r(), KT.r(ci))
                tt(POOL, KD.ap.rearrange("p (c u) -> p c u", u=64), KT.ap[:, ci, :].rearrange("p (c u) -> p c u", u=64),
                   EL.ap[:, ci, :].unsqueeze(2).to_broadcast([128, 8, 64]), ALU.mult, KT.r(ci) + EL.r(ci), KD.r())
                pbt = bank()
                for s in range(4):
                    tr(PSB[pbt][:, s * 128:(s + 1) * 128], KD.ap[:, s * 128:(s + 1) * 128], IDB.ap, KD.r() + IDB.r(), PR[pbt])
                cp(ACT, KDT.ap[:, ci, :], PSB[pbt][:, 0:512], PR[pbt], KDT.r(ci))
            proj_fm(W, 3072 + hf * 512, 4, H, TT, c_f)
            if stop_after == "hgB":
                return

            for jj in range(4):
                j = hf * 4 + jj
                OF = F_[5 + (j % 2)]
                pbs = bank()
                for s in range(4):
                    sl_ = slice(s * 128, (s + 1) * 128)
                    mm(PS[pbs][:, sl_], KT.ap[:, jj, sl_], QT.ap[:, jj, sl_], s == 0, s == 3, KT.r(jj) + QT.r(jj), PR[pbs])
                tt(DVE, PT.ap, PS[pbs].rearrange("p (s t) -> p s t", s=4), MASKH.unsqueeze(1).to_broadcast([128, 4, 128]),
                   ALU.mult, PR[pbs] + CST.r(), PT.r())
                pd = [bank(), bank()]
                for c in range(8):
                    s, hh = c // 2, c % 2
                    mm(PS[pd[hh]][:, s * 128:(s + 1) * 128],
                       KDT.ap[hh * 64:(hh + 1) * 64, jj, s * 128:(s + 1) * 128],
                       VT.ap[hh * 64:(hh + 1) * 64, s, jj * 128:(jj + 1) * 128], s == 0, s == 3,
                       KDT.r(jj) + VT.r(s), PR[pd[hh]])
                po = bank()
                for s in range(4):
                    mm(PS[po][:, s * 128:(s + 1) * 128], VT.ap[:, s, jj * 128:(jj + 1) * 128], PT.ap[:, s, :], s == 0, False,
                       VT.r(s) + PT.r(), PR[po])
                SBj = SB[j % 2]
                cp(POOL, SBj.ap[:, 0, :], ST0.ap[:, j, :], ST0.r(j), SBj.r(0))
                for c in range(8):
                    mm(PS[po][:, c * 64:(c + 1) * 64], SBj.ap[:, c, :], QT.ap[:, jj, c * 64:(c + 1) * 64], False, c == 7,
                       SBj.r(c) + QT.r(jj), PR[po])
                    stt(ST0.ap[:, j, :], ST0.ap[:, j, :], EL.ap[:, jj, c:c + 1], PS[pd[c % 2]][:, (c // 2) * 128:(c // 2 + 1) * 128],
                        ALU.mult, ALU.add, ST0.r(j) + EL.r(jj) + PR[pd[c % 2]], ST0.r(j))
                    if c < 7:
                        cp(POOL, SBj.ap[:, c + 1, :], ST0.ap[:, j, :], ST0.r(j), SBj.r(c + 1))
                act(OF.ap, PS[po], AF.Copy, PR[po], OF.r())
                act(OSQ.ap, OF.ap, AF.Square, OF.r(), OSQ.r())
                pn = bank()
                mm(PS[pn], ONES.ap, OSQ.ap, True, True, ONES.r() + OSQ.r(), PR[pn])
                act(RS.ap, PS[pn], AF.Sqrt, PR[pn] + EPSC.r(), RS.r(), bias=EPSC.ap[:, 0:1], scale=1.0 / 128)
                recip(RS.ap, RS.ap, RS.r(), RS.r())
                stt(OF.ap, OF.ap, pc("bng", j), RS.ap, ALU.mult, ALU.mult, OF.r() + RS.r() + PC.r(), OF.r())
                tt(POOL, Y.ap[:, 8 + j, :], OF.ap, GATE.ap[:, jj, :], ALU.mult, OF.r() + GATE.r(jj), Y.r(8 + j))
        dump("yb", Y.ap[:, 8:16, :], Y.r(8, 16))
        outproj_add("e_w_out", Y, 16)

    def xattn(l):
        kb.off = pers_off
        QA = alloc([8, TT], BF16)
        EX = [alloc([2, TT], BF16) for _ in range(2)]
        RC = [alloc([TT]) for _ in range(2)]
        SQ = alloc([8, TT], BF16)
        norm_fm(X, TT, "gmq", l * 8, H, SQ)

        def c_q(ci, pb):
            act(QA.ap[:, ci, :], PS[pb], AF.Copy, PR[pb], QA.r(ci), scale=1.0 / 16.0)
        proj_fm(f"xq{l}", 0, 8, H, TT, c_q)
        for hd in range(4):
            EXh, RCh = EX[hd % 2], RC[hd % 2]
            for mh in range(2):
                pb = bank()
                for dc in range(2):
                    mm(PS[pb], KF[l].ap[:, 2 * hd + dc, mh * 128:(mh + 1) * 128], QA.ap[:, 2 * hd + dc, :], dc == 0, dc == 1,
                       KF[l].r(2 * hd + dc) + QA.r(2 * hd + dc), PR[pb])
                act(EXh.ap[:, mh, :], PS[pb], AF.Exp, PR[pb], EXh.r(mh))
            pb = bank()
            for mh in range(2):
                mm(PS[pb], ONES.ap, EXh.ap[:, mh, :], mh == 0, mh == 1, ONES.r() + EXh.r(mh), PR[pb])
            recip(RCh.ap, PS[pb], PR[pb], RCh.r())
            for dc in range(2):
                pb = bank()
                c = 2 * hd + dc
                for mh in range(2):
                    mm(PS[pb], VM[l].ap[:, mh, c * 128:(c + 1) * 128], EXh.ap[:, mh, :], mh == 0, mh == 1,
                       VM[l].r(mh) + EXh.r(mh), PR[pb])
                tt(DVE, Y.ap[:, c, :], PS[pb], RCh.ap, ALU.mult, PR[pb] + RCh.r(), Y.r(c))
        outproj_add(f"xo{l}", Y, 8)

    def ffn(l):
        kb.off = pers_off
        UP = alloc([32, TT], BF16)
        RL = [alloc([TT]) for _ in range(2)]
        SQ = alloc([8, TT], BF16)
        norm_fm(X, TT, "gffn", l * 8, H, SQ)

        def c_up(ci, pb):
            R_ = RL[ci % 2]
            act(R_.ap, PS[pb], AF.Relu, PR[pb], R_.r())
            tt(DVE, UP.ap[:, ci, :], R_.ap, PS[pb], ALU.mult, R_.r() + PR[pb], UP.r(ci))
        proj_fm(f"w1_{l}", 0, 32, H, TT, c_up)
        for c in range(8):
            ws = wload(f"w2_{l}", 0, 32, c * 128, 128)
            pb = bank()
            for k in range(32):
                mm(PS[pb], ws.ap[:, k, :], UP.ap[:, k, :], k == 0, k == 31, ws.regs + UP.r(k), PR[pb])
            tt(DVE, X.ap[:, c, :], X.ap[:, c, :], PS[pb], ALU.add, X.r(c) + PR[pb], X.r(c))

    def l1_mixer(t):
        kb.off = pers_off
        DT = alloc([TT])
        ACU = alloc([TT])
        DTT = alloc([4, 32])
        NAT = alloc([4, 32])
        m1 = kb.off
        SQ = alloc([8, TT], BF16)
        norm_fm(X, TT, "gmix", 8, H, SQ)
        kb.off = m1
        DA = alloc([TT])
        XTM = alloc([4, 1024])
        ZS = alloc([4, 1024], BF16)
        BF_ = alloc([4, TT], BF16)
        CF_ = alloc([4, TT], BF16)
        BT = alloc([4, 512], BF16)
        m2 = kb.off
        XA = [alloc([516]) for _ in range(2)]
        XC = [alloc([TT]) for _ in range(2)]
        kb.off = m2
        WS_ = alloc([16])
        ALB = alloc([16])
        EAL = alloc([16])
        SS = alloc([4])
        LL = [alloc([4, 128]) for _ in range(2)]
        CBM = [alloc([128]) for _ in range(2)]
        MT = [alloc([4, 128], BF16) for _ in range(2)]
        CT = [alloc([4, 128], BF16) for _ in range(2)]
        XDT = alloc([1024], BF16)
        XDW = alloc([1024], BF16)
        STB = alloc([1024], BF16)
        YA = alloc([1024])
        YB = alloc([1024])
        YN = alloc([1024], BF16)
        W = "o_w_in"
        ws = wload(W, 0, 8, 6144, 32)
        pb = bank()
        for k in range(8):
            mm(PS[pb][0:32, :], ws.ap[:, k, :], H.ap[:, k, :], k == 0, k == 7, ws.regs + H.r(k), PR[pb])
        act(DT.ap[0:32, :], PS[pb][0:32, :], AF.Exp, PR[pb] + PC.r(), DT.r(), bias=PC.ap[0:32, PCI["dtb"]:PCI["dtb"] + 1])
        act(DT.ap[0:32, :], DT.ap[0:32, :], AF.Ln, DT.r(), DT.r(), bias=1.0)
        ts(DVE, DA.ap[0:32, :], DT.ap[0:32, :], ANEG.ap[0:32, 0:1], None, ALU.mult, None, DT.r() + ANEG.r(), DA.r())
        scan(ACU.ap[0:32, :], R128[0:32, :], DA.ap[0:32, :], 0.0, DA.r() + CST.r(), ACU.r())
        pb = bank()
        for s in range(4):
            tr(PS[pb][:, s * 32:(s + 1) * 32], DT.ap[0:32, s * 128:(s + 1) * 128], IDF[0:32, 0:32], DT.r() + CST.r(), PR[pb])
        cp(DVE, DTT.ap, PS[pb][:, 0:128].rearrange("p (s h) -> p s h", s=4), PR[pb], DTT.r())
        pb = bank()
        for s in range(4):
            tr(PS[pb][:, s * 32:(s + 1) * 32], ACU.ap[0:32, s * 128:(s + 1) * 128], IDF[0:32, 0:32], ACU.r() + CST.r(), PR[pb])
        ts(DVE, NAT.ap, PS[pb][:, 0:128].rearrange("p (s h) -> p s h", s=4), -1.0, None, ALU.mult, None, PR[pb], NAT.r())

        YBANKS = (0, 1)
        OTH = (2, 3, 4, 5, 6, 7)
        for hf in range(2):
            for sl in range(2):
                ws = wload(W, 0, 8, hf * 1024 + sl * 512, 512)
                for s in range(4):
                    pb = bank()
                    for k in range(8):
                        mm(PS[pb], H.ap[:, k, s * 128:(s + 1) * 128], ws.ap[:, k, :], k == 0, k == 7, ws.regs + H.r(k), PR[pb])
                    act(ZS.ap[:, s, sl * 512:(sl + 1) * 512], PS[pb], AF.Silu, PR[pb],
                        ZS.rb(s * 1024 + sl * 512, s * 1024 + sl * 512 + 512))

            def c_x(ci, pb, hf=hf):
                ch = hf * 8 + ci
                XAb, XCb = XA[ci % 2], XC[ci % 2]
                conv_chunk(pb, XAb, CAR1, ch, "mcw", 32, "mcb", XCb.ap, XCb.r())
                act(XCb.ap, XCb.ap, AF.Silu, XCb.r(), XCb.r())
                pbt = bank()
                for s in range(4):
                    tr(PS[pbt][:, s * 128:(s + 1) * 128], XCb.ap[:, s * 128:(s + 1) * 128], IDF, XCb.r() + CST.r(), PR[pbt])
                cp(DVE, XTM.ap[:, :, ci * 128:(ci + 1) * 128], PS[pbt].rearrange("p (s f) -> p s f", s=4), PR[pbt], XTM.r())
            proj_fm(W, 2048 + hf * 1024, 8, H, TT, c_x)

            def c_b(gi, pb, hf=hf):
                ch = 16 + hf * 4 + gi
                XAb, XCb = XA[gi % 2], XC[gi % 2]
                conv_chunk(pb, XAb, CAR1, ch, "mcw", 32, "mcb", XCb.ap, XCb.r())
                act(BF_.ap[:, gi, :], XCb.ap, AF.Silu, XCb.r(), BF_.r(gi))
                pbt = bank()
                for s in range(4):
                    tr(PSB[pbt][:, s * 128:(s + 1) * 128], BF_.ap[:, gi, s * 128:(s + 1) * 128], IDB.ap, BF_.r(gi) + IDB.r(), PR[pbt])
                cp(DVE, BT.ap[:, :, gi * 128:(gi + 1) * 128], PSB[pbt][:, 0:512].rearrange("p (s f) -> p s f", s=4), PR[pbt], BT.r())
            proj_fm(W, 4096 + hf * 512, 4, H, TT, c_b)

            def c_c(gi, pb, hf=hf):
                ch = 24 + hf * 4 + gi
                XAb, XCb = XA[gi % 2], XC[gi % 2]
                conv_chunk(pb, XAb, CAR1, ch, "mcw", 32, "mcb", XCb.ap, XCb.r())
                act(CF_.ap[:, gi, :], XCb.ap, AF.Silu, XCb.r(), CF_.r(gi))
            proj_fm(W, 5120 + hf * 512, 4, H, TT, c_c)

            H0 = hf * 16
            for s in range(4):
                cs = slice(s * 128, (s + 1) * 128)
                tt(DVE, XDT.ap.rearrange("p (h q) -> p h q", q=64), XTM.ap[:, s, :].rearrange("p (h q) -> p h q", q=64),
                   DTT.ap[:, s, H0:H0 + 16].unsqueeze(2).to_broadcast([128, 16, 64]), ALU.mult, XTM.r(s) + DTT.r(), XDT.r())
                cp(POOL, STB.ap, ST1.ap[:, hf * 1024:(hf + 1) * 1024], ST1.rb(hf * 1024, hf * 1024 + 1024), STB.r())
                for gl in range(4):
                    i2 = gl % 2
                    pcb = bank(OTH)
                    mm(PS[pcb][:, 0:128], BF_.ap[:, gl, cs], CF_.ap[:, gl, cs], True, True, BF_.r(gl) + CF_.r(gl), PR[pcb])
                    tt(DVE, CBM[i2].ap, PS[pcb][:, 0:128], MASKC, ALU.mult, PR[pcb] + CST.r(), CBM[i2].r())
                    pa = bank(OTH)
                    for hh in range(4):
                        h = H0 + 4 * gl + hh
                        mm(PS[pa][:, hh * 128:(hh + 1) * 128], IDF[0:32, h:h + 1].to_broadcast([32, 128]), ACU.ap[0:32, cs],
                           hh == 0, hh == 3, CST.r() + ACU.r(), PR[pa])
                    for hh in range(4):
                        h = H0 + 4 * gl + hh
                        act(LL[i2].ap[:, hh, :], PS[pa][:, hh * 128:(hh + 1) * 128], AF.Exp, PR[pa] + NAT.r(), LL[i2].r(hh),
                            bias=NAT.ap[:, s, h:h + 1])
                    stt(MT[i2].ap, LL[i2].ap, 1.0, CBM[i2].ap.unsqueeze(1).to_broadcast([128, 4, 128]), ALU.min, ALU.mult,
                        LL[i2].r() + CBM[i2].r(), MT[i2].r())
                    pav = PS[pa].rearrange("p (h t) -> p h t", h=4)
                    cp(DVE, ALB.ap[:, 4 * gl:4 * gl + 4], pav[:, :, 127], PR[pa], ALB.r())
                    act(LL[i2].ap, pav, AF.Exp, PR[pa], LL[i2].r())
                    tt(DVE, CT[i2].ap, LL[i2].ap, CF_.ap[:, gl, cs].unsqueeze(1).to_broadcast([128, 4, 128]), ALU.mult,
                       LL[i2].r() + CF_.r(gl), CT[i2].r())
                    for hh in range(4):
                        hl = 4 * gl + hh
                        yb = YBANKS[hl // 8]
                        oc = slice((hl % 8) * 64, (hl % 8 + 1) * 64)
                        mm(PS[yb][:, oc], MT[i2].ap[:, hh, :], XDT.ap[:, hl * 64:(hl + 1) * 64], (hl % 8 == 0), False,
                           MT[i2].r(hh) + XDT.r(), PR[yb])
                        mm(PS[yb][:, oc], CT[i2].ap[:, hh, :], STB.ap[:, hl * 64:(hl + 1) * 64], False, (hl % 8 == 7),
                           CT[i2].r(hh) + STB.r(), PR[yb])
                STh = ST1.ap[:, hf * 1024:(hf + 1) * 1024]
                STr = ST1.rb(hf * 1024, hf * 1024 + 1024)
                act(EAL.ap, ALB.ap, AF.Exp, ALB.r(), EAL.r())
                tt(DVE, WS_.ap, ALB.ap, NAT.ap[:, s, H0:H0 + 16], ALU.add, ALB.r() + NAT.r(), WS_.r())
                act(WS_.ap, WS_.ap, AF.Exp, WS_.r(), WS_.r())
                tt(DVE, WS_.ap, WS_.ap, DTT.ap[:, s, H0:H0 + 16], ALU.mult, WS_.r() + DTT.r(), WS_.r())
                tt(DVE, XDW.ap.rearrange("p (h q) -> p h q", q=64), XTM.ap[:, s, :].rearrange("p (h q) -> p h q", q=64),
                   WS_.ap.unsqueeze(2).to_broadcast([128, 16, 64]), ALU.mult, XTM.r(s) + WS_.r(), XDW.r())
                tt(POOL, STh.rearrange("p (h q) -> p h q", q=64), STh.rearrange("p (h q) -> p h q", q=64),
                   EAL.ap.unsqueeze(2).to_broadcast([128, 16, 64]), ALU.mult, STr + EAL.r(), STr)
                for half in range(2):
                    pd = bank(OTH)
                    for gg in range(2):
                        gl = half * 2 + gg
                        mm(PS[pd][:, gg * 256:(gg + 1) * 256], BT.ap[:, s, gl * 128:(gl + 1) * 128], XDW.ap[:, gl * 256:(gl + 1) * 256],
                           gg == 0, gg == 1, BT.r(s) + XDW.r(), PR[pd])
                    sl_ = slice(half * 512, (half + 1) * 512)
                    tt(DVE, STh[:, sl_], STh[:, sl_], PS[pd], ALU.add, STr + PR[pd], STr)
                tt(POOL, YA.ap.rearrange("p (h q) -> p h q", q=64), XTM.ap[:, s, :].rearrange("p (h q) -> p h q", q=64),
                   DBC.ap[:, H0:H0 + 16].unsqueeze(2).to_broadcast([128, 16, 64]), ALU.mult, XTM.r(s) + DBC.r(), YA.r())
                for q in range(2):
                    sl_ = slice(q * 512, (q + 1) * 512)
                    tt(DVE, YA.ap[:, sl_], YA.ap[:, sl_], PS[YBANKS[q]], ALU.add, YA.r() + PR[YBANKS[q]], YA.r())
                tt(POOL, YA.ap, YA.ap, ZS.ap[:, s, :], ALU.mult, YA.r() + ZS.r(s), YA.r())
                tt(POOL, YB.ap, YA.ap, YA.ap, ALU.mult, YA.r(), YB.r())
                op(DVE, lambda: nc.vector.tensor_reduce(out=SS.ap, in_=YB.ap.rearrange("p (g q) -> p g q", q=256),
                                                        axis=mybir.AxisListType.X, op=ALU.add), YB.r(), SS.r())
                act(SS.ap, SS.ap, AF.Sqrt, SS.r() + EPSC.r(), SS.r(), bias=EPSC.ap[:, 0:1], scale=1.0 / 256)
                recip(SS.ap, SS.ap, SS.r(), SS.r())
                tt(DVE, YA.ap.rearrange("p (g q) -> p g q", q=256), YA.ap.rearrange("p (g q) -> p g q", q=256),
                   SS.ap.unsqueeze(2).to_broadcast([128, 4, 256]), ALU.mult, YA.r() + SS.r(), YA.r())
                tt(POOL, YN.ap, YA.ap, NG.ap[:, hf * 1024:(hf + 1) * 1024], ALU.mult, YA.r() + NG.r(), YN.r())
                for q in range(2):
                    pbt = bank(OTH)
                    for cc in range(4):
                        c = q * 4 + cc
                        tr(PSB[pbt][:, cc * 128:(cc + 1) * 128], YN.ap[:, c * 128:(c + 1) * 128], IDB.ap, YN.r() + IDB.r(), PR[pbt])
                    c0 = hf * 8 + q * 4
                    cp(ACT, Y.ap[:, c0:c0 + 4, cs], PSB[pbt][:, 0:512].rearrange("p (c t) -> p c t", c=4), PR[pbt],
                       Y.r(c0, c0 + 4))
        dump("ym", Y.ap, Y.r())
        outproj_add("o_w_out", Y, 16)

    def final(t):
        kb.off = pers_off
        HF = alloc([8, TT])
        IO = alloc([4, 1024])
        SQ = alloc([8, TT], BF16)
        norm_fm(X, TT, "gfin", 0, HF, SQ)
        for s in range(4):
            for hf in range(2):
                pb = bank()
                for cc in range(4):
                    c = hf * 4 + cc
                    tr(PS[pb][:, cc * 128:(cc + 1) * 128], HF.ap[:, c, s * 128:(s + 1) * 128], IDF, HF.r(c) + CST.r(), PR[pb])
                cp(ACT if hf else DVE, IO.ap[:, s, hf * 512:(hf + 1) * 512], PS[pb], PR[pb], IO.r(s))
        dma(POOL, T_OUT, out_d[t * TT:(t + 1) * TT, :].rearrange("(s p) f -> p s f", p=128), IO.ap, IO.r(), [])

    stages = ["l0", "a0", "f0", "l1", "a1", "f1"]
    for t in range(NT):
        load_x(t)
        for st in stages:
            if stop_after == "load":
                break
            if st == "l0":
                l0_mixer(t)
            elif st == "a0":
                xattn(0)
            elif st == "f0":
                ffn(0)
            elif st == "l1":
                l1_mixer(t)
            elif st == "a1":
                xattn(1)
            elif st == "f1":
                ffn(1)
            if stop_after == st or (stop_after in ("rglru", "hgA", "hgB") and st == "l0"):
                break
        final(t)
    nc.gpsimd.wait_ge(T_OUT.sem, T_OUT.n * 16)
    for t_ in kb.dbgt:
        nc.gpsimd.wait_ge(t_.sem, t_.n * 16)
    es.close()
    kb.counts = counts
    kb.wneed = wneed
    return nc, kb


def _cols(v):
    v = np.asarray(v, np.float32).reshape(-1)
    return np.ascontiguousarray(v.reshape(v.size // 128, 128).T)


def make_shared_inputs(inp):
    pc = np.zeros((128, NPC), np.float32)

    def put(name, arr):
        a = _cols(arr)
        pc[:, PCI[name]:PCI[name] + a.shape[1]] = a
    put("gmix", inp["norm_mix_g"])
    put("gmq", inp["norm_mem_q_g"])
    put("gmkv", inp["norm_mem_kv_g"])
    put("gffn", inp["norm_ffn_g"])
    put("gfin", inp["final_norm_g"])
    put("acw", inp["a_conv_w"][0])
    put("acb", inp["a_conv_b"][0])
    put("arb", inp["a_gate_r_b"][0])
    put("aib", inp["a_gate_i_b"][0])
    put("alam", inp["a_lambda"][0])
    put("lbl", inp["b_lb_logits"])
    put("bng", inp["b_norm_g"][0])
    put("mcw", inp["m_conv_w"][0])
    put("mcb", inp["m_conv_b"][0])
    pc[:32, PCI["dtb"]] = np.asarray(inp["m_dt_bias"][0], np.float32)
    pc[:32, PCI["alog"]] = np.asarray(inp["m_a_log"][0], np.float32)
    cst = np.zeros((128, NCST), np.float32)
    cst[:, C_ID:C_ID + 128] = np.eye(128, dtype=np.float32)
    s_ = np.arange(128)[:, None]
    t_ = np.arange(128)[None, :]
    cst[:, C_MH:C_MH + 128] = ((s_ <= t_) & ((s_ // 64) == (t_ // 64))).astype(np.float32)
    cst[:, C_MC:C_MC + 128] = (s_ <= t_).astype(np.float32)
    cst[:, C_R64:C_R64 + 512] = (np.arange(512) % 64 != 0).astype(np.float32)[None, :]
    cst[:, C_R128:C_R128 + 512] = (np.arange(512) % 128 != 0).astype(np.float32)[None, :]
    sh = {"pc": pc, "cst": cst}
    sh["gr"] = np.ascontiguousarray(np.asarray(inp["a_gate_r_w"][0], np.float32).transpose(1, 0, 2).reshape(128, 1024))
    sh["gi"] = np.ascontiguousarray(np.asarray(inp["a_gate_i_w"][0], np.float32).transpose(1, 0, 2).reshape(128, 1024))
    sh["mng"] = np.asarray(inp["m_norm_g"], np.float32).reshape(1, 2048)
    sh["md"] = np.repeat(np.asarray(inp["m_d"], np.float32).reshape(1, 32), 1, axis=0)
    sh["e_w_in"] = np.asarray(inp["e_w_in"][0], np.float32)
    sh["e_w_out"] = np.asarray(inp["e_w_out"][0], np.float32)
    sh["o_w_in"] = np.asarray(inp["o_w_in"][0], np.float32)
    sh["o_w_out"] = np.asarray(inp["o_w_out"][0], np.float32)
    for l in range(2):
        sh[f"xq{l}"] = np.asarray(inp["xq_w"][l], np.float32)
        sh[f"xk{l}"] = np.asarray(inp["xk_w"][l], np.float32)
        sh[f"xv{l}"] = np.asarray(inp["xv_w"][l], np.float32)
        sh[f"xo{l}"] = np.asarray(inp["xo_w"][l], np.float32)
        sh[f"w1_{l}"] = np.asarray(inp["ffn_w1"][l], np.float32)
        sh[f"w2_{l}"] = np.asarray(inp["ffn_w2"][l], np.float32)
    return sh


_CACHE = {}


def kernel(**inputs):
    x = np.asarray(inputs["x"], np.float32)
    mem = np.asarray(inputs["mem"], np.float32)
    B, S, _ = x.shape
    if S not in _CACHE:
        _CACHE[S] = build(S)[0]
    nc = _CACHE[S]
    sh = make_shared_inputs(inputs)
    in_maps = []
    for b in range(B):
        m = dict(sh)
        m["x"] = np.ascontiguousarray(x[b])
        m["mem"] = np.ascontiguousarray(mem[b])
        in_maps.append(m)
    res = run_bass_kernel_spmd(nc, in_maps, core_ids=list(range(B)))
    return np.stack([np.asarray(r["out"], np.float32) for r in res.results], axis=0)
```

```python
import numpy as np
from contextlib import ExitStack
import concourse.bass as bass
import concourse.mybir as mybir
from concourse.bass_utils import run_bass_kernel_spmd

F32 = mybir.dt.float32
BF16 = mybir.dt.bfloat16
AF = mybir.ActivationFunctionType
ALU = mybir.AluOpType
D = 1024
TT = 512
EPS = 1e-6
MEM = 256

PCI = {}
_n = 0
for _nm, _c in [("gmix", 16), ("gmq", 16), ("gmkv", 16), ("gffn", 16), ("gfin", 8), ("acw", 32), ("acb", 8),
                ("arb", 8), ("aib", 8), ("alam", 8), ("lbl", 24), ("bng", 8), ("mcw", 128), ("mcb", 32),
                ("dtb", 1), ("alog", 1)]:
    PCI[_nm] = _n
    _n += _c
NPC = _n
C_ID, C_MH, C_MC, C_R64, C_R128 = 0, 128, 256, 384, 896
NCST = 1408

WSHAPES = {"e_w_in": (1024, 6144), "e_w_out": (2048, 1024), "o_w_in": (1024, 6176), "o_w_out": (2048, 1024),
           "xq0": (1024, 1024), "xk0": (1024, 1024), "xv0": (1024, 1024), "xo0": (1024, 1024),
           "xq1": (1024, 1024), "xk1": (1024, 1024), "xv1": (1024, 1024), "xo1": (1024, 1024),
           "w1_0": (1024, 4096), "w2_0": (4096, 1024), "w1_1": (1024, 4096), "w2_1": (4096, 1024)}
WORDER = ["xk0", "xv0", "xk1", "xv1", "e_w_in", "e_w_out", "xq0", "xo0", "w1_0", "w2_0",
          "o_w_in", "o_w_out", "xq1", "xo1", "w1_1", "w2_1"]


class Reg:
    __slots__ = ("w", "r")

    def __init__(self):
        self.w = None
        self.r = {}


class Trk:
    def __init__(self, sem, inc):
        self.sem = sem
        self.inc = inc
        self.n = 0


class Eng:
    def __init__(self, eng, trk, is_pe=False):
        self.eng = eng
        self.trk = trk
        self.seen = {}
        self.is_pe = is_pe


class View:
    def __init__(self, kb, b0, dtype, shape):
        self.kb = kb
        self.b0 = b0
        self.es = 4 if dtype == F32 else 2
        self.shape = tuple(shape)
        n = int(np.prod(shape))
        self.nbytes = n * self.es
        base = kb.arena[:, b0 // 4:(b0 + self.nbytes) // 4]
        ap = base if dtype == F32 else base.bitcast(BF16)
        if len(shape) == 2:
            ap = ap.rearrange("p (a b) -> p a b", a=shape[0])
        elif len(shape) == 3:
            ap = ap.rearrange("p (a b c) -> p a b c", a=shape[0], b=shape[1])
        self.ap = ap
        self.slot = self.nbytes // shape[0] if len(shape) > 1 else self.nbytes

    def r(self, i=None, j=None):
        G = self.kb.G
        if i is None:
            lo, hi = self.b0, self.b0 + self.nbytes
        else:
            if j is None:
                j = i + 1
            lo, hi = self.b0 + i * self.slot, self.b0 + j * self.slot
        return self.kb.regs[lo // G:(hi + G - 1) // G]

    def rb(self, lo_el, hi_el):
        G = self.kb.G
        lo, hi = self.b0 + lo_el * self.es, self.b0 + hi_el * self.es
        return self.kb.regs[lo // G:(hi + G - 1) // G]


def build(S, stop_after=None, dbg=None):
    NT = S // TT
    nc = bass.Bass("TRN2", target_bir_lowering=False)
    kb = type("KB", (), {})()
    es = ExitStack()
    E_ = es.enter_context

    def din(name, shape, dt=F32):
        return nc.dram_tensor(name, list(shape), dt, kind="ExternalInput").ap()

    x_d = din("x", [S, D])
    mem_d = din("mem", [MEM, D])
    pc_d = din("pc", [128, NPC])
    cst_d = din("cst", [128, NCST])
    gr_d = din("gr", [128, 1024])
    gi_d = din("gi", [128, 1024])
    mng_d = din("mng", [1, 2048])
    md_d = din("md", [1, 32])
    _stage_w = {"load": [], "rglru": ["e_w_in"], "hgA": [], "hgB": [], "l0": ["e_w_in", "e_w_out"], "a0": ["xq0", "xo0"], "f0": ["w1_0", "w2_0"],
                "l1": ["o_w_in", "o_w_out"], "a1": ["xq1", "xo1"], "f1": ["w1_1", "w2_1"]}
    wneed = ["xk0", "xv0", "xk1", "xv1"]
    for _st in ["load", "rglru", "hgA", "hgB", "l0", "a0", "f0", "l1", "a1", "f1"]:
        wneed += _stage_w[_st]
        if stop_after == _st:
            break
    wneed = [k for k in WORDER if k in set(wneed)]
    wf = {k: din(k, WSHAPES[k]) for k in wneed}
    wb = {k: nc.dram_tensor(k + "_b", list(WSHAPES[k]), BF16, kind="Internal").ap() for k in wneed}
    out_d = nc.dram_tensor("out", [S, D], F32, kind="ExternalOutput").ap()
    dbg_d = {}
    if dbg:
        for nm, shp in dbg.items():
            dbg_d[nm] = nc.dram_tensor("dbg_" + nm, list(shp), F32, kind="ExternalOutput").ap()

    ARENA_BYTES = 206 * 1024
    kb.G = 512
    kb.arena = E_(nc.sbuf_tensor("arena", [128, ARENA_BYTES // 4], F32))[:]
    kb.regs = [Reg() for _ in range(ARENA_BYTES // kb.G)]
    kb.off = 0
    kb.dbgt = []

    def alloc(shape, dtype=F32):
        v = View(kb, kb.off, dtype, shape)
        kb.off += (v.nbytes + kb.G - 1) // kb.G * kb.G
        assert kb.off <= ARENA_BYTES, f"SBUF arena overflow {kb.off}"
        return v

    def mksem(name):
        return E_(nc.semaphore(name))

    PE = Eng(nc.tensor, Trk(mksem("s_pe"), 1), is_pe=True)
    ACT = Eng(nc.scalar, Trk(mksem("s_act"), 1))
    DVE = Eng(nc.vector, Trk(mksem("s_dve"), 1))
    POOL = Eng(nc.gpsimd, Trk(mksem("s_pool"), 1))
    SP = Eng(nc.sync, None)
    kb.nsem = 0

    def newtrk():
        kb.nsem += 1
        return Trk(mksem(f"s_dma{kb.nsem}"), 16)
    NCAST = 6
    T_CAST = [newtrk() for _ in range(NCAST)]
    T_XIN = newtrk()
    T_OUT = newtrk()
    counts = {"wait": 0, "ins": 0}

    def op(E, fn, r=(), w=(), trk=None):
        trk = trk or E.trk
        need = {}
        for g in r:
            if g.w is not None:
                t, c = g.w
                if need.get(t, 0) < c:
                    need[t] = c
        for g in w:
            if g.w is not None:
                t, c = g.w
                if need.get(t, 0) < c:
                    need[t] = c
            for t, c in g.r.items():
                if need.get(t, 0) < c:
                    need[t] = c
        for t, c in need.items():
            if E.is_pe and t is E.trk:
                continue
            if E.seen.get(t, 0) < c:
                E.eng.wait_ge(t.sem, c * t.inc)
                E.seen[t] = c
                counts["wait"] += 1
        ins = fn()
        trk.n += 1
        ins.then_inc(trk.sem, trk.inc)
        counts["ins"] += 1
        n = trk.n
        for g in r:
            if g.r.get(trk, 0) < n:
                g.r[trk] = n
        for g in w:
            g.w = (trk, n)
            g.r = {}

    PSt = [E_(nc.psum_tensor(f"ps{i}", [128, 512], F32)) for i in range(8)]
    PS = [t[:] for t in PSt]
    PSB = [t[:].bitcast(BF16) for t in PSt]
    PR = [[Reg()] for _ in range(8)]
    kb.pb = 0

    def bank(pool=(0, 1, 2, 3, 4, 5, 6, 7)):
        kb.pb = (kb.pb + 1) % len(pool)
        return pool[kb.pb]

    def mm(out, lhsT, rhs, start, stop, r, w):
        op(PE, lambda: nc.tensor.matmul(out, lhsT=lhsT, rhs=rhs, start=start, stop=stop), r, w)

    def tr(out, in_, ident, r, w):
        op(PE, lambda: nc.tensor.transpose(out=out, in_=in_, identity=ident), r, w)

    def act(out, in_, func, r, w, bias=None, scale=None):
        kw = {}
        if bias is not None:
            kw["bias"] = bias
        if scale is not None:
            kw["scale"] = scale
        op(ACT, lambda: nc.scalar.activation(out=out, in_=in_, func=func, **kw), r, w)

    def ts(E, out, in0, s1, s2, op0, op1, r, w):
        if s2 is None:
            op(E, lambda: E.eng.tensor_scalar(out=out, in0=in0, scalar1=s1, scalar2=None, op0=op0), r, w)
        else:
            op(E, lambda: E.eng.tensor_scalar(out=out, in0=in0, scalar1=s1, scalar2=s2, op0=op0, op1=op1), r, w)

    def tt(E, out, in0, in1, o, r, w):
        op(E, lambda: E.eng.tensor_tensor(out=out, in0=in0, in1=in1, op=o), r, w)

    def stt(out, in0, scalar, in1, op0, op1, r, w):
        op(DVE, lambda: nc.vector.scalar_tensor_tensor(out=out, in0=in0, scalar=scalar, in1=in1, op0=op0, op1=op1), r, w)

    def scan(out, d0, d1, initial, r, w):
        op(DVE, lambda: nc.vector.tensor_tensor_scan(out=out, data0=d0, data1=d1, initial=initial,
                                                     op0=ALU.mult, op1=ALU.add), r, w)

    def cp(E, out, in_, r, w):
        if E is ACT:
            op(ACT, lambda: nc.scalar.activation(out=out, in_=in_, func=AF.Copy), r, w)
        else:
            op(E, lambda: E.eng.tensor_copy(out=out, in_=in_), r, w)

    def recip(out, in_, r, w):
        op(DVE, lambda: nc.vector.reciprocal(out=out, in_=in_), r, w)

    def dma(E, trk, out, in_, r, w):
        op(E, lambda: E.eng.dma_start(out=out, in_=in_), r, w, trk=trk)

    WBR = {k: [Reg() for _ in range(v[0] // 128)] for k, v in WSHAPES.items()}

    PC = alloc([NPC])
    CST = alloc([NCST])
    IDB = alloc([128], BF16)
    ONES = alloc([128], BF16)
    EPSC = alloc([1])
    NG = alloc([2048])
    DBC = alloc([32])
    GR = alloc([8, 128], BF16)
    GI = alloc([8, 128], BF16)
    CL = alloc([8])
    CL2 = alloc([8])
    LB = alloc([8])
    OML = alloc([8])
    ANEG = alloc([1])
    KF = [alloc([8, MEM], BF16) for _ in range(2)]
    VM = [alloc([2, 1024], BF16) for _ in range(2)]
    X = alloc([8, TT])
    H = alloc([8, TT], BF16)
    Y = alloc([16, TT], BF16)
    RS = alloc([TT])
    CAR0 = alloc([8, 4])
    HST = alloc([8])
    ST0 = alloc([8, 128])
    CAR1 = alloc([32, 4])
    ST1 = alloc([2048])
    NRING = 4
    RING = [alloc([8 * 512], BF16) for _ in range(NRING)]
    T_RING = [newtrk() for _ in range(NRING)]
    kb.ring = 0
    pers_off = kb.off

    def pc(name, i):
        c = PCI[name] + i
        return PC.ap[:, c:c + 1]

    IDF = CST.ap[:, C_ID:C_ID + 128]
    MASKH = CST.ap[:, C_MH:C_MH + 128]
    MASKC = CST.ap[:, C_MC:C_MC + 128]
    R64 = CST.ap[:, C_R64:C_R64 + 512]
    R128 = CST.ap[:, C_R128:C_R128 + 512]

    class Slab:
        pass

    def wload(name, k0, kc, c0, ncols):
        assert kc * ncols * 2 <= 8192
        v = RING[kb.ring]
        trk_ = T_RING[kb.ring]
        kb.ring = (kb.ring + 1) % NRING
        s = Slab()
        base = v.ap[:, 0:kc * ncols]
        s.ap = base.rearrange("p (k n) -> p k n", k=kc)
        s.regs = v.rb(0, kc * ncols)
        src = wb[name][k0 * 128:(k0 + kc) * 128, c0:c0 + ncols].rearrange("(k p) n -> p k n", p=128)
        dma(SP, trk_, s.ap, src, r=WBR[name][k0:k0 + kc], w=s.regs)
        return s

    dma(SP, newtrk(), PC.ap, pc_d[:, :], [], PC.r())
    dma(SP, newtrk(), CST.ap, cst_d[:, :], [], CST.r())
    dma(SP, newtrk(), NG.ap, mng_d.partition_broadcast(128), [], NG.r())
    dma(SP, newtrk(), DBC.ap, md_d.partition_broadcast(128), [], DBC.r())
    kb.ncast = 0
    for name in wneed:
        K_, N_ = WSHAPES[name]
        for kb_ in range(K_ // 128):
            tc_ = T_CAST[kb.ncast % NCAST]
            kb.ncast += 1
            if tc_.n > 0:
                nc.gpsimd.wait_ge(tc_.sem, tc_.n * 16)
            dma(POOL, tc_, wb[name][kb_ * 128:(kb_ + 1) * 128, :], wf[name][kb_ * 128:(kb_ + 1) * 128, :],
                [], [WBR[name][kb_]])

    m0 = kb.off
    TMPF = alloc([1024])
    dma(SP, newtrk(), TMPF.ap, gr_d[:, :], [], TMPF.r())
    cp(DVE, GR.ap, TMPF.ap.rearrange("p (a b) -> p a b", a=8), TMPF.r(), GR.r())
    TMPG = alloc([1024])
    dma(SP, newtrk(), TMPG.ap, gi_d[:, :], [], TMPG.r())
    cp(DVE, GI.ap, TMPG.ap.rearrange("p (a b) -> p a b", a=8), TMPG.r(), GI.r())
    cp(DVE, IDB.ap, IDF, CST.r(), IDB.r())
    op(DVE, lambda: nc.vector.memset(ONES.ap, 1.0), [], ONES.r())
    op(DVE, lambda: nc.vector.memset(EPSC.ap, EPS), [], EPSC.r())
    for v in (CAR0, HST, ST0, CAR1, ST1):
        op(DVE, lambda v=v: nc.vector.memset(v.ap, 0.0), [], v.r())
    T8 = alloc([24])
    act(T8.ap[:, 0:8], PC.ap[:, PCI["alam"]:PCI["alam"] + 8], AF.Exp, PC.r(), T8.r(), scale=-1.0)
    act(T8.ap[:, 0:8], T8.ap[:, 0:8], AF.Ln, T8.r(), T8.r(), bias=1.0)
    ts(DVE, CL.ap, T8.ap[:, 0:8], -8.0, None, ALU.mult, None, T8.r(), CL.r())
    ts(DVE, CL2.ap, T8.ap[:, 0:8], -16.0, None, ALU.mult, None, T8.r(), CL2.r())
    act(T8.ap, PC.ap[:, PCI["lbl"]:PCI["lbl"] + 24], AF.Exp, PC.r(), T8.r())
    tt(DVE, LB.ap, T8.ap[:, 0:8], T8.ap[:, 8:16], ALU.add, T8.r(), LB.r())
    tt(DVE, LB.ap, LB.ap, T8.ap[:, 16:24], ALU.add, T8.r() + LB.r(), LB.r())
    recip(LB.ap, LB.ap, LB.r(), LB.r())
    tt(DVE, LB.ap, LB.ap, T8.ap[:, 0:8], ALU.mult, LB.r() + T8.r(), LB.r())
    ts(DVE, OML.ap, LB.ap, -1.0, 1.0, ALU.mult, ALU.add, LB.r(), OML.r())
    act(ANEG.ap, pc("alog", 0), AF.Exp, PC.r(), ANEG.r())
    ts(DVE, ANEG.ap, ANEG.ap, -1.0, None, ALU.mult, None, ANEG.r(), ANEG.r())

    def norm_fm(Xv, n, gname, goff, out, tmp_sq):
        for c in range(8):
            act(tmp_sq.ap[:, c, :n], Xv.ap[:, c, :n], AF.Square, Xv.r(c), tmp_sq.r(c))
        pb = bank()
        for c in range(8):
            mm(PS[pb][:, :n], ONES.ap, tmp_sq.ap[:, c, :n], c == 0, c == 7, tmp_sq.r(c) + ONES.r(), PR[pb])
        act(RS.ap[:, :n], PS[pb][:, :n], AF.Sqrt, PR[pb] + EPSC.r(), RS.r(), bias=EPSC.ap[:, 0:1], scale=1.0 / D)
        recip(RS.ap[:, :n], RS.ap[:, :n], RS.r(), RS.r())
        for c in range(8):
            stt(out.ap[:, c, :n], Xv.ap[:, c, :n], pc(gname, goff + c), RS.ap[:, :n], ALU.mult, ALU.mult,
                Xv.r(c) + RS.r() + PC.r(), out.r(c))

    def step_all(pend):
        for g_ in list(pend):
            try:
                next(g_)
            except StopIteration:
                pend.remove(g_)

    def drain(pend):
        while pend:
            step_all(pend)

    def interleave(gens):
        pend = []
        for g_ in gens:
            pend.append(g_)
        drain(pend)

    def proj_fm(name, c0, nchunks, Hv, n, consumer, kc=8, k0=0, pend=None):
        own = pend is None
        if own:
            pend = []
        done = 0
        while done < nchunks:
            g = min(4, nchunks - done)
            ws = wload(name, k0, kc, c0 + done * 128, g * 128)
            for cc in range(g):
                pb = bank()
                for k in range(kc):
                    mm(PS[pb][:, :n], ws.ap[:, k, cc * 128:(cc + 1) * 128], Hv.ap[:, k, :n], k == 0, k == kc - 1,
                       ws.regs + Hv.r(k), PR[pb])
                gen_ = consumer(done + cc, pb)
                started = None
                if gen_ is not None:
                    try:
                        next(gen_)
                        started = gen_
                    except StopIteration:
                        pass
                step_all(pend)
                if started is not None:
                    pend.append(started)
            done += g
        if own:
            drain(pend)

    def outproj_add(name, Yv, kc):
        ncol = 8192 // (kc * 2)
        per = ncol // 128
        for g0 in range(0, 8, per):
            ws = wload(name, 0, kc, g0 * 128, ncol)
            for cc in range(per):
                c = g0 + cc
                pb = bank()
                for k in range(kc):
                    mm(PS[pb], ws.ap[:, k, cc * 128:(cc + 1) * 128], Yv.ap[:, k, :], k == 0, k == kc - 1,
                       ws.regs + Yv.r(k), PR[pb])
                tt(DVE, X.ap[:, c, :], X.ap[:, c, :], PS[pb], ALU.add, X.r(c) + PR[pb], X.r(c))

    def dump(nm, view_ap, regs):
        if nm in dbg_d:
            kb.dbgt.append(newtrk())
            dma(POOL, kb.dbgt[-1], dbg_d[nm], view_ap, regs, [])

    kb.off = m0
    MIN = alloc([2, 1024])
    MX = alloc([8, MEM])
    MH = alloc([8, MEM], BF16)
    MSQ = alloc([8, MEM], BF16)
    dma(SP, newtrk(), MIN.ap, mem_d.rearrange("(s p) f -> p s f", p=128), [], MIN.r())
    for c in range(8):
        pb = bank()
        for s in range(2):
            tr(PS[pb][:, s * 128:(s + 1) * 128], MIN.ap[:, s, c * 128:(c + 1) * 128], IDF, MIN.r(s) + CST.r(), PR[pb])
        cp(ACT, MX.ap[:, c, :], PS[pb][:, 0:MEM], PR[pb], MX.r(c))
    for l in range(2):
        norm_fm(MX, MEM, "gmkv", l * 8, MH, MSQ)

        def kcons(ci, pb, l=l):
            cp(ACT, KF[l].ap[:, ci, :], PS[pb][:, :MEM], PR[pb], KF[l].r(ci))
        proj_fm(f"xk{l}", 0, 8, MH, MEM, kcons)
        for sl in range(2):
            ws = wload(f"xv{l}", 0, 8, sl * 512, 512)
            for mh in range(2):
                pb = bank()
                for k in range(8):
                    mm(PS[pb], MH.ap[:, k, mh * 128:(mh + 1) * 128], ws.ap[:, k, :], k == 0, k == 7,
                       ws.regs + MH.r(k), PR[pb])
                cp(DVE, VM[l].ap[:, mh, sl * 512:(sl + 1) * 512], PS[pb], PR[pb], VM[l].rb(mh * 1024 + sl * 512, mh * 1024 + sl * 512 + 512))
    kb.off = pers_off

    def load_x(t):
        kb.off = pers_off
        IO = alloc([4, 1024])
        dma(SP, T_XIN, IO.ap, x_d[t * TT:(t + 1) * TT, :].rearrange("(s p) f -> p s f", p=128), [], IO.r())
        for c in range(8):
            pb = bank()
            for s in range(4):
                tr(PS[pb][:, s * 128:(s + 1) * 128], IO.ap[:, s, c * 128:(c + 1) * 128], IDF, IO.r(s) + CST.r(), PR[pb])
            cp(ACT if c % 2 else DVE, X.ap[:, c, :], PS[pb], PR[pb], X.r(c))

    def conv_chunk(pb, XAv, CARv, j, wname, wstride, bname, XCv_ap, XC_regs):
        cp(POOL, XAv.ap[:, 0:3], CARv.ap[:, j, 0:3], CARv.r(j), XAv.r())
        cp(ACT, XAv.ap[:, 3:515], PS[pb], PR[pb], XAv.r())
        cp(POOL, CARv.ap[:, j, 0:3], XAv.ap[:, 512:515], XAv.r(), CARv.r(j))
        ts(DVE, XCv_ap, XAv.ap[:, 3:515], pc(wname, 3 * wstride + j), pc(bname, j), ALU.mult, ALU.add,
           XAv.r() + PC.r(), XC_regs)
        for k in range(3):
            stt(XCv_ap, XAv.ap[:, k:k + 512], pc(wname, k * wstride + j), XCv_ap, ALU.mult, ALU.add,
                XAv.r() + XC_regs + PC.r(), XC_regs)

    def l0_mixer(t):
        kb.off = pers_off
        SQ = alloc([8, TT], BF16)
        norm_fm(X, TT, "gmix", 0, H, SQ)
        kb.off = pers_off
        FA = [[alloc([TT]) for _ in range(6)] for _ in range(2)]
        GATE = alloc([4, TT], BF16)
        GATE2 = alloc([4, TT], BF16)
        QS = alloc([4, TT], BF16)
        QT = alloc([4, TT], BF16)
        KT = alloc([4, TT], BF16)
        KDT = alloc([4, TT], BF16)
        VT = alloc([4, 512], BF16)
        XA = [alloc([516]) for _ in range(2)]
        XCBs = [alloc([TT], BF16) for _ in range(2)]
        KDs = [alloc([TT], BF16) for _ in range(2)]
        PTs = [alloc([4, 128], BF16) for _ in range(2)]
        SB = [alloc([8, 128], BF16) for _ in range(2)]
        EL = alloc([4, 8])
        OSQs = [alloc([TT], BF16) for _ in range(2)]
        RS2 = [alloc([TT]) for _ in range(2)]
        W = "e_w_in"

        for hf in range(2):
            def c_ga(ci, pb):
                act(GATE.ap[:, ci, :], PS[pb], AF.Gelu, PR[pb], GATE.r(ci))
            proj_fm(W, 1024 + hf * 512, 4, H, TT, c_ga)

            def c_xa(ci, pb, hf=hf):
                j = hf * 4 + ci
                XAb = XA[j % 2]
                XC, RR, II, AA, A2, HS = FA[j % 2]
                XCB = XCBs[j % 2]
                conv_chunk(pb, XAb, CAR0, j, "acw", 8, "acb", XC.ap, XC.r())
                cp(POOL, XCB.ap, XC.ap, XC.r(), XCB.r())
                yield
                p1 = bank()
                mm(PS[p1], GR.ap[:, j, :], XCB.ap, True, True, GR.r() + XCB.r(), PR[p1])
                act(RR.ap, PS[p1], AF.Sigmoid, PR[p1] + PC.r(), RR.r(), bias=pc("arb", j))
                p2 = bank()
                mm(PS[p2], GI.ap[:, j, :], XCB.ap, True, True, GI.r() + XCB.r(), PR[p2])
                act(II.ap, PS[p2], AF.Sigmoid, PR[p2] + PC.r(), II.r(), bias=pc("aib", j))
                act(AA.ap, RR.ap, AF.Exp, RR.r() + CL.r(), AA.r(), scale=CL.ap[:, j:j + 1])
                act(A2.ap, RR.ap, AF.Exp, RR.r() + CL2.r(), A2.r(), scale=CL2.ap[:, j:j + 1])
                act(A2.ap, A2.ap, AF.Sqrt, A2.r(), A2.r(), bias=1.0, scale=-1.0)
                tt(DVE, II.ap, II.ap, XC.ap, ALU.mult, II.r() + XC.r(), II.r())
                tt(DVE, II.ap, II.ap, A2.ap, ALU.mult, II.r() + A2.r(), II.r())
                scan(HS.ap, AA.ap, II.ap, HST.ap[:, j:j + 1], AA.r() + II.r() + HST.r(), HS.r())
                cp(POOL, HST.ap[:, j:j + 1], HS.ap[:, 511:512], HS.r(), HST.r())
                tt(POOL, Y.ap[:, j, :], HS.ap, GATE.ap[:, ci, :], ALU.mult, HS.r() + GATE.r(ci), Y.r(j))
            proj_fm(W, hf * 512, 4, H, TT, c_xa)
        dump("ya", Y.ap[:, 0:8, :], Y.r(0, 8))
        if stop_after == "rglru":
            return

        for hf in range(2):
            def c_g(ci, pb):
                act(GATE2.ap[:, ci, :], PS[pb], AF.Silu, PR[pb], GATE2.r(ci))
            proj_fm(W, 5120 + hf * 512, 4, H, TT, c_g)

            def c_q(ci, pb):
                act(QS.ap[:, ci, :], PS[pb], AF.Silu, PR[pb], QS.r(ci))
            proj_fm(W, 2048 + hf * 512, 4, H, TT, c_q)
            ws = wload(W, 0, 8, 4096 + hf * 512, 512)
            for s in range(4):
                pb = bank()
                for k in range(8):
                    mm(PS[pb], H.ap[:, k, s * 128:(s + 1) * 128], ws.ap[:, k, :], k == 0, k == 7, ws.regs + H.r(k), PR[pb])
                cp(ACT if s % 2 else DVE, VT.ap[:, s, :], PS[pb], PR[pb], VT.r(s))
            if stop_after == "hgA":
                return

            def c_f(ci, pb, hf=hf):
                j = hf * 4 + ci
                FF, LF, BB, E1, E2, _u = FA[j % 2]
                KD = KDs[j % 2]
                act(FF.ap, PS[pb], AF.Sigmoid, PR[pb], FF.r())
                ts(DVE, FF.ap, FF.ap, OML.ap[:, j:j + 1], LB.ap[:, j:j + 1], ALU.mult, ALU.add, FF.r() + OML.r() + LB.r(), FF.r())
                act(LF.ap, FF.ap, AF.Ln, FF.r(), LF.r())
                scan(BB.ap, R64, LF.ap, 0.0, LF.r() + CST.r(), BB.r())
                act(E1.ap, BB.ap, AF.Exp, BB.r(), E1.r())
                act(E2.ap, BB.ap, AF.Exp, BB.r(), E2.r(), scale=-1.0)
                cp(POOL, EL.ap[:, ci, :], E1.ap.rearrange("p (c u) -> p c u", u=64)[:, :, 63], E1.r(), EL.r(ci))
                tt(DVE, QT.ap[:, ci, :], QS.ap[:, ci, :], E1.ap, ALU.mult, QS.r(ci) + E1.r(), QT.r(ci))
                ts(DVE, FF.ap, FF.ap, -1.0, 1.0, ALU.mult, ALU.add, FF.r(), FF.r())
                tt(DVE, KT.ap[:, ci, :], FF.ap, E2.ap, ALU.mult, FF.r() + E2.r(), KT.r(ci))
                tt(POOL, KD.ap.rearrange("p (c u) -> p c u", u=64), KT.ap[:, ci, :].rearrange("p (c u) -> p c u", u=64),
                   EL.ap[:, ci, :].unsqueeze(2).to_broadcast([128, 8, 64]), ALU.mult, KT.r(ci) + EL.r(ci), KD.r())
                yield
                pbt = bank()
                for s in range(4):
                    tr(PSB[pbt][:, s * 128:(s + 1) * 128], KD.ap[:, s * 128:(s + 1) * 128], IDB.ap, KD.r() + IDB.r(), PR[pbt])
                cp(ACT, KDT.ap[:, ci, :], PSB[pbt][:, 0:512], PR[pbt], KDT.r(ci))
            proj_fm(W, 3072 + hf * 512, 4, H, TT, c_f)
            if stop_after == "hgB":
                return

            def head_gen(jj, hf=hf):
                j = hf * 4 + jj
                OF = FA[j % 2][5]
                PT, OSQ, RSh = PTs[j % 2], OSQs[j % 2], RS2[j % 2]
                pbs = bank()
                for s in range(4):
                    sl_ = slice(s * 128, (s + 1) * 128)
                    mm(PS[pbs][:, sl_], KT.ap[:, jj, sl_], QT.ap[:, jj, sl_], s == 0, s == 3, KT.r(jj) + QT.r(jj), PR[pbs])
                tt(DVE, PT.ap, PS[pbs].rearrange("p (s t) -> p s t", s=4), MASKH.unsqueeze(1).to_broadcast([128, 4, 128]),
                   ALU.mult, PR[pbs] + CST.r(), PT.r())
                pd = [bank(), bank()]
                for c in range(8):
                    s, hh = c // 2, c % 2
                    mm(PS[pd[hh]][:, s * 128:(s + 1) * 128],
                       KDT.ap[hh * 64:(hh + 1) * 64, jj, s * 128:(s + 1) * 128],
                       VT.ap[hh * 64:(hh + 1) * 64, s, jj * 128:(jj + 1) * 128], s == 0, s == 3,
                       KDT.r(jj) + VT.r(s), PR[pd[hh]])
                SBj = SB[j % 2]
                cp(POOL, SBj.ap[:, 0, :], ST0.ap[:, j, :], ST0.r(j), SBj.r(0))
                yield
                po = bank()
                for s in range(4):
                    mm(PS[po][:, s * 128:(s + 1) * 128], VT.ap[:, s, jj * 128:(jj + 1) * 128], PT.ap[:, s, :], s == 0, False,
                       VT.r(s) + PT.r(), PR[po])
                for c in range(8):
                    mm(PS[po][:, c * 64:(c + 1) * 64], SBj.ap[:, c, :], QT.ap[:, jj, c * 64:(c + 1) * 64], False, c == 7,
                       SBj.r(c) + QT.r(jj), PR[po])
                    stt(ST0.ap[:, j, :], ST0.ap[:, j, :], EL.ap[:, jj, c:c + 1], PS[pd[c % 2]][:, (c // 2) * 128:(c // 2 + 1) * 128],
                        ALU.mult, ALU.add, ST0.r(j) + EL.r(jj) + PR[pd[c % 2]], ST0.r(j))
                    if c < 7:
                        cp(POOL, SBj.ap[:, c + 1, :], ST0.ap[:, j, :], ST0.r(j), SBj.r(c + 1))
                    yield
                act(OF.ap, PS[po], AF.Copy, PR[po], OF.r())
                act(OSQ.ap, OF.ap, AF.Square, OF.r(), OSQ.r())
                pn = bank()
                mm(PS[pn], ONES.ap, OSQ.ap, True, True, ONES.r() + OSQ.r(), PR[pn])
                yield
                act(RSh.ap, PS[pn], AF.Sqrt, PR[pn] + EPSC.r(), RSh.r(), bias=EPSC.ap[:, 0:1], scale=1.0 / 128)
                recip(RSh.ap, RSh.ap, RSh.r(), RSh.r())
                stt(OF.ap, OF.ap, pc("bng", j), RSh.ap, ALU.mult, ALU.mult, OF.r() + RSh.r() + PC.r(), OF.r())
                tt(POOL, Y.ap[:, 8 + j, :], OF.ap, GATE2.ap[:, jj, :], ALU.mult, OF.r() + GATE2.r(jj), Y.r(8 + j))
            interleave([head_gen(0), head_gen(1)])
            interleave([head_gen(2), head_gen(3)])
        dump("yb", Y.ap[:, 8:16, :], Y.r(8, 16))
        outproj_add("e_w_out", Y, 16)

    def xattn(l):
        kb.off = pers_off
        QA = alloc([8, TT], BF16)
        EX = [alloc([2, TT], BF16) for _ in range(2)]
        RC = [alloc([TT]) for _ in range(2)]
        SQ = alloc([8, TT], BF16)
        norm_fm(X, TT, "gmq", l * 8, H, SQ)

        def c_q(ci, pb):
            act(QA.ap[:, ci, :], PS[pb], AF.Copy, PR[pb], QA.r(ci), scale=1.0 / 16.0)
        proj_fm(f"xq{l}", 0, 8, H, TT, c_q)
        for hd in range(4):
            EXh, RCh = EX[hd % 2], RC[hd % 2]
            for mh in range(2):
                pb = bank()
                for dc in range(2):
                    mm(PS[pb], KF[l].ap[:, 2 * hd + dc, mh * 128:(mh + 1) * 128], QA.ap[:, 2 * hd + dc, :], dc == 0, dc == 1,
                       KF[l].r(2 * hd + dc) + QA.r(2 * hd + dc), PR[pb])
                act(EXh.ap[:, mh, :], PS[pb], AF.Exp, PR[pb], EXh.r(mh))
            pb = bank()
            for mh in range(2):
                mm(PS[pb], ONES.ap, EXh.ap[:, mh, :], mh == 0, mh == 1, ONES.r() + EXh.r(mh), PR[pb])
            recip(RCh.ap, PS[pb], PR[pb], RCh.r())
            for dc in range(2):
                pb = bank()
                c = 2 * hd + dc
                for mh in range(2):
                    mm(PS[pb], VM[l].ap[:, mh, c * 128:(c + 1) * 128], EXh.ap[:, mh, :], mh == 0, mh == 1,
                       VM[l].r(mh) + EXh.r(mh), PR[pb])
                tt(DVE, Y.ap[:, c, :], PS[pb], RCh.ap, ALU.mult, PR[pb] + RCh.r(), Y.r(c))
        outproj_add(f"xo{l}", Y, 8)

    def ffn(l):
        kb.off = pers_off
        UP = alloc([32, TT], BF16)
        RL = [alloc([TT]) for _ in range(2)]
        SQ = alloc([8, TT], BF16)
        norm_fm(X, TT, "gffn", l * 8, H, SQ)

        def c_up(ci, pb):
            R_ = RL[ci % 2]
            act(R_.ap, PS[pb], AF.Relu, PR[pb], R_.r())
            tt(DVE, UP.ap[:, ci, :], R_.ap, PS[pb], ALU.mult, R_.r() + PR[pb], UP.r(ci))
        proj_fm(f"w1_{l}", 0, 32, H, TT, c_up)
        for c in range(8):
            ws = wload(f"w2_{l}", 0, 32, c * 128, 128)
            pb = bank()
            for k in range(32):
                mm(PS[pb], ws.ap[:, k, :], UP.ap[:, k, :], k == 0, k == 31, ws.regs + UP.r(k), PR[pb])
            tt(DVE, X.ap[:, c, :], X.ap[:, c, :], PS[pb], ALU.add, X.r(c) + PR[pb], X.r(c))

    def l1_mixer(t):
        kb.off = pers_off
        DT = alloc([TT])
        ACU = alloc([TT])
        DTT = alloc([4, 32])
        NAT = alloc([4, 32])
        m1 = kb.off
        SQ = alloc([8, TT], BF16)
        norm_fm(X, TT, "gmix", 8, H, SQ)
        kb.off = m1
        DA = alloc([TT])
        XTM = alloc([4, 1024])
        ZS = alloc([4, 1024], BF16)
        BF_ = alloc([4, TT], BF16)
        CF_ = alloc([4, TT], BF16)
        BT = alloc([4, 512], BF16)
        m2 = kb.off
        XA = [alloc([516]) for _ in range(2)]
        XC = [alloc([TT]) for _ in range(2)]
        kb.off = m2
        WS_ = alloc([16])
        ALB = alloc([16])
        EAL = alloc([16])
        SS = alloc([4])
        LL = [alloc([4, 128]) for _ in range(2)]
        LL2 = [alloc([4, 128]) for _ in range(2)]
        CBM = [alloc([128]) for _ in range(2)]
        MT = [alloc([4, 128], BF16) for _ in range(2)]
        CT = [alloc([4, 128], BF16) for _ in range(2)]
        XDT = alloc([1024], BF16)
        XDW = alloc([1024], BF16)
        STB = alloc([1024], BF16)
        YA = alloc([1024])
        YB = alloc([1024])
        YN = alloc([1024], BF16)
        W = "o_w_in"
        ws = wload(W, 0, 8, 6144, 32)
        pb = bank()
        for k in range(8):
            mm(PS[pb][0:32, :], ws.ap[:, k, :], H.ap[:, k, :], k == 0, k == 7, ws.regs + H.r(k), PR[pb])
        act(DT.ap[0:32, :], PS[pb][0:32, :], AF.Exp, PR[pb] + PC.r(), DT.r(), bias=PC.ap[0:32, PCI["dtb"]:PCI["dtb"] + 1])
        act(DT.ap[0:32, :], DT.ap[0:32, :], AF.Ln, DT.r(), DT.r(), bias=1.0)
        ts(DVE, DA.ap[0:32, :], DT.ap[0:32, :], ANEG.ap[0:32, 0:1], None, ALU.mult, None, DT.r() + ANEG.r(), DA.r())
        scan(ACU.ap[0:32, :], R128[0:32, :], DA.ap[0:32, :], 0.0, DA.r() + CST.r(), ACU.r())
        pb = bank()
        for s in range(4):
            tr(PS[pb][:, s * 32:(s + 1) * 32], DT.ap[0:32, s * 128:(s + 1) * 128], IDF[0:32, 0:32], DT.r() + CST.r(), PR[pb])
        cp(DVE, DTT.ap, PS[pb][:, 0:128].rearrange("p (s h) -> p s h", s=4), PR[pb], DTT.r())
        pb = bank()
        for s in range(4):
            tr(PS[pb][:, s * 32:(s + 1) * 32], ACU.ap[0:32, s * 128:(s + 1) * 128], IDF[0:32, 0:32], ACU.r() + CST.r(), PR[pb])
        ts(DVE, NAT.ap, PS[pb][:, 0:128].rearrange("p (s h) -> p s h", s=4), -1.0, None, ALU.mult, None, PR[pb], NAT.r())

        YBANKS = (0, 1)
        OTH = (2, 3, 4, 5, 6, 7)
        for hf in range(2):
            for sl in range(2):
                ws = wload(W, 0, 8, hf * 1024 + sl * 512, 512)
                for s in range(4):
                    pb = bank()
                    for k in range(8):
                        mm(PS[pb], H.ap[:, k, s * 128:(s + 1) * 128], ws.ap[:, k, :], k == 0, k == 7, ws.regs + H.r(k), PR[pb])
                    act(ZS.ap[:, s, sl * 512:(sl + 1) * 512], PS[pb], AF.Silu, PR[pb],
                        ZS.rb(s * 1024 + sl * 512, s * 1024 + sl * 512 + 512))

            def c_x(ci, pb, hf=hf):
                ch = hf * 8 + ci
                XAb, XCb = XA[ci % 2], XC[ci % 2]
                conv_chunk(pb, XAb, CAR1, ch, "mcw", 32, "mcb", XCb.ap, XCb.r())
                act(XCb.ap, XCb.ap, AF.Silu, XCb.r(), XCb.r())
                yield
                pbt = bank()
                for s in range(4):
                    tr(PS[pbt][:, s * 128:(s + 1) * 128], XCb.ap[:, s * 128:(s + 1) * 128], IDF, XCb.r() + CST.r(), PR[pbt])
                cp(DVE, XTM.ap[:, :, ci * 128:(ci + 1) * 128], PS[pbt].rearrange("p (s f) -> p s f", s=4), PR[pbt], XTM.r())
            proj_fm(W, 2048 + hf * 1024, 8, H, TT, c_x)

            def c_b(gi, pb, hf=hf):
                ch = 16 + hf * 4 + gi
                XAb, XCb = XA[gi % 2], XC[gi % 2]
                conv_chunk(pb, XAb, CAR1, ch, "mcw", 32, "mcb", XCb.ap, XCb.r())
                act(BF_.ap[:, gi, :], XCb.ap, AF.Silu, XCb.r(), BF_.r(gi))
                yield
                pbt = bank()
                for s in range(4):
                    tr(PSB[pbt][:, s * 128:(s + 1) * 128], BF_.ap[:, gi, s * 128:(s + 1) * 128], IDB.ap, BF_.r(gi) + IDB.r(), PR[pbt])
                cp(DVE, BT.ap[:, :, gi * 128:(gi + 1) * 128], PSB[pbt][:, 0:512].rearrange("p (s f) -> p s f", s=4), PR[pbt], BT.r())
            proj_fm(W, 4096 + hf * 512, 4, H, TT, c_b)

            def c_c(gi, pb, hf=hf):
                ch = 24 + hf * 4 + gi
                XAb, XCb = XA[gi % 2], XC[gi % 2]
                conv_chunk(pb, XAb, CAR1, ch, "mcw", 32, "mcb", XCb.ap, XCb.r())
                act(CF_.ap[:, gi, :], XCb.ap, AF.Silu, XCb.r(), CF_.r(gi))
            proj_fm(W, 5120 + hf * 512, 4, H, TT, c_c)

            H0 = hf * 16
            for s in range(4):
                cs = slice(s * 128, (s + 1) * 128)
                tt(DVE, XDT.ap.rearrange("p (h q) -> p h q", q=64), XTM.ap[:, s, :].rearrange("p (h q) -> p h q", q=64),
                   DTT.ap[:, s, H0:H0 + 16].unsqueeze(2).to_broadcast([128, 16, 64]), ALU.mult, XTM.r(s) + DTT.r(), XDT.r())
                cp(POOL, STB.ap, ST1.ap[:, hf * 1024:(hf + 1) * 1024], ST1.rb(hf * 1024, hf * 1024 + 1024), STB.r())
                def grp_gen(gl, s=s, cs=cs, H0=H0):
                    i2 = gl % 2
                    pcb = bank(OTH)
                    mm(PS[pcb][:, 0:128], BF_.ap[:, gl, cs], CF_.ap[:, gl, cs], True, True, BF_.r(gl) + CF_.r(gl), PR[pcb])
                    tt(DVE, CBM[i2].ap, PS[pcb][:, 0:128], MASKC, ALU.mult, PR[pcb] + CST.r(), CBM[i2].r())
                    pa = bank(OTH)
                    for hh in range(4):
                        h = H0 + 4 * gl + hh
                        mm(PS[pa][:, hh * 128:(hh + 1) * 128], IDF[0:32, h:h + 1].to_broadcast([32, 128]), ACU.ap[0:32, cs],
                           hh == 0, hh == 3, CST.r() + ACU.r(), PR[pa])
                    for hh in range(4):
                        h = H0 + 4 * gl + hh
                        act(LL[i2].ap[:, hh, :], PS[pa][:, hh * 128:(hh + 1) * 128], AF.Exp, PR[pa] + NAT.r(), LL[i2].r(hh),
                            bias=NAT.ap[:, s, h:h + 1])
                    stt(MT[i2].ap, LL[i2].ap, 1.0, CBM[i2].ap.unsqueeze(1).to_broadcast([128, 4, 128]), ALU.min, ALU.mult,
                        LL[i2].r() + CBM[i2].r(), MT[i2].r())
                    pav = PS[pa].rearrange("p (h t) -> p h t", h=4)
                    cp(DVE, ALB.ap[:, 4 * gl:4 * gl + 4], pav[:, :, 127], PR[pa], ALB.r())
                    act(LL2[i2].ap, pav, AF.Exp, PR[pa], LL2[i2].r())
                    tt(DVE, CT[i2].ap, LL2[i2].ap, CF_.ap[:, gl, cs].unsqueeze(1).to_broadcast([128, 4, 128]), ALU.mult,
                       LL2[i2].r() + CF_.r(gl), CT[i2].r())
                    yield
                    for hh in range(4):
                        hl = 4 * gl + hh
                        yb = YBANKS[hl // 8]
                        oc = slice((hl % 8) * 64, (hl % 8 + 1) * 64)
                        mm(PS[yb][:, oc], MT[i2].ap[:, hh, :], XDT.ap[:, hl * 64:(hl + 1) * 64], (hl % 8 == 0), False,
                           MT[i2].r(hh) + XDT.r(), PR[yb])
                        mm(PS[yb][:, oc], CT[i2].ap[:, hh, :], STB.ap[:, hl * 64:(hl + 1) * 64], False, (hl % 8 == 7),
                           CT[i2].r(hh) + STB.r(), PR[yb])
                pend_ = []
                for gl in range(4):
                    g_ = grp_gen(gl)
                    next(g_)
                    step_all(pend_)
                    pend_.append(g_)
                drain(pend_)
                STh = ST1.ap[:, hf * 1024:(hf + 1) * 1024]
                STr = ST1.rb(hf * 1024, hf * 1024 + 1024)
                act(EAL.ap, ALB.ap, AF.Exp, ALB.r(), EAL.r())
                tt(DVE, WS_.ap, ALB.ap, NAT.ap[:, s, H0:H0 + 16], ALU.add, ALB.r() + NAT.r(), WS_.r())
                act(WS_.ap, WS_.ap, AF.Exp, WS_.r(), WS_.r())
                tt(DVE, WS_.ap, WS_.ap, DTT.ap[:, s, H0:H0 + 16], ALU.mult, WS_.r() + DTT.r(), WS_.r())
                tt(DVE, XDW.ap.rearrange("p (h q) -> p h q", q=64), XTM.ap[:, s, :].rearrange("p (h q) -> p h q", q=64),
                   WS_.ap.unsqueeze(2).to_broadcast([128, 16, 64]), ALU.mult, XTM.r(s) + WS_.r(), XDW.r())
                tt(POOL, STh.rearrange("p (h q) -> p h q", q=64), STh.rearrange("p (h q) -> p h q", q=64),
                   EAL.ap.unsqueeze(2).to_broadcast([128, 16, 64]), ALU.mult, STr + EAL.r(), STr)
                for half in range(2):
                    pd = bank(OTH)
                    for gg in range(2):
                        gl = half * 2 + gg
                        mm(PS[pd][:, gg * 256:(gg + 1) * 256], BT.ap[:, s, gl * 128:(gl + 1) * 128], XDW.ap[:, gl * 256:(gl + 1) * 256],
                           gg == 0, gg == 1, BT.r(s) + XDW.r(), PR[pd])
                    sl_ = slice(half * 512, (half + 1) * 512)
                    tt(DVE, STh[:, sl_], STh[:, sl_], PS[pd], ALU.add, STr + PR[pd], STr)
                tt(POOL, YA.ap.rearrange("p (h q) -> p h q", q=64), XTM.ap[:, s, :].rearrange("p (h q) -> p h q", q=64),
                   DBC.ap[:, H0:H0 + 16].unsqueeze(2).to_broadcast([128, 16, 64]), ALU.mult, XTM.r(s) + DBC.r(), YA.r())
                for q in range(2):
                    sl_ = slice(q * 512, (q + 1) * 512)
                    tt(DVE, YA.ap[:, sl_], YA.ap[:, sl_], PS[YBANKS[q]], ALU.add, YA.r() + PR[YBANKS[q]], YA.r())
                tt(POOL, YA.ap, YA.ap, ZS.ap[:, s, :], ALU.mult, YA.r() + ZS.r(s), YA.r())
                tt(POOL, YB.ap, YA.ap, YA.ap, ALU.mult, YA.r(), YB.r())
                op(DVE, lambda: nc.vector.tensor_reduce(out=SS.ap, in_=YB.ap.rearrange("p (g q) -> p g q", q=256),
                                                        axis=mybir.AxisListType.X, op=ALU.add), YB.r(), SS.r())
                act(SS.ap, SS.ap, AF.Sqrt, SS.r() + EPSC.r(), SS.r(), bias=EPSC.ap[:, 0:1], scale=1.0 / 256)
                recip(SS.ap, SS.ap, SS.r(), SS.r())
                tt(DVE, YA.ap.rearrange("p (g q) -> p g q", q=256), YA.ap.rearrange("p (g q) -> p g q", q=256),
                   SS.ap.unsqueeze(2).to_broadcast([128, 4, 256]), ALU.mult, YA.r() + SS.r(), YA.r())
                tt(POOL, YN.ap, YA.ap, NG.ap[:, hf * 1024:(hf + 1) * 1024], ALU.mult, YA.r() + NG.r(), YN.r())
                for q in range(2):
                    pbt = bank(OTH)
                    for cc in range(4):
                        c = q * 4 + cc
                        tr(PSB[pbt][:, cc * 128:(cc + 1) * 128], YN.ap[:, c * 128:(c + 1) * 128], IDB.ap, YN.r() + IDB.r(), PR[pbt])
                    c0 = hf * 8 + q * 4
                    cp(ACT, Y.ap[:, c0:c0 + 4, cs], PSB[pbt][:, 0:512].rearrange("p (c t) -> p c t", c=4), PR[pbt],
                       Y.r(c0, c0 + 4))
        dump("ym", Y.ap, Y.r())
        outproj_add("o_w_out", Y, 16)

    def final(t):
        kb.off = pers_off
        HF = alloc([8, TT])
        IO = alloc([4, 1024])
        SQ = alloc([8, TT], BF16)
        norm_fm(X, TT, "gfin", 0, HF, SQ)
        for s in range(4):
            for hf in range(2):
                pb = bank()
                for cc in range(4):
                    c = hf * 4 + cc
                    tr(PS[pb][:, cc * 128:(cc + 1) * 128], HF.ap[:, c, s * 128:(s + 1) * 128], IDF, HF.r(c) + CST.r(), PR[pb])
                cp(ACT if hf else DVE, IO.ap[:, s, hf * 512:(hf + 1) * 512], PS[pb], PR[pb], IO.r(s))
        dma(POOL, T_OUT, out_d[t * TT:(t + 1) * TT, :].rearrange("(s p) f -> p s f", p=128), IO.ap, IO.r(), [])

    stages = ["l0", "a0", "f0", "l1", "a1", "f1"]
    for t in range(NT):
        load_x(t)
        for st in stages:
            if stop_after == "load":
                break
            if st == "l0":
                l0_mixer(t)
            elif st == "a0":
                xattn(0)
            elif st == "f0":
                ffn(0)
            elif st == "l1":
                l1_mixer(t)
            elif st == "a1":
                xattn(1)
            elif st == "f1":
                ffn(1)
            if stop_after == st or (stop_after in ("rglru", "hgA", "hgB") and st == "l0"):
                break
        final(t)
    nc.gpsimd.wait_ge(T_OUT.sem, T_OUT.n * 16)
    for t_ in kb.dbgt:
        nc.gpsimd.wait_ge(t_.sem, t_.n * 16)
    es.close()
    kb.counts = counts
    kb.wneed = wneed
    return nc, kb


def _cols(v):
    v = np.asarray(v, np.float32).reshape(-1)
    return np.ascontiguousarray(v.reshape(v.size // 128, 128).T)


def make_shared_inputs(inp):
    pc = np.zeros((128, NPC), np.float32)

    def put(name, arr):
        a = _cols(arr)
        pc[:, PCI[name]:PCI[name] + a.shape[1]] = a
    put("gmix", inp["norm_mix_g"])
    put("gmq", inp["norm_mem_q_g"])
    put("gmkv", inp["norm_mem_kv_g"])
    put("gffn", inp["norm_ffn_g"])
    put("gfin", inp["final_norm_g"])
    put("acw", inp["a_conv_w"][0])
    put("acb", inp["a_conv_b"][0])
    put("arb", inp["a_gate_r_b"][0])
    put("aib", inp["a_gate_i_b"][0])
    put("alam", inp["a_lambda"][0])
    put("lbl", inp["b_lb_logits"])
    put("bng", inp["b_norm_g"][0])
    put("mcw", inp["m_conv_w"][0])
    put("mcb", inp["m_conv_b"][0])
    pc[:32, PCI["dtb"]] = np.asarray(inp["m_dt_bias"][0], np.float32)
    pc[:32, PCI["alog"]] = np.asarray(inp["m_a_log"][0], np.float32)
    cst = np.zeros((128, NCST), np.float32)
    cst[:, C_ID:C_ID + 128] = np.eye(128, dtype=np.float32)
    s_ = np.arange(128)[:, None]
    t_ = np.arange(128)[None, :]
    cst[:, C_MH:C_MH + 128] = ((s_ <= t_) & ((s_ // 64) == (t_ // 64))).astype(np.float32)
    cst[:, C_MC:C_MC + 128] = (s_ <= t_).astype(np.float32)
    cst[:, C_R64:C_R64 + 512] = (np.arange(512) % 64 != 0).astype(np.float32)[None, :]
    cst[:, C_R128:C_R128 + 512] = (np.arange(512) % 128 != 0).astype(np.float32)[None, :]
    sh = {"pc": pc, "cst": cst}
    sh["gr"] = np.ascontiguousarray(np.asarray(inp["a_gate_r_w"][0], np.float32).transpose(1, 0, 2).reshape(128, 1024))
    sh["gi"] = np.ascontiguousarray(np.asarray(inp["a_gate_i_w"][0], np.float32).transpose(1, 0, 2).reshape(128, 1024))
    sh["mng"] = np.asarray(inp["m_norm_g"], np.float32).reshape(1, 2048)
    sh["md"] = np.repeat(np.asarray(inp["m_d"], np.float32).reshape(1, 32), 1, axis=0)
    sh["e_w_in"] = np.asarray(inp["e_w_in"][0], np.float32)
    sh["e_w_out"] = np.asarray(inp["e_w_out"][0], np.float32)
    sh["o_w_in"] = np.asarray(inp["o_w_in"][0], np.float32)
    sh["o_w_out"] = np.asarray(inp["o_w_out"][0], np.float32)
    for l in range(2):
        sh[f"xq{l}"] = np.asarray(inp["xq_w"][l], np.float32)
        sh[f"xk{l}"] = np.asarray(inp["xk_w"][l], np.float32)
        sh[f"xv{l}"] = np.asarray(inp["xv_w"][l], np.float32)
        sh[f"xo{l}"] = np.asarray(inp["xo_w"][l], np.float32)
        sh[f"w1_{l}"] = np.asarray(inp["ffn_w1"][l], np.float32)
        sh[f"w2_{l}"] = np.asarray(inp["ffn_w2"][l], np.float32)
    return sh


_CACHE = {}


def kernel(**inputs):
    x = np.asarray(inputs["x"], np.float32)
    mem = np.asarray(inputs["mem"], np.float32)
    B, S, _ = x.shape
    if S not in _CACHE:
        _CACHE[S] = build(S)[0]
    nc = _CACHE[S]
    sh = make_shared_inputs(inputs)
    in_maps = []
    for b in range(B):
        m = dict(sh)
        m["x"] = np.ascontiguousarray(x[b])
        m["mem"] = np.ascontiguousarray(mem[b])
        in_maps.append(m)
    res = run_bass_kernel_spmd(nc, in_maps, core_ids=list(range(B)))
    return np.stack([np.asarray(r["out"], np.float32) for r in res.results], axis=0)
```

```python
import numpy as np
from contextlib import ExitStack
import concourse.bass as bass
import concourse.mybir as mybir
from concourse.bass_utils import run_bass_kernel_spmd

F32 = mybir.dt.float32
BF16 = mybir.dt.bfloat16
AF = mybir.ActivationFunctionType
ALU = mybir.AluOpType
D = 1024
TT = 512
EPS = 1e-6
MEM = 256

PCI = {}
_n = 0
for _nm, _c in [("gmix", 16), ("gmq", 16), ("gmkv", 16), ("gffn", 16), ("gfin", 8), ("acw", 32), ("acb", 8),
                ("arb", 8), ("aib", 8), ("alam", 8), ("lbl", 24), ("bng", 8), ("mcw", 128), ("mcb", 32),
                ("dtb", 1), ("alog", 1)]:
    PCI[_nm] = _n
    _n += _c
NPC = _n
C_ID, C_MH, C_MC, C_R64, C_R128 = 0, 128, 256, 384, 896
NCST = 1408

WSHAPES = {"e_w_in": (1024, 6144), "e_w_out": (2048, 1024), "o_w_in": (1024, 6176), "o_w_out": (2048, 1024),
           "xq0": (1024, 1024), "xk0": (1024, 1024), "xv0": (1024, 1024), "xo0": (1024, 1024),
           "xq1": (1024, 1024), "xk1": (1024, 1024), "xv1": (1024, 1024), "xo1": (1024, 1024),
           "w1_0": (1024, 4096), "w2_0": (4096, 1024), "w1_1": (1024, 4096), "w2_1": (4096, 1024)}
WORDER = ["e_w_in", "e_w_out", "xk0", "xv0", "xk1", "xv1", "xq0", "xo0", "w1_0", "w2_0",
          "o_w_in", "o_w_out", "xq1", "xo1", "w1_1", "w2_1"]


class Reg:
    __slots__ = ("w", "r")

    def __init__(self):
        self.w = None
        self.r = {}


class Trk:
    def __init__(self, sem, inc):
        self.sem = sem
        self.inc = inc
        self.n = 0


class Eng:
    def __init__(self, eng, trk, is_pe=False):
        self.eng = eng
        self.trk = trk
        self.seen = {}
        self.is_pe = is_pe


class View:
    def __init__(self, kb, b0, dtype, shape):
        self.kb = kb
        self.b0 = b0
        self.es = 4 if dtype == F32 else 2
        self.shape = tuple(shape)
        n = int(np.prod(shape))
        self.nbytes = n * self.es
        base = kb.arena[:, b0 // 4:(b0 + self.nbytes) // 4]
        ap = base if dtype == F32 else base.bitcast(BF16)
        if len(shape) == 2:
            ap = ap.rearrange("p (a b) -> p a b", a=shape[0])
        elif len(shape) == 3:
            ap = ap.rearrange("p (a b c) -> p a b c", a=shape[0], b=shape[1])
        self.ap = ap
        self.slot = self.nbytes // shape[0] if len(shape) > 1 else self.nbytes

    def r(self, i=None, j=None):
        G = self.kb.G
        if i is None:
            lo, hi = self.b0, self.b0 + self.nbytes
        else:
            if j is None:
                j = i + 1
            lo, hi = self.b0 + i * self.slot, self.b0 + j * self.slot
        return self.kb.regs[lo // G:(hi + G - 1) // G]

    def rb(self, lo_el, hi_el):
        G = self.kb.G
        lo, hi = self.b0 + lo_el * self.es, self.b0 + hi_el * self.es
        return self.kb.regs[lo // G:(hi + G - 1) // G]


def build(S, stop_after=None, dbg=None):
    NT = S // TT
    nc = bass.Bass("TRN2", target_bir_lowering=False)
    kb = type("KB", (), {})()
    es = ExitStack()
    E_ = es.enter_context

    def din(name, shape, dt=F32):
        return nc.dram_tensor(name, list(shape), dt, kind="ExternalInput").ap()

    x_d = din("x", [S, D])
    mem_d = din("mem", [MEM, D])
    pc_d = din("pc", [128, NPC])
    cst_d = din("cst", [128, NCST])
    gr_d = din("gr", [128, 1024])
    gi_d = din("gi", [128, 1024])
    mng_d = din("mng", [1, 2048])
    md_d = din("md", [1, 32])
    _stage_w = {"load": [], "rglru": ["e_w_in"], "hgA": [], "hgB": [], "l0": ["e_w_in", "e_w_out"], "a0": ["xq0", "xo0"], "f0": ["w1_0", "w2_0"],
                "l1": ["o_w_in", "o_w_out"], "a1": ["xq1", "xo1"], "f1": ["w1_1", "w2_1"]}
    wneed = ["xk0", "xv0", "xk1", "xv1"]
    for _st in ["load", "rglru", "hgA", "hgB", "l0", "a0", "f0", "l1", "a1", "f1"]:
        wneed += _stage_w[_st]
        if stop_after == _st:
            break
    wneed = [k for k in WORDER if k in set(wneed)]
    wf = {k: din(k, WSHAPES[k]) for k in wneed}
    wb = {k: nc.dram_tensor(k + "_b", list(WSHAPES[k]), BF16, kind="Internal").ap() for k in wneed}
    out_d = nc.dram_tensor("out", [S, D], F32, kind="ExternalOutput").ap()
    dbg_d = {}
    if dbg:
        for nm, shp in dbg.items():
            dbg_d[nm] = nc.dram_tensor("dbg_" + nm, list(shp), F32, kind="ExternalOutput").ap()

    ARENA_BYTES = 206 * 1024
    kb.G = 512
    kb.arena = E_(nc.sbuf_tensor("arena", [128, ARENA_BYTES // 4], F32))[:]
    kb.regs = [Reg() for _ in range(ARENA_BYTES // kb.G)]
    kb.off = 0
    kb.dbgt = []

    def alloc(shape, dtype=F32):
        v = View(kb, kb.off, dtype, shape)
        kb.off += (v.nbytes + kb.G - 1) // kb.G * kb.G
        assert kb.off <= ARENA_BYTES, f"SBUF arena overflow {kb.off}"
        return v

    def mksem(name):
        return E_(nc.semaphore(name))

    PE = Eng(nc.tensor, Trk(mksem("s_pe"), 1), is_pe=True)
    ACT = Eng(nc.scalar, Trk(mksem("s_act"), 1))
    DVE = Eng(nc.vector, Trk(mksem("s_dve"), 1))
    POOL = Eng(nc.gpsimd, Trk(mksem("s_pool"), 1))
    SP = Eng(nc.sync, None)
    kb.nsem = 0

    def newtrk():
        kb.nsem += 1
        return Trk(mksem(f"s_dma{kb.nsem}"), 16)
    NCAST = 6
    T_CAST = [newtrk() for _ in range(NCAST)]
    T_XIN = newtrk()
    T_OUT = newtrk()
    counts = {"wait": 0, "ins": 0}

    def op(E, fn, r=(), w=(), trk=None):
        trk = trk or E.trk
        need = {}
        for g in r:
            if g.w is not None:
                t, c = g.w
                if need.get(t, 0) < c:
                    need[t] = c
        for g in w:
            if g.w is not None:
                t, c = g.w
                if need.get(t, 0) < c:
                    need[t] = c
            for t, c in g.r.items():
                if need.get(t, 0) < c:
                    need[t] = c
        for t, c in need.items():
            if E.is_pe and t is E.trk:
                continue
            if E.seen.get(t, 0) < c:
                E.eng.wait_ge(t.sem, c * t.inc)
                E.seen[t] = c
                counts["wait"] += 1
        ins = fn()
        trk.n += 1
        ins.then_inc(trk.sem, trk.inc)
        counts["ins"] += 1
        n = trk.n
        for g in r:
            if g.r.get(trk, 0) < n:
                g.r[trk] = n
        for g in w:
            g.w = (trk, n)
            g.r = {}

    PSt = [E_(nc.psum_tensor(f"ps{i}", [128, 512], F32)) for i in range(8)]
    PS = [t[:] for t in PSt]
    PSB = [t[:].bitcast(BF16) for t in PSt]
    PR = [[Reg()] for _ in range(8)]
    kb.pb = 0

    def bank(pool=(0, 1, 2, 3, 4, 5, 6, 7)):
        kb.pb = (kb.pb + 1) % len(pool)
        return pool[kb.pb]

    def mm(out, lhsT, rhs, start, stop, r, w):
        op(PE, lambda: nc.tensor.matmul(out, lhsT=lhsT, rhs=rhs, start=start, stop=stop), r, w)

    def tr(out, in_, ident, r, w):
        op(PE, lambda: nc.tensor.transpose(out=out, in_=in_, identity=ident), r, w)

    def act(out, in_, func, r, w, bias=None, scale=None):
        kw = {}
        if bias is not None:
            kw["bias"] = bias
        if scale is not None:
            kw["scale"] = scale
        op(ACT, lambda: nc.scalar.activation(out=out, in_=in_, func=func, **kw), r, w)

    def ts(E, out, in0, s1, s2, op0, op1, r, w):
        if s2 is None:
            op(E, lambda: E.eng.tensor_scalar(out=out, in0=in0, scalar1=s1, scalar2=None, op0=op0), r, w)
        else:
            op(E, lambda: E.eng.tensor_scalar(out=out, in0=in0, scalar1=s1, scalar2=s2, op0=op0, op1=op1), r, w)

    def tt(E, out, in0, in1, o, r, w):
        op(E, lambda: E.eng.tensor_tensor(out=out, in0=in0, in1=in1, op=o), r, w)

    def stt(out, in0, scalar, in1, op0, op1, r, w):
        op(DVE, lambda: nc.vector.scalar_tensor_tensor(out=out, in0=in0, scalar=scalar, in1=in1, op0=op0, op1=op1), r, w)

    def scan(out, d0, d1, initial, r, w):
        op(DVE, lambda: nc.vector.tensor_tensor_scan(out=out, data0=d0, data1=d1, initial=initial,
                                                     op0=ALU.mult, op1=ALU.add), r, w)

    def cp(E, out, in_, r, w):
        if E is ACT:
            op(ACT, lambda: nc.scalar.activation(out=out, in_=in_, func=AF.Copy), r, w)
        else:
            op(E, lambda: E.eng.tensor_copy(out=out, in_=in_), r, w)

    def recip(out, in_, r, w):
        op(DVE, lambda: nc.vector.reciprocal(out=out, in_=in_), r, w)

    def dma(E, trk, out, in_, r, w):
        op(E, lambda: E.eng.dma_start(out=out, in_=in_), r, w, trk=trk)

    WBR = {k: [Reg() for _ in range(v[0] // 128)] for k, v in WSHAPES.items()}

    PC = alloc([NPC])
    CST = alloc([NCST])
    IDB = alloc([128], BF16)
    ONES = alloc([128], BF16)
    EPSC = alloc([1])
    NG = alloc([2048])
    DBC = alloc([32])
    GR = alloc([8, 128], BF16)
    GI = alloc([8, 128], BF16)
    CL = alloc([8])
    CL2 = alloc([8])
    LB = alloc([8])
    OML = alloc([8])
    ANEG = alloc([1])
    KF = [alloc([8, MEM], BF16) for _ in range(2)]
    VM = [alloc([2, 1024], BF16) for _ in range(2)]
    X = alloc([8, TT])
    H = alloc([8, TT], BF16)
    Y = alloc([16, TT], BF16)
    RS = alloc([TT])
    CAR0 = alloc([8, 4])
    HST = alloc([8])
    ST0 = alloc([8, 128])
    CAR1 = alloc([32, 4])
    ST1 = alloc([2048])
    NRING = 4
    RING = [alloc([8 * 512], BF16) for _ in range(NRING)]
    T_RING = [newtrk() for _ in range(NRING)]
    kb.ring = 0
    pers_off = kb.off

    def pc(name, i):
        c = PCI[name] + i
        return PC.ap[:, c:c + 1]

    IDF = CST.ap[:, C_ID:C_ID + 128]
    MASKH = CST.ap[:, C_MH:C_MH + 128]
    MASKC = CST.ap[:, C_MC:C_MC + 128]
    R64 = CST.ap[:, C_R64:C_R64 + 512]
    R128 = CST.ap[:, C_R128:C_R128 + 512]

    class Slab:
        pass

    def wload(name, k0, kc, c0, ncols):
        assert kc * ncols * 2 <= 8192
        v = RING[kb.ring]
        trk_ = T_RING[kb.ring]
        kb.ring = (kb.ring + 1) % NRING
        s = Slab()
        base = v.ap[:, 0:kc * ncols]
        s.ap = base.rearrange("p (k n) -> p k n", k=kc)
        s.regs = v.rb(0, kc * ncols)
        src = wb[name][k0 * 128:(k0 + kc) * 128, c0:c0 + ncols].rearrange("(k p) n -> p k n", p=128)
        dma(SP, trk_, s.ap, src, r=WBR[name][k0:k0 + kc], w=s.regs)
        return s

    dma(SP, newtrk(), PC.ap, pc_d[:, :], [], PC.r())
    dma(SP, newtrk(), CST.ap, cst_d[:, :], [], CST.r())
    dma(SP, newtrk(), NG.ap, mng_d.partition_broadcast(128), [], NG.r())
    dma(SP, newtrk(), DBC.ap, md_d.partition_broadcast(128), [], DBC.r())
    kb.ncast = 0
    for name in wneed:
        K_, N_ = WSHAPES[name]
        for kb_ in range(K_ // 128):
            tc_ = T_CAST[kb.ncast % NCAST]
            kb.ncast += 1
            if tc_.n > 0:
                nc.gpsimd.wait_ge(tc_.sem, tc_.n * 16)
            dma(POOL, tc_, wb[name][kb_ * 128:(kb_ + 1) * 128, :], wf[name][kb_ * 128:(kb_ + 1) * 128, :],
                [], [WBR[name][kb_]])

    m0 = kb.off
    TMPF = alloc([1024])
    dma(SP, newtrk(), TMPF.ap, gr_d[:, :], [], TMPF.r())
    cp(DVE, GR.ap, TMPF.ap.rearrange("p (a b) -> p a b", a=8), TMPF.r(), GR.r())
    TMPG = alloc([1024])
    dma(SP, newtrk(), TMPG.ap, gi_d[:, :], [], TMPG.r())
    cp(DVE, GI.ap, TMPG.ap.rearrange("p (a b) -> p a b", a=8), TMPG.r(), GI.r())
    cp(DVE, IDB.ap, IDF, CST.r(), IDB.r())
    op(DVE, lambda: nc.vector.memset(ONES.ap, 1.0), [], ONES.r())
    op(DVE, lambda: nc.vector.memset(EPSC.ap, EPS), [], EPSC.r())
    for v in (CAR0, HST, ST0, CAR1, ST1):
        op(DVE, lambda v=v: nc.vector.memset(v.ap, 0.0), [], v.r())
    T8 = alloc([24])
    act(T8.ap[:, 0:8], PC.ap[:, PCI["alam"]:PCI["alam"] + 8], AF.Exp, PC.r(), T8.r(), scale=-1.0)
    act(T8.ap[:, 0:8], T8.ap[:, 0:8], AF.Ln, T8.r(), T8.r(), bias=1.0)
    ts(DVE, CL.ap, T8.ap[:, 0:8], -8.0, None, ALU.mult, None, T8.r(), CL.r())
    ts(DVE, CL2.ap, T8.ap[:, 0:8], -16.0, None, ALU.mult, None, T8.r(), CL2.r())
    act(T8.ap, PC.ap[:, PCI["lbl"]:PCI["lbl"] + 24], AF.Exp, PC.r(), T8.r())
    tt(DVE, LB.ap, T8.ap[:, 0:8], T8.ap[:, 8:16], ALU.add, T8.r(), LB.r())
    tt(DVE, LB.ap, LB.ap, T8.ap[:, 16:24], ALU.add, T8.r() + LB.r(), LB.r())
    recip(LB.ap, LB.ap, LB.r(), LB.r())
    tt(DVE, LB.ap, LB.ap, T8.ap[:, 0:8], ALU.mult, LB.r() + T8.r(), LB.r())
    ts(DVE, OML.ap, LB.ap, -1.0, 1.0, ALU.mult, ALU.add, LB.r(), OML.r())
    act(ANEG.ap, pc("alog", 0), AF.Exp, PC.r(), ANEG.r())
    ts(DVE, ANEG.ap, ANEG.ap, -1.0, None, ALU.mult, None, ANEG.r(), ANEG.r())

    def norm_fm(Xv, n, gname, goff, out, tmp_sq):
        for c in range(8):
            act(tmp_sq.ap[:, c, :n], Xv.ap[:, c, :n], AF.Square, Xv.r(c), tmp_sq.r(c))
        pb = bank()
        for c in range(8):
            mm(PS[pb][:, :n], ONES.ap, tmp_sq.ap[:, c, :n], c == 0, c == 7, tmp_sq.r(c) + ONES.r(), PR[pb])
        act(RS.ap[:, :n], PS[pb][:, :n], AF.Sqrt, PR[pb] + EPSC.r(), RS.r(), bias=EPSC.ap[:, 0:1], scale=1.0 / D)
        recip(RS.ap[:, :n], RS.ap[:, :n], RS.r(), RS.r())
        for c in range(8):
            stt(out.ap[:, c, :n], Xv.ap[:, c, :n], pc(gname, goff + c), RS.ap[:, :n], ALU.mult, ALU.mult,
                Xv.r(c) + RS.r() + PC.r(), out.r(c))

    def step_all(pend):
        for g_ in list(pend):
            try:
                next(g_)
            except StopIteration:
                pend.remove(g_)

    def drain(pend):
        while pend:
            step_all(pend)

    def interleave(gens):
        pend = []
        for g_ in gens:
            pend.append(g_)
        drain(pend)

    def proj_fm(name, c0, nchunks, Hv, n, consumer, kc=8, k0=0, pend=None):
        own = pend is None
        if own:
            pend = []
        done = 0
        while done < nchunks:
            g = min(4, nchunks - done)
            ws = wload(name, k0, kc, c0 + done * 128, g * 128)
            for cc in range(g):
                pb = bank()
                for k in range(kc):
                    mm(PS[pb][:, :n], ws.ap[:, k, cc * 128:(cc + 1) * 128], Hv.ap[:, k, :n], k == 0, k == kc - 1,
                       ws.regs + Hv.r(k), PR[pb])
                gen_ = consumer(done + cc, pb)
                started = None
                if gen_ is not None:
                    try:
                        next(gen_)
                        started = gen_
                    except StopIteration:
                        pass
                step_all(pend)
                if started is not None:
                    pend.append(started)
            done += g
        if own:
            drain(pend)

    def outproj_add(name, Yv, kc):
        ncol = 8192 // (kc * 2)
        per = ncol // 128
        for g0 in range(0, 8, per):
            ws = wload(name, 0, kc, g0 * 128, ncol)
            for cc in range(per):
                c = g0 + cc
                pb = bank()
                for k in range(kc):
                    mm(PS[pb], ws.ap[:, k, cc * 128:(cc + 1) * 128], Yv.ap[:, k, :], k == 0, k == kc - 1,
                       ws.regs + Yv.r(k), PR[pb])
                tt(DVE, X.ap[:, c, :], X.ap[:, c, :], PS[pb], ALU.add, X.r(c) + PR[pb], X.r(c))

    def dump(nm, view_ap, regs):
        if nm in dbg_d:
            kb.dbgt.append(newtrk())
            dma(POOL, kb.dbgt[-1], dbg_d[nm], view_ap, regs, [])

    def mem_kv():
        kb.off = pers_off
        MIN = alloc([2, 1024])
        MX = alloc([8, MEM])
        MH = alloc([8, MEM], BF16)
        MSQ = alloc([8, MEM], BF16)
        dma(SP, newtrk(), MIN.ap, mem_d.rearrange("(s p) f -> p s f", p=128), [], MIN.r())
        for c in range(8):
            pb = bank()
            for s in range(2):
                tr(PS[pb][:, s * 128:(s + 1) * 128], MIN.ap[:, s, c * 128:(c + 1) * 128], IDF, MIN.r(s) + CST.r(), PR[pb])
            cp(ACT, MX.ap[:, c, :], PS[pb][:, 0:MEM], PR[pb], MX.r(c))
        for l in range(2):
            norm_fm(MX, MEM, "gmkv", l * 8, MH, MSQ)

            def kcons(ci, pb, l=l):
                cp(ACT, KF[l].ap[:, ci, :], PS[pb][:, :MEM], PR[pb], KF[l].r(ci))
            proj_fm(f"xk{l}", 0, 8, MH, MEM, kcons)
            for sl in range(2):
                ws = wload(f"xv{l}", 0, 8, sl * 512, 512)
                for mh in range(2):
                    pb = bank()
                    for k in range(8):
                        mm(PS[pb], MH.ap[:, k, mh * 128:(mh + 1) * 128], ws.ap[:, k, :], k == 0, k == 7,
                           ws.regs + MH.r(k), PR[pb])
                    cp(DVE, VM[l].ap[:, mh, sl * 512:(sl + 1) * 512], PS[pb], PR[pb], VM[l].rb(mh * 1024 + sl * 512, mh * 1024 + sl * 512 + 512))


    def load_x(t):
        kb.off = pers_off
        IO = alloc([4, 1024])
        dma(SP, T_XIN, IO.ap, x_d[t * TT:(t + 1) * TT, :].rearrange("(s p) f -> p s f", p=128), [], IO.r())
        for c in range(8):
            pb = bank()
            for s in range(4):
                tr(PS[pb][:, s * 128:(s + 1) * 128], IO.ap[:, s, c * 128:(c + 1) * 128], IDF, IO.r(s) + CST.r(), PR[pb])
            cp(ACT if c % 2 else DVE, X.ap[:, c, :], PS[pb], PR[pb], X.r(c))

    def conv_chunk(pb, XAv, CARv, j, wname, wstride, bname, XCv_ap, XC_regs, TMPv=None):
        cp(POOL, XAv.ap[:, 0:3], CARv.ap[:, j, 0:3], CARv.r(j), XAv.r())
        cp(ACT, XAv.ap[:, 3:515], PS[pb], PR[pb], XAv.r())
        cp(POOL, CARv.ap[:, j, 0:3], XAv.ap[:, 512:515], XAv.r(), CARv.r(j))
        act(XCv_ap, PS[pb], AF.Identity, PR[pb] + PC.r(), XC_regs, bias=pc(bname, j), scale=pc(wname, 3 * wstride + j))
        for k in range(3):
            stt(XCv_ap, XAv.ap[:, k:k + 512], pc(wname, k * wstride + j), XCv_ap, ALU.mult, ALU.add,
                XAv.r() + XC_regs + PC.r(), XC_regs)

    def l0_mixer(t):
        kb.off = pers_off
        SQ = alloc([8, TT], BF16)
        norm_fm(X, TT, "gmix", 0, H, SQ)
        kb.off = pers_off
        FA = [[alloc([TT]) for _ in range(6)] for _ in range(2)]
        GATE = alloc([4, TT], BF16)
        GATE2 = alloc([4, TT], BF16)
        QS = alloc([4, TT], BF16)
        QT = alloc([4, TT], BF16)
        KT = alloc([4, TT], BF16)
        KDT = alloc([4, TT], BF16)
        VT = alloc([4, 512], BF16)
        XA = [alloc([516]) for _ in range(2)]
        XCBs = [alloc([TT], BF16) for _ in range(2)]
        KDs = [alloc([TT], BF16) for _ in range(2)]
        PTs = [alloc([4, 128], BF16) for _ in range(2)]
        SB = [alloc([8, 128], BF16) for _ in range(2)]
        EL = alloc([4, 8])
        OSQs = [alloc([TT], BF16) for _ in range(2)]
        RS2 = [alloc([TT]) for _ in range(2)]
        W = "e_w_in"

        for hf in range(2):
            def c_ga(ci, pb):
                act(GATE.ap[:, ci, :], PS[pb], AF.Gelu, PR[pb], GATE.r(ci))
            proj_fm(W, 1024 + hf * 512, 4, H, TT, c_ga)

            def c_xa(ci, pb, hf=hf):
                j = hf * 4 + ci
                XAb = XA[j % 2]
                XC, RR, II, AA, A2, HS = FA[j % 2]
                XCB = XCBs[j % 2]
                conv_chunk(pb, XAb, CAR0, j, "acw", 8, "acb", XC.ap, XC.r())
                cp(POOL, XCB.ap, XC.ap, XC.r(), XCB.r())
                yield
                p1 = bank()
                mm(PS[p1], GR.ap[:, j, :], XCB.ap, True, True, GR.r() + XCB.r(), PR[p1])
                act(RR.ap, PS[p1], AF.Sigmoid, PR[p1] + PC.r(), RR.r(), bias=pc("arb", j))
                p2 = bank()
                mm(PS[p2], GI.ap[:, j, :], XCB.ap, True, True, GI.r() + XCB.r(), PR[p2])
                act(II.ap, PS[p2], AF.Sigmoid, PR[p2] + PC.r(), II.r(), bias=pc("aib", j))
                act(AA.ap, RR.ap, AF.Exp, RR.r() + CL.r(), AA.r(), scale=CL.ap[:, j:j + 1])
                act(A2.ap, RR.ap, AF.Exp, RR.r() + CL2.r(), A2.r(), scale=CL2.ap[:, j:j + 1])
                act(A2.ap, A2.ap, AF.Sqrt, A2.r(), A2.r(), bias=1.0, scale=-1.0)
                tt(DVE, II.ap, II.ap, XC.ap, ALU.mult, II.r() + XC.r(), II.r())
                tt(DVE, II.ap, II.ap, A2.ap, ALU.mult, II.r() + A2.r(), II.r())
                scan(HS.ap, AA.ap, II.ap, HST.ap[:, j:j + 1], AA.r() + II.r() + HST.r(), HS.r())
                cp(POOL, HST.ap[:, j:j + 1], HS.ap[:, 511:512], HS.r(), HST.r())
                tt(POOL, Y.ap[:, j, :], HS.ap, GATE.ap[:, ci, :], ALU.mult, HS.r() + GATE.r(ci), Y.r(j))
            proj_fm(W, hf * 512, 4, H, TT, c_xa)
        dump("ya", Y.ap[:, 0:8, :], Y.r(0, 8))
        if stop_after == "rglru":
            return

        for hf in range(2):
            def c_g(ci, pb):
                act(GATE2.ap[:, ci, :], PS[pb], AF.Silu, PR[pb], GATE2.r(ci))
            proj_fm(W, 5120 + hf * 512, 4, H, TT, c_g)

            def c_q(ci, pb):
                act(QS.ap[:, ci, :], PS[pb], AF.Silu, PR[pb], QS.r(ci))
            proj_fm(W, 2048 + hf * 512, 4, H, TT, c_q)
            ws = wload(W, 0, 8, 4096 + hf * 512, 512)
            for s in range(4):
                pb = bank()
                for k in range(8):
                    mm(PS[pb], H.ap[:, k, s * 128:(s + 1) * 128], ws.ap[:, k, :], k == 0, k == 7, ws.regs + H.r(k), PR[pb])
                cp(ACT if s % 2 else DVE, VT.ap[:, s, :], PS[pb], PR[pb], VT.r(s))
            if stop_after == "hgA":
                return

            def c_f(ci, pb, hf=hf):
                j = hf * 4 + ci
                FF, LF, BB, E1, E2, _u = FA[j % 2]
                KD = KDs[j % 2]
                act(FF.ap, PS[pb], AF.Sigmoid, PR[pb], FF.r())
                ts(DVE, FF.ap, FF.ap, OML.ap[:, j:j + 1], LB.ap[:, j:j + 1], ALU.mult, ALU.add, FF.r() + OML.r() + LB.r(), FF.r())
                act(LF.ap, FF.ap, AF.Ln, FF.r(), LF.r())
                scan(BB.ap, R64, LF.ap, 0.0, LF.r() + CST.r(), BB.r())
                act(E1.ap, BB.ap, AF.Exp, BB.r(), E1.r())
                act(E2.ap, BB.ap, AF.Exp, BB.r(), E2.r(), scale=-1.0)
                cp(POOL, EL.ap[:, ci, :], E1.ap.rearrange("p (c u) -> p c u", u=64)[:, :, 63], E1.r(), EL.r(ci))
                tt(DVE, QT.ap[:, ci, :], QS.ap[:, ci, :], E1.ap, ALU.mult, QS.r(ci) + E1.r(), QT.r(ci))
                ts(DVE, FF.ap, FF.ap, -1.0, 1.0, ALU.mult, ALU.add, FF.r(), FF.r())
                tt(DVE, KT.ap[:, ci, :], FF.ap, E2.ap, ALU.mult, FF.r() + E2.r(), KT.r(ci))
                tt(POOL, KD.ap.rearrange("p (c u) -> p c u", u=64), KT.ap[:, ci, :].rearrange("p (c u) -> p c u", u=64),
                   EL.ap[:, ci, :].unsqueeze(2).to_broadcast([128, 8, 64]), ALU.mult, KT.r(ci) + EL.r(ci), KD.r())
                yield
                pbt = bank()
                for s in range(4):
                    tr(PSB[pbt][:, s * 128:(s + 1) * 128], KD.ap[:, s * 128:(s + 1) * 128], IDB.ap, KD.r() + IDB.r(), PR[pbt])
                cp(ACT, KDT.ap[:, ci, :], PSB[pbt][:, 0:512], PR[pbt], KDT.r(ci))
            proj_fm(W, 3072 + hf * 512, 4, H, TT, c_f)
            if stop_after == "hgB":
                return

            def head_gen(jj, hf=hf):
                j = hf * 4 + jj
                OF = FA[j % 2][5]
                PT, OSQ, RSh = PTs[j % 2], OSQs[j % 2], RS2[j % 2]
                pbs = bank()
                for s in range(4):
                    sl_ = slice(s * 128, (s + 1) * 128)
                    mm(PS[pbs][:, sl_], KT.ap[:, jj, sl_], QT.ap[:, jj, sl_], s == 0, s == 3, KT.r(jj) + QT.r(jj), PR[pbs])
                tt(DVE, PT.ap, PS[pbs].rearrange("p (s t) -> p s t", s=4), MASKH.unsqueeze(1).to_broadcast([128, 4, 128]),
                   ALU.mult, PR[pbs] + CST.r(), PT.r())
                pd = [bank(), bank()]
                for c in range(8):
                    s, hh = c // 2, c % 2
                    mm(PS[pd[hh]][:, s * 128:(s + 1) * 128],
                       KDT.ap[hh * 64:(hh + 1) * 64, jj, s * 128:(s + 1) * 128],
                       VT.ap[hh * 64:(hh + 1) * 64, s, jj * 128:(jj + 1) * 128], s == 0, s == 3,
                       KDT.r(jj) + VT.r(s), PR[pd[hh]])
                SBj = SB[j % 2]
                cp(POOL, SBj.ap[:, 0, :], ST0.ap[:, j, :], ST0.r(j), SBj.r(0))
                yield
                po = bank()
                for s in range(4):
                    mm(PS[po][:, s * 128:(s + 1) * 128], VT.ap[:, s, jj * 128:(jj + 1) * 128], PT.ap[:, s, :], s == 0, False,
                       VT.r(s) + PT.r(), PR[po])
                for c in range(8):
                    mm(PS[po][:, c * 64:(c + 1) * 64], SBj.ap[:, c, :], QT.ap[:, jj, c * 64:(c + 1) * 64], False, c == 7,
                       SBj.r(c) + QT.r(jj), PR[po])
                    stt(ST0.ap[:, j, :], ST0.ap[:, j, :], EL.ap[:, jj, c:c + 1], PS[pd[c % 2]][:, (c // 2) * 128:(c // 2 + 1) * 128],
                        ALU.mult, ALU.add, ST0.r(j) + EL.r(jj) + PR[pd[c % 2]], ST0.r(j))
                    if c < 7:
                        cp(POOL, SBj.ap[:, c + 1, :], ST0.ap[:, j, :], ST0.r(j), SBj.r(c + 1))
                    yield
                act(OF.ap, PS[po], AF.Copy, PR[po], OF.r())
                act(OSQ.ap, OF.ap, AF.Square, OF.r(), OSQ.r())
                pn = bank()
                mm(PS[pn], ONES.ap, OSQ.ap, True, True, ONES.r() + OSQ.r(), PR[pn])
                yield
                act(RSh.ap, PS[pn], AF.Sqrt, PR[pn] + EPSC.r(), RSh.r(), bias=EPSC.ap[:, 0:1], scale=1.0 / 128)
                recip(RSh.ap, RSh.ap, RSh.r(), RSh.r())
                stt(OF.ap, OF.ap, pc("bng", j), RSh.ap, ALU.mult, ALU.mult, OF.r() + RSh.r() + PC.r(), OF.r())
                tt(POOL, Y.ap[:, 8 + j, :], OF.ap, GATE2.ap[:, jj, :], ALU.mult, OF.r() + GATE2.r(jj), Y.r(8 + j))
            interleave([head_gen(0), head_gen(1)])
            interleave([head_gen(2), head_gen(3)])
        dump("yb", Y.ap[:, 8:16, :], Y.r(8, 16))
        outproj_add("e_w_out", Y, 16)

    def xattn(l):
        kb.off = pers_off
        QA = alloc([8, TT], BF16)
        EX = [alloc([2, TT], BF16) for _ in range(2)]
        RC = [alloc([TT]) for _ in range(2)]
        SQ = alloc([8, TT], BF16)
        norm_fm(X, TT, "gmq", l * 8, H, SQ)

        def c_q(ci, pb):
            act(QA.ap[:, ci, :], PS[pb], AF.Copy, PR[pb], QA.r(ci), scale=1.0 / 16.0)
        proj_fm(f"xq{l}", 0, 8, H, TT, c_q)
        for hd in range(4):
            EXh, RCh = EX[hd % 2], RC[hd % 2]
            for mh in range(2):
                pb = bank()
                for dc in range(2):
                    mm(PS[pb], KF[l].ap[:, 2 * hd + dc, mh * 128:(mh + 1) * 128], QA.ap[:, 2 * hd + dc, :], dc == 0, dc == 1,
                       KF[l].r(2 * hd + dc) + QA.r(2 * hd + dc), PR[pb])
                act(EXh.ap[:, mh, :], PS[pb], AF.Exp, PR[pb], EXh.r(mh))
            pb = bank()
            for mh in range(2):
                mm(PS[pb], ONES.ap, EXh.ap[:, mh, :], mh == 0, mh == 1, ONES.r() + EXh.r(mh), PR[pb])
            recip(RCh.ap, PS[pb], PR[pb], RCh.r())
            for dc in range(2):
                pb = bank()
                c = 2 * hd + dc
                for mh in range(2):
                    mm(PS[pb], VM[l].ap[:, mh, c * 128:(c + 1) * 128], EXh.ap[:, mh, :], mh == 0, mh == 1,
                       VM[l].r(mh) + EXh.r(mh), PR[pb])
                tt(DVE, Y.ap[:, c, :], PS[pb], RCh.ap, ALU.mult, PR[pb] + RCh.r(), Y.r(c))
        outproj_add(f"xo{l}", Y, 8)

    def ffn(l):
        kb.off = pers_off
        UP = alloc([32, TT], BF16)
        RL = [alloc([TT]) for _ in range(2)]
        SQ = alloc([8, TT], BF16)
        norm_fm(X, TT, "gffn", l * 8, H, SQ)

        def c_up(ci, pb):
            R_ = RL[ci % 2]
            act(R_.ap, PS[pb], AF.Relu, PR[pb], R_.r())
            tt(DVE, UP.ap[:, ci, :], R_.ap, PS[pb], ALU.mult, R_.r() + PR[pb], UP.r(ci))
        proj_fm(f"w1_{l}", 0, 32, H, TT, c_up)
        for c in range(8):
            ws = wload(f"w2_{l}", 0, 32, c * 128, 128)
            pb = bank()
            for k in range(32):
                mm(PS[pb], ws.ap[:, k, :], UP.ap[:, k, :], k == 0, k == 31, ws.regs + UP.r(k), PR[pb])
            tt(DVE, X.ap[:, c, :], X.ap[:, c, :], PS[pb], ALU.add, X.r(c) + PR[pb], X.r(c))

    def l1_mixer(t):
        kb.off = pers_off
        DT = alloc([TT])
        ACU = alloc([TT])
        DTT = alloc([4, 32])
        NAT = alloc([4, 32])
        m1 = kb.off
        SQ = alloc([8, TT], BF16)
        norm_fm(X, TT, "gmix", 8, H, SQ)
        kb.off = m1
        DA = alloc([TT])
        XTM = alloc([4, 1024])
        ZS = alloc([4, 1024], BF16)
        BF_ = alloc([4, TT], BF16)
        CF_ = alloc([4, TT], BF16)
        BT = alloc([4, 512], BF16)
        m2 = kb.off
        XA = [alloc([516]) for _ in range(3)]
        XC = [alloc([TT]) for _ in range(3)]
        kb.off = m2
        WS_ = alloc([16])
        ALB = alloc([16])
        EAL = alloc([16])
        SS = alloc([4])
        LL = [alloc([4, 128]) for _ in range(2)]
        LL2 = [alloc([4, 128]) for _ in range(2)]
        CBM = [alloc([128]) for _ in range(2)]
        MT = [alloc([4, 128], BF16) for _ in range(2)]
        CT = [alloc([4, 128], BF16) for _ in range(2)]
        XDT = alloc([1024], BF16)
        XDW = alloc([1024], BF16)
        STB = alloc([1024], BF16)
        YA = alloc([1024])
        YB = alloc([1024])
        YN = alloc([1024], BF16)
        W = "o_w_in"
        ws = wload(W, 0, 8, 6144, 32)
        pb = bank()
        for k in range(8):
            mm(PS[pb][0:32, :], ws.ap[:, k, :], H.ap[:, k, :], k == 0, k == 7, ws.regs + H.r(k), PR[pb])
        act(DT.ap[0:32, :], PS[pb][0:32, :], AF.Exp, PR[pb] + PC.r(), DT.r(), bias=PC.ap[0:32, PCI["dtb"]:PCI["dtb"] + 1])
        act(DT.ap[0:32, :], DT.ap[0:32, :], AF.Ln, DT.r(), DT.r(), bias=1.0)
        ts(DVE, DA.ap[0:32, :], DT.ap[0:32, :], ANEG.ap[0:32, 0:1], None, ALU.mult, None, DT.r() + ANEG.r(), DA.r())
        scan(ACU.ap[0:32, :], R128[0:32, :], DA.ap[0:32, :], 0.0, DA.r() + CST.r(), ACU.r())
        pb = bank()
        for s in range(4):
            tr(PS[pb][:, s * 32:(s + 1) * 32], DT.ap[0:32, s * 128:(s + 1) * 128], IDF[0:32, 0:32], DT.r() + CST.r(), PR[pb])
        cp(DVE, DTT.ap, PS[pb][:, 0:128].rearrange("p (s h) -> p s h", s=4), PR[pb], DTT.r())
        pb = bank()
        for s in range(4):
            tr(PS[pb][:, s * 32:(s + 1) * 32], ACU.ap[0:32, s * 128:(s + 1) * 128], IDF[0:32, 0:32], ACU.r() + CST.r(), PR[pb])
        ts(DVE, NAT.ap, PS[pb][:, 0:128].rearrange("p (s h) -> p s h", s=4), -1.0, None, ALU.mult, None, PR[pb], NAT.r())

        YBANKS = (0, 1)
        OTH = (2, 3, 4, 5, 6, 7)
        for hf in range(2):
            for sl in range(2):
                ws = wload(W, 0, 8, hf * 1024 + sl * 512, 512)
                for s in range(4):
                    pb = bank()
                    for k in range(8):
                        mm(PS[pb], H.ap[:, k, s * 128:(s + 1) * 128], ws.ap[:, k, :], k == 0, k == 7, ws.regs + H.r(k), PR[pb])
                    act(ZS.ap[:, s, sl * 512:(sl + 1) * 512], PS[pb], AF.Silu, PR[pb],
                        ZS.rb(s * 1024 + sl * 512, s * 1024 + sl * 512 + 512))

            def c_x(ci, pb, hf=hf):
                ch = hf * 8 + ci
                XAb, XCb = XA[ci % 3], XC[ci % 3]
                conv_chunk(pb, XAb, CAR1, ch, "mcw", 32, "mcb", XCb.ap, XCb.r())
                yield
                act(XCb.ap, XCb.ap, AF.Silu, XCb.r(), XCb.r())
                yield
                pbt = bank()
                for s in range(4):
                    tr(PS[pbt][:, s * 128:(s + 1) * 128], XCb.ap[:, s * 128:(s + 1) * 128], IDF, XCb.r() + CST.r(), PR[pbt])
                cp(DVE, XTM.ap[:, :, ci * 128:(ci + 1) * 128], PS[pbt].rearrange("p (s f) -> p s f", s=4), PR[pbt], XTM.r())
            proj_fm(W, 2048 + hf * 1024, 8, H, TT, c_x)

            def c_b(gi, pb, hf=hf):
                ch = 16 + hf * 4 + gi
                XAb, XCb = XA[gi % 3], XC[gi % 3]
                conv_chunk(pb, XAb, CAR1, ch, "mcw", 32, "mcb", XCb.ap, XCb.r())
                yield
                act(BF_.ap[:, gi, :], XCb.ap, AF.Silu, XCb.r(), BF_.r(gi))
                yield
                pbt = bank()
                for s in range(4):
                    tr(PSB[pbt][:, s * 128:(s + 1) * 128], BF_.ap[:, gi, s * 128:(s + 1) * 128], IDB.ap, BF_.r(gi) + IDB.r(), PR[pbt])
                cp(DVE, BT.ap[:, :, gi * 128:(gi + 1) * 128], PSB[pbt][:, 0:512].rearrange("p (s f) -> p s f", s=4), PR[pbt], BT.r())
            proj_fm(W, 4096 + hf * 512, 4, H, TT, c_b)

            def c_c(gi, pb, hf=hf):
                ch = 24 + hf * 4 + gi
                XAb, XCb = XA[gi % 3], XC[gi % 3]
                conv_chunk(pb, XAb, CAR1, ch, "mcw", 32, "mcb", XCb.ap, XCb.r())
                yield
                act(CF_.ap[:, gi, :], XCb.ap, AF.Silu, XCb.r(), CF_.r(gi))
            proj_fm(W, 5120 + hf * 512, 4, H, TT, c_c)

            H0 = hf * 16
            prevB = None
            for s in range(4):
                cs = slice(s * 128, (s + 1) * 128)
                tt(DVE, XDT.ap.rearrange("p (h q) -> p h q", q=64), XTM.ap[:, s, :].rearrange("p (h q) -> p h q", q=64),
                   DTT.ap[:, s, H0:H0 + 16].unsqueeze(2).to_broadcast([128, 16, 64]), ALU.mult, XTM.r(s) + DTT.r(), XDT.r())
                cp(ACT, STB.ap, ST1.ap[:, hf * 1024:(hf + 1) * 1024], ST1.rb(hf * 1024, hf * 1024 + 1024), STB.r())
                def grp_gen(gl, s=s, cs=cs, H0=H0):
                    i2 = gl % 2
                    pcb = bank(OTH)
                    mm(PS[pcb][:, 0:128], BF_.ap[:, gl, cs], CF_.ap[:, gl, cs], True, True, BF_.r(gl) + CF_.r(gl), PR[pcb])
                    tt(DVE, CBM[i2].ap, PS[pcb][:, 0:128], MASKC, ALU.mult, PR[pcb] + CST.r(), CBM[i2].r())
                    pa = bank(OTH)
                    for hh in range(4):
                        h = H0 + 4 * gl + hh
                        mm(PS[pa][:, hh * 128:(hh + 1) * 128], IDF[0:32, h:h + 1].to_broadcast([32, 128]), ACU.ap[0:32, cs],
                           hh == 0, hh == 3, CST.r() + ACU.r(), PR[pa])
                    for hh in range(4):
                        h = H0 + 4 * gl + hh
                        act(LL[i2].ap[:, hh, :], PS[pa][:, hh * 128:(hh + 1) * 128], AF.Exp, PR[pa] + NAT.r(), LL[i2].r(hh),
                            bias=NAT.ap[:, s, h:h + 1])
                    stt(MT[i2].ap, LL[i2].ap, 1.0, CBM[i2].ap.unsqueeze(1).to_broadcast([128, 4, 128]), ALU.min, ALU.mult,
                        LL[i2].r() + CBM[i2].r(), MT[i2].r())
                    pav = PS[pa].rearrange("p (h t) -> p h t", h=4)
                    cp(DVE, ALB.ap[:, 4 * gl:4 * gl + 4], pav[:, :, 127], PR[pa], ALB.r())
                    act(LL2[i2].ap, pav, AF.Exp, PR[pa], LL2[i2].r())
                    tt(DVE, CT[i2].ap, LL2[i2].ap, CF_.ap[:, gl, cs].unsqueeze(1).to_broadcast([128, 4, 128]), ALU.mult,
                       LL2[i2].r() + CF_.r(gl), CT[i2].r())
                    yield
                    for hh in range(4):
                        hl = 4 * gl + hh
                        yb = YBANKS[hl // 8]
                        oc = slice((hl % 8) * 64, (hl % 8 + 1) * 64)
                        mm(PS[yb][:, oc], MT[i2].ap[:, hh, :], XDT.ap[:, hl * 64:(hl + 1) * 64], (hl % 8 == 0), False,
                           MT[i2].r(hh) + XDT.r(), PR[yb])
                        mm(PS[yb][:, oc], CT[i2].ap[:, hh, :], STB.ap[:, hl * 64:(hl + 1) * 64], False, (hl % 8 == 7),
                           CT[i2].r(hh) + STB.r(), PR[yb])
                pend_ = [prevB] if prevB is not None else []
                for gl in range(4):
                    g_ = grp_gen(gl)
                    next(g_)
                    step_all(pend_)
                    pend_.append(g_)
                drain(pend_)
                STh = ST1.ap[:, hf * 1024:(hf + 1) * 1024]
                STr = ST1.rb(hf * 1024, hf * 1024 + 1024)
                act(EAL.ap, ALB.ap, AF.Exp, ALB.r(), EAL.r())
                tt(DVE, WS_.ap, ALB.ap, NAT.ap[:, s, H0:H0 + 16], ALU.add, ALB.r() + NAT.r(), WS_.r())
                act(WS_.ap, WS_.ap, AF.Exp, WS_.r(), WS_.r())
                tt(DVE, WS_.ap, WS_.ap, DTT.ap[:, s, H0:H0 + 16], ALU.mult, WS_.r() + DTT.r(), WS_.r())
                tt(DVE, XDW.ap.rearrange("p (h q) -> p h q", q=64), XTM.ap[:, s, :].rearrange("p (h q) -> p h q", q=64),
                   WS_.ap.unsqueeze(2).to_broadcast([128, 16, 64]), ALU.mult, XTM.r(s) + WS_.r(), XDW.r())
                tt(POOL, STh.rearrange("p (h q) -> p h q", q=64), STh.rearrange("p (h q) -> p h q", q=64),
                   EAL.ap.unsqueeze(2).to_broadcast([128, 16, 64]), ALU.mult, STr + EAL.r(), STr)
                for half in range(2):
                    pd = bank(OTH)
                    for gg in range(2):
                        gl = half * 2 + gg
                        mm(PS[pd][:, gg * 256:(gg + 1) * 256], BT.ap[:, s, gl * 128:(gl + 1) * 128], XDW.ap[:, gl * 256:(gl + 1) * 256],
                           gg == 0, gg == 1, BT.r(s) + XDW.r(), PR[pd])
                    sl_ = slice(half * 512, (half + 1) * 512)
                    tt(DVE, STh[:, sl_], STh[:, sl_], PS[pd], ALU.add, STr + PR[pd], STr)
                tt(POOL, YA.ap.rearrange("p (h q) -> p h q", q=64), XTM.ap[:, s, :].rearrange("p (h q) -> p h q", q=64),
                   DBC.ap[:, H0:H0 + 16].unsqueeze(2).to_broadcast([128, 16, 64]), ALU.mult, XTM.r(s) + DBC.r(), YA.r())
                for q in range(2):
                    sl_ = slice(q * 512, (q + 1) * 512)
                    tt(DVE, YA.ap[:, sl_], YA.ap[:, sl_], PS[YBANKS[q]], ALU.add, YA.r() + PR[YBANKS[q]], YA.r())

                def fin_gen(s=s, cs=cs, hf=hf):
                    tt(POOL, YA.ap, YA.ap, ZS.ap[:, s, :], ALU.mult, YA.r() + ZS.r(s), YA.r())
                    yield
                    for g4 in range(4):
                        op(ACT, lambda g4=g4: nc.scalar.activation(out=YB.ap[:, g4 * 256:(g4 + 1) * 256],
                                                                    in_=YA.ap[:, g4 * 256:(g4 + 1) * 256], func=AF.Square,
                                                                    accum_out=SS.ap[:, g4:g4 + 1]),
                           YA.r(), YB.r() + SS.r())
                    act(SS.ap, SS.ap, AF.Sqrt, SS.r() + EPSC.r(), SS.r(), bias=EPSC.ap[:, 0:1], scale=1.0 / 256)
                    recip(SS.ap, SS.ap, SS.r(), SS.r())
                    yield
                    for g4 in range(4):
                        act(YA.ap[:, g4 * 256:(g4 + 1) * 256], YA.ap[:, g4 * 256:(g4 + 1) * 256], AF.Copy, YA.r() + SS.r(), YA.r(),
                            scale=SS.ap[:, g4:g4 + 1])
                    tt(POOL, YN.ap, YA.ap, NG.ap[:, hf * 1024:(hf + 1) * 1024], ALU.mult, YA.r() + NG.r(), YN.r())
                    yield
                    for q in range(2):
                        pbt = bank(OTH)
                        for cc in range(4):
                            c = q * 4 + cc
                            tr(PSB[pbt][:, cc * 128:(cc + 1) * 128], YN.ap[:, c * 128:(c + 1) * 128], IDB.ap, YN.r() + IDB.r(), PR[pbt])
                        c0 = hf * 8 + q * 4
                        cp(ACT, Y.ap[:, c0:c0 + 4, cs], PSB[pbt][:, 0:512].rearrange("p (c t) -> p c t", c=4), PR[pbt],
                           Y.r(c0, c0 + 4))
                        yield
                prevB = fin_gen()
            drain([prevB])
        dump("ym", Y.ap, Y.r())
        outproj_add("o_w_out", Y, 16)

    def final(t):
        kb.off = pers_off
        HF = alloc([8, TT])
        IO = alloc([4, 1024])
        SQ = alloc([8, TT], BF16)
        norm_fm(X, TT, "gfin", 0, HF, SQ)
        for s in range(4):
            for hf in range(2):
                pb = bank()
                for cc in range(4):
                    c = hf * 4 + cc
                    tr(PS[pb][:, cc * 128:(cc + 1) * 128], HF.ap[:, c, s * 128:(s + 1) * 128], IDF, HF.r(c) + CST.r(), PR[pb])
                cp(ACT if hf else DVE, IO.ap[:, s, hf * 512:(hf + 1) * 512], PS[pb], PR[pb], IO.r(s))
        dma(POOL, T_OUT, out_d[t * TT:(t + 1) * TT, :].rearrange("(s p) f -> p s f", p=128), IO.ap, IO.r(), [])

    stages = ["l0", "a0", "f0", "l1", "a1", "f1"]
    for t in range(NT):
        load_x(t)
        for st in stages:
            if stop_after == "load":
                break
            if st == "l0":
                l0_mixer(t)
            elif st == "a0":
                if t == 0:
                    mem_kv()
                xattn(0)
            elif st == "f0":
                ffn(0)
            elif st == "l1":
                l1_mixer(t)
            elif st == "a1":
                xattn(1)
            elif st == "f1":
                ffn(1)
            if stop_after == st or (stop_after in ("rglru", "hgA", "hgB") and st == "l0"):
                break
        final(t)
    nc.gpsimd.wait_ge(T_OUT.sem, T_OUT.n * 16)
    for t_ in kb.dbgt:
        nc.gpsimd.wait_ge(t_.sem, t_.n * 16)
    es.close()
    kb.counts = counts
    kb.wneed = wneed
    return nc, kb


def _cols(v):
    v = np.asarray(v, np.float32).reshape(-1)
    return np.ascontiguousarray(v.reshape(v.size // 128, 128).T)


def make_shared_inputs(inp):
    pc = np.zeros((128, NPC), np.float32)

    def put(name, arr):
        a = _cols(arr)
        pc[:, PCI[name]:PCI[name] + a.shape[1]] = a
    put("gmix", inp["norm_mix_g"])
    put("gmq", inp["norm_mem_q_g"])
    put("gmkv", inp["norm_mem_kv_g"])
    put("gffn", inp["norm_ffn_g"])
    put("gfin", inp["final_norm_g"])
    put("acw", inp["a_conv_w"][0])
    put("acb", inp["a_conv_b"][0])
    put("arb", inp["a_gate_r_b"][0])
    put("aib", inp["a_gate_i_b"][0])
    put("alam", inp["a_lambda"][0])
    put("lbl", inp["b_lb_logits"])
    put("bng", inp["b_norm_g"][0])
    put("mcw", inp["m_conv_w"][0])
    put("mcb", inp["m_conv_b"][0])
    pc[:32, PCI["dtb"]] = np.asarray(inp["m_dt_bias"][0], np.float32)
    pc[:32, PCI["alog"]] = np.asarray(inp["m_a_log"][0], np.float32)
    cst = np.zeros((128, NCST), np.float32)
    cst[:, C_ID:C_ID + 128] = np.eye(128, dtype=np.float32)
    s_ = np.arange(128)[:, None]
    t_ = np.arange(128)[None, :]
    cst[:, C_MH:C_MH + 128] = ((s_ <= t_) & ((s_ // 64) == (t_ // 64))).astype(np.float32)
    cst[:, C_MC:C_MC + 128] = (s_ <= t_).astype(np.float32)
    cst[:, C_R64:C_R64 + 512] = (np.arange(512) % 64 != 0).astype(np.float32)[None, :]
    cst[:, C_R128:C_R128 + 512] = (np.arange(512) % 128 != 0).astype(np.float32)[None, :]
    sh = {"pc": pc, "cst": cst}
    sh["gr"] = np.ascontiguousarray(np.asarray(inp["a_gate_r_w"][0], np.float32).transpose(1, 0, 2).reshape(128, 1024))
    sh["gi"] = np.ascontiguousarray(np.asarray(inp["a_gate_i_w"][0], np.float32).transpose(1, 0, 2).reshape(128, 1024))
    sh["mng"] = np.asarray(inp["m_norm_g"], np.float32).reshape(1, 2048)
    sh["md"] = np.repeat(np.asarray(inp["m_d"], np.float32).reshape(1, 32), 1, axis=0)
    sh["e_w_in"] = np.asarray(inp["e_w_in"][0], np.float32)
    sh["e_w_out"] = np.asarray(inp["e_w_out"][0], np.float32)
    sh["o_w_in"] = np.asarray(inp["o_w_in"][0], np.float32)
    sh["o_w_out"] = np.asarray(inp["o_w_out"][0], np.float32)
    for l in range(2):
        sh[f"xq{l}"] = np.asarray(inp["xq_w"][l], np.float32)
        sh[f"xk{l}"] = np.asarray(inp["xk_w"][l], np.float32)
        sh[f"xv{l}"] = np.asarray(inp["xv_w"][l], np.float32)
        sh[f"xo{l}"] = np.asarray(inp["xo_w"][l], np.float32)
        sh[f"w1_{l}"] = np.asarray(inp["ffn_w1"][l], np.float32)
        sh[f"w2_{l}"] = np.asarray(inp["ffn_w2"][l], np.float32)
    return sh


_CACHE = {}


def kernel(**inputs):
    x = np.asarray(inputs["x"], np.float32)
    mem = np.asarray(inputs["mem"], np.float32)
    B, S, _ = x.shape
    if S not in _CACHE:
        _CACHE[S] = build(S)[0]
    nc = _CACHE[S]
    sh = make_shared_inputs(inputs)
    in_maps = []
    for b in range(B):
        m = dict(sh)
        m["x"] = np.ascontiguousarray(x[b])
        m["mem"] = np.ascontiguousarray(mem[b])
        in_maps.append(m)
    res = run_bass_kernel_spmd(nc, in_maps, core_ids=list(range(B)))
    return np.stack([np.asarray(r["out"], np.float32) for r in res.results], axis=0)
```

```python
import numpy as np
from contextlib import ExitStack
import concourse.bass as bass
import concourse.mybir as mybir
from concourse.bass_utils import run_bass_kernel_spmd

F32 = mybir.dt.float32
BF16 = mybir.dt.bfloat16
AF = mybir.ActivationFunctionType
ALU = mybir.AluOpType
D = 1024
TT = 512
EPS = 1e-6
MEM = 256

PCI = {}
_n = 0
for _nm, _c in [("gmix", 16), ("gmq", 16), ("gmkv", 16), ("gffn", 16), ("gfin", 8), ("acw", 32), ("acb", 8),
                ("arb", 8), ("aib", 8), ("alam", 8), ("lbl", 24), ("bng", 8), ("mcw", 128), ("mcb", 32),
                ("dtb", 1), ("alog", 1)]:
    PCI[_nm] = _n
    _n += _c
NPC = _n
C_ID, C_MH, C_MC, C_R64, C_R128 = 0, 128, 256, 384, 896
NCST = 1408

WSHAPES = {"e_w_in": (1024, 6144), "e_w_out": (2048, 1024), "o_w_in": (1024, 6176), "o_w_out": (2048, 1024),
           "xq0": (1024, 1024), "xk0": (1024, 1024), "xv0": (1024, 1024), "xo0": (1024, 1024),
           "xq1": (1024, 1024), "xk1": (1024, 1024), "xv1": (1024, 1024), "xo1": (1024, 1024),
           "w1_0": (1024, 4096), "w2_0": (4096, 1024), "w1_1": (1024, 4096), "w2_1": (4096, 1024)}
WORDER = ["e_w_in", "e_w_out", "xk0", "xv0", "xk1", "xv1", "xq0", "xo0", "w1_0", "w2_0",
          "o_w_in", "o_w_out", "xq1", "xo1", "w1_1", "w2_1"]


class Reg:
    __slots__ = ("w", "r")

    def __init__(self):
        self.w = None
        self.r = {}


class Trk:
    def __init__(self, sem, inc):
        self.sem = sem
        self.inc = inc
        self.n = 0


class Eng:
    def __init__(self, eng, trk, is_pe=False):
        self.eng = eng
        self.trk = trk
        self.seen = {}
        self.is_pe = is_pe


class View:
    def __init__(self, kb, b0, dtype, shape):
        self.kb = kb
        self.b0 = b0
        self.es = 4 if dtype == F32 else 2
        self.shape = tuple(shape)
        n = int(np.prod(shape))
        self.nbytes = n * self.es
        base = kb.arena[:, b0 // 4:(b0 + self.nbytes) // 4]
        ap = base if dtype == F32 else base.bitcast(BF16)
        if len(shape) == 2:
            ap = ap.rearrange("p (a b) -> p a b", a=shape[0])
        elif len(shape) == 3:
            ap = ap.rearrange("p (a b c) -> p a b c", a=shape[0], b=shape[1])
        self.ap = ap
        self.slot = self.nbytes // shape[0] if len(shape) > 1 else self.nbytes

    def r(self, i=None, j=None):
        G = self.kb.G
        if i is None:
            lo, hi = self.b0, self.b0 + self.nbytes
        else:
            if j is None:
                j = i + 1
            lo, hi = self.b0 + i * self.slot, self.b0 + j * self.slot
        return self.kb.regs[lo // G:(hi + G - 1) // G]

    def rb(self, lo_el, hi_el):
        G = self.kb.G
        lo, hi = self.b0 + lo_el * self.es, self.b0 + hi_el * self.es
        return self.kb.regs[lo // G:(hi + G - 1) // G]


def build(S, stop_after=None, dbg=None):
    NT = S // TT
    nc = bass.Bass("TRN2", target_bir_lowering=False)
    kb = type("KB", (), {})()
    es = ExitStack()
    E_ = es.enter_context

    def din(name, shape, dt=F32):
        return nc.dram_tensor(name, list(shape), dt, kind="ExternalInput").ap()

    x_d = din("x", [S, D])
    mem_d = din("mem", [MEM, D])
    pc_d = din("pc", [128, NPC])
    cst_d = din("cst", [128, NCST])
    gr_d = din("gr", [128, 1024])
    gi_d = din("gi", [128, 1024])
    mng_d = din("mng", [1, 2048])
    md_d = din("md", [1, 32])
    _stage_w = {"load": [], "rglru": ["e_w_in"], "hgA": [], "hgB": [], "l0": ["e_w_in", "e_w_out"], "a0": ["xq0", "xo0"], "f0": ["w1_0", "w2_0"],
                "l1": ["o_w_in", "o_w_out"], "a1": ["xq1", "xo1"], "f1": ["w1_1", "w2_1"]}
    wneed = ["xk0", "xv0", "xk1", "xv1"]
    for _st in ["load", "rglru", "hgA", "hgB", "l0", "a0", "f0", "l1", "a1", "f1"]:
        wneed += _stage_w[_st]
        if stop_after == _st:
            break
    wneed = [k for k in WORDER if k in set(wneed)]
    wf = {k: din(k, WSHAPES[k]) for k in wneed}
    wb = {k: nc.dram_tensor(k + "_b", list(WSHAPES[k]), BF16, kind="Internal").ap() for k in wneed}
    out_d = nc.dram_tensor("out", [S, D], F32, kind="ExternalOutput").ap()
    dbg_d = {}
    if dbg:
        for nm, shp in dbg.items():
            dbg_d[nm] = nc.dram_tensor("dbg_" + nm, list(shp), F32, kind="ExternalOutput").ap()

    ARENA_BYTES = 206 * 1024
    kb.G = 512
    kb.arena = E_(nc.sbuf_tensor("arena", [128, ARENA_BYTES // 4], F32))[:]
    kb.regs = [Reg() for _ in range(ARENA_BYTES // kb.G)]
    kb.off = 0
    kb.dbgt = []

    def alloc(shape, dtype=F32):
        v = View(kb, kb.off, dtype, shape)
        kb.off += (v.nbytes + kb.G - 1) // kb.G * kb.G
        assert kb.off <= ARENA_BYTES, f"SBUF arena overflow {kb.off}"
        return v

    def mksem(name):
        return E_(nc.semaphore(name))

    PE = Eng(nc.tensor, Trk(mksem("s_pe"), 1), is_pe=True)
    ACT = Eng(nc.scalar, Trk(mksem("s_act"), 1))
    DVE = Eng(nc.vector, Trk(mksem("s_dve"), 1))
    POOL = Eng(nc.gpsimd, Trk(mksem("s_pool"), 1))
    SP = Eng(nc.sync, None)
    kb.nsem = 0

    def newtrk():
        kb.nsem += 1
        return Trk(mksem(f"s_dma{kb.nsem}"), 16)
    NCAST = 6
    T_CAST = [newtrk() for _ in range(NCAST)]
    T_XIN = newtrk()
    T_OUT = newtrk()
    counts = {"wait": 0, "ins": 0}

    def op(E, fn, r=(), w=(), trk=None):
        trk = trk or E.trk
        need = {}
        for g in r:
            if g.w is not None:
                t, c = g.w
                if need.get(t, 0) < c:
                    need[t] = c
        for g in w:
            if g.w is not None:
                t, c = g.w
                if need.get(t, 0) < c:
                    need[t] = c
            for t, c in g.r.items():
                if need.get(t, 0) < c:
                    need[t] = c
        for t, c in need.items():
            if E.is_pe and t is E.trk:
                continue
            if E.seen.get(t, 0) < c:
                E.eng.wait_ge(t.sem, c * t.inc)
                E.seen[t] = c
                counts["wait"] += 1
        ins = fn()
        trk.n += 1
        ins.then_inc(trk.sem, trk.inc)
        counts["ins"] += 1
        n = trk.n
        for g in r:
            if g.r.get(trk, 0) < n:
                g.r[trk] = n
        for g in w:
            g.w = (trk, n)
            g.r = {}

    PSt = [E_(nc.psum_tensor(f"ps{i}", [128, 512], F32)) for i in range(8)]
    PS = [t[:] for t in PSt]
    PSB = [t[:].bitcast(BF16) for t in PSt]
    PR = [[Reg()] for _ in range(8)]
    kb.pb = 0

    def bank(pool=(0, 1, 2, 3, 4, 5, 6, 7)):
        kb.pb = (kb.pb + 1) % len(pool)
        return pool[kb.pb]

    def mm(out, lhsT, rhs, start, stop, r, w):
        op(PE, lambda: nc.tensor.matmul(out, lhsT=lhsT, rhs=rhs, start=start, stop=stop), r, w)

    def tr(out, in_, ident, r, w):
        op(PE, lambda: nc.tensor.transpose(out=out, in_=in_, identity=ident), r, w)

    def act(out, in_, func, r, w, bias=None, scale=None):
        kw = {}
        if bias is not None:
            kw["bias"] = bias
        if scale is not None:
            kw["scale"] = scale
        op(ACT, lambda: nc.scalar.activation(out=out, in_=in_, func=func, **kw), r, w)

    def ts(E, out, in0, s1, s2, op0, op1, r, w):
        if s2 is None:
            op(E, lambda: E.eng.tensor_scalar(out=out, in0=in0, scalar1=s1, scalar2=None, op0=op0), r, w)
        else:
            op(E, lambda: E.eng.tensor_scalar(out=out, in0=in0, scalar1=s1, scalar2=s2, op0=op0, op1=op1), r, w)

    def tt(E, out, in0, in1, o, r, w):
        op(E, lambda: E.eng.tensor_tensor(out=out, in0=in0, in1=in1, op=o), r, w)

    def stt(out, in0, scalar, in1, op0, op1, r, w):
        op(DVE, lambda: nc.vector.scalar_tensor_tensor(out=out, in0=in0, scalar=scalar, in1=in1, op0=op0, op1=op1), r, w)

    def scan(out, d0, d1, initial, r, w):
        op(DVE, lambda: nc.vector.tensor_tensor_scan(out=out, data0=d0, data1=d1, initial=initial,
                                                     op0=ALU.mult, op1=ALU.add), r, w)

    def cp(E, out, in_, r, w):
        if E is ACT:
            op(ACT, lambda: nc.scalar.activation(out=out, in_=in_, func=AF.Copy), r, w)
        else:
            op(E, lambda: E.eng.tensor_copy(out=out, in_=in_), r, w)

    def recip(out, in_, r, w):
        op(DVE, lambda: nc.vector.reciprocal(out=out, in_=in_), r, w)

    def dma(E, trk, out, in_, r, w):
        op(E, lambda: E.eng.dma_start(out=out, in_=in_), r, w, trk=trk)

    WBR = {k: [Reg() for _ in range(v[0] // 128)] for k, v in WSHAPES.items()}

    PC = alloc([NPC])
    CST = alloc([NCST])
    IDB = alloc([128], BF16)
    ONES = alloc([128], BF16)
    EPSC = alloc([1])
    NG = alloc([2048])
    DBC = alloc([32])
    GR = alloc([8, 128], BF16)
    GI = alloc([8, 128], BF16)
    CL = alloc([8])
    CL2 = alloc([8])
    LB = alloc([8])
    OML = alloc([8])
    ANEG = alloc([1])
    NB = alloc([16])
    KF = [alloc([8, MEM], BF16) for _ in range(2)]
    VM = [alloc([2, 1024], BF16) for _ in range(2)]
    X = alloc([8, TT])
    H = alloc([8, TT], BF16)
    Y = alloc([16, TT], BF16)
    RS = alloc([TT])
    CAR0 = alloc([8, 4])
    HST = alloc([8])
    ST0 = alloc([8, 128])
    CAR1 = alloc([32, 4])
    ST1 = alloc([2048])
    NRING = 4
    RING = [alloc([8 * 512], BF16) for _ in range(NRING)]
    T_RING = [newtrk() for _ in range(NRING)]
    kb.ring = 0
    pers_off = kb.off

    def pc(name, i):
        c = PCI[name] + i
        return PC.ap[:, c:c + 1]

    IDF = CST.ap[:, C_ID:C_ID + 128]
    MASKH = CST.ap[:, C_MH:C_MH + 128]
    MASKC = CST.ap[:, C_MC:C_MC + 128]
    R64 = CST.ap[:, C_R64:C_R64 + 512]
    R128 = CST.ap[:, C_R128:C_R128 + 512]

    class Slab:
        pass

    def wload(name, k0, kc, c0, ncols):
        assert kc * ncols * 2 <= 8192
        v = RING[kb.ring]
        trk_ = T_RING[kb.ring]
        kb.ring = (kb.ring + 1) % NRING
        s = Slab()
        base = v.ap[:, 0:kc * ncols]
        s.ap = base.rearrange("p (k n) -> p k n", k=kc)
        s.regs = v.rb(0, kc * ncols)
        src = wb[name][k0 * 128:(k0 + kc) * 128, c0:c0 + ncols].rearrange("(k p) n -> p k n", p=128)
        dma(SP, trk_, s.ap, src, r=WBR[name][k0:k0 + kc], w=s.regs)
        return s

    dma(SP, newtrk(), PC.ap, pc_d[:, :], [], PC.r())
    dma(SP, newtrk(), CST.ap, cst_d[:, :], [], CST.r())
    dma(SP, newtrk(), NG.ap, mng_d.partition_broadcast(128), [], NG.r())
    dma(SP, newtrk(), DBC.ap, md_d.partition_broadcast(128), [], DBC.r())
    kb.ncast = 0
    for name in wneed:
        K_, N_ = WSHAPES[name]
        for kb_ in range(K_ // 128):
            tc_ = T_CAST[kb.ncast % NCAST]
            kb.ncast += 1
            if tc_.n > 0:
                nc.gpsimd.wait_ge(tc_.sem, tc_.n * 16)
            dma(POOL, tc_, wb[name][kb_ * 128:(kb_ + 1) * 128, :], wf[name][kb_ * 128:(kb_ + 1) * 128, :],
                [], [WBR[name][kb_]])

    m0 = kb.off
    TMPF = alloc([1024])
    dma(SP, newtrk(), TMPF.ap, gr_d[:, :], [], TMPF.r())
    cp(DVE, GR.ap, TMPF.ap.rearrange("p (a b) -> p a b", a=8), TMPF.r(), GR.r())
    TMPG = alloc([1024])
    dma(SP, newtrk(), TMPG.ap, gi_d[:, :], [], TMPG.r())
    cp(DVE, GI.ap, TMPG.ap.rearrange("p (a b) -> p a b", a=8), TMPG.r(), GI.r())
    cp(DVE, IDB.ap, IDF, CST.r(), IDB.r())
    op(DVE, lambda: nc.vector.memset(ONES.ap, 1.0), [], ONES.r())
    op(DVE, lambda: nc.vector.memset(EPSC.ap, EPS), [], EPSC.r())
    for v in (CAR0, HST, ST0, CAR1, ST1):
        op(DVE, lambda v=v: nc.vector.memset(v.ap, 0.0), [], v.r())
    T8 = alloc([24])
    act(T8.ap[:, 0:8], PC.ap[:, PCI["alam"]:PCI["alam"] + 8], AF.Exp, PC.r(), T8.r(), scale=-1.0)
    act(T8.ap[:, 0:8], T8.ap[:, 0:8], AF.Ln, T8.r(), T8.r(), bias=1.0)
    ts(DVE, CL.ap, T8.ap[:, 0:8], -8.0, None, ALU.mult, None, T8.r(), CL.r())
    ts(DVE, CL2.ap, T8.ap[:, 0:8], -16.0, None, ALU.mult, None, T8.r(), CL2.r())
    act(T8.ap, PC.ap[:, PCI["lbl"]:PCI["lbl"] + 24], AF.Exp, PC.r(), T8.r())
    tt(DVE, LB.ap, T8.ap[:, 0:8], T8.ap[:, 8:16], ALU.add, T8.r(), LB.r())
    tt(DVE, LB.ap, LB.ap, T8.ap[:, 16:24], ALU.add, T8.r() + LB.r(), LB.r())
    recip(LB.ap, LB.ap, LB.r(), LB.r())
    tt(DVE, LB.ap, LB.ap, T8.ap[:, 0:8], ALU.mult, LB.r() + T8.r(), LB.r())
    ts(DVE, OML.ap, LB.ap, -1.0, 1.0, ALU.mult, ALU.add, LB.r(), OML.r())
    ts(DVE, NB.ap, PC.ap[:, PCI["arb"]:PCI["arb"] + 16], -1.0, None, ALU.mult, None, PC.r(), NB.r())
    act(ANEG.ap, pc("alog", 0), AF.Exp, PC.r(), ANEG.r())
    ts(DVE, ANEG.ap, ANEG.ap, -1.0, None, ALU.mult, None, ANEG.r(), ANEG.r())

    def norm_fm(Xv, n, gname, goff, out, tmp_sq):
        for c in range(8):
            act(tmp_sq.ap[:, c, :n], Xv.ap[:, c, :n], AF.Square, Xv.r(c), tmp_sq.r(c))
        pb = bank()
        for c in range(8):
            mm(PS[pb][:, :n], ONES.ap, tmp_sq.ap[:, c, :n], c == 0, c == 7, tmp_sq.r(c) + ONES.r(), PR[pb])
        act(RS.ap[:, :n], PS[pb][:, :n], AF.Ln, PR[pb] + EPSC.r(), RS.r(), bias=EPSC.ap[:, 0:1], scale=1.0 / D)
        act(RS.ap[:, :n], RS.ap[:, :n], AF.Exp, RS.r(), RS.r(), scale=-0.5)
        for c in range(8):
            stt(out.ap[:, c, :n], Xv.ap[:, c, :n], pc(gname, goff + c), RS.ap[:, :n], ALU.mult, ALU.mult,
                Xv.r(c) + RS.r() + PC.r(), out.r(c))

    def step_all(pend):
        for g_ in list(pend):
            try:
                next(g_)
            except StopIteration:
                pend.remove(g_)

    def drain(pend):
        while pend:
            step_all(pend)

    def interleave(gens):
        pend = []
        for g_ in gens:
            pend.append(g_)
        drain(pend)

    def proj_fm(name, c0, nchunks, Hv, n, consumer, kc=8, k0=0, pend=None):
        own = pend is None
        if own:
            pend = []
        done = 0
        while done < nchunks:
            g = min(4, nchunks - done)
            ws = wload(name, k0, kc, c0 + done * 128, g * 128)
            for cc in range(g):
                pb = bank()
                for k in range(kc):
                    mm(PS[pb][:, :n], ws.ap[:, k, cc * 128:(cc + 1) * 128], Hv.ap[:, k, :n], k == 0, k == kc - 1,
                       ws.regs + Hv.r(k), PR[pb])
                gen_ = consumer(done + cc, pb)
                started = None
                if gen_ is not None:
                    try:
                        next(gen_)
                        started = gen_
                    except StopIteration:
                        pass
                step_all(pend)
                if started is not None:
                    pend.append(started)
            done += g
        if own:
            drain(pend)

    def outproj_add(name, Yv, kc):
        ncol = 8192 // (kc * 2)
        per = ncol // 128
        for g0 in range(0, 8, per):
            ws = wload(name, 0, kc, g0 * 128, ncol)
            for cc in range(per):
                c = g0 + cc
                pb = bank()
                for k in range(kc):
                    mm(PS[pb], ws.ap[:, k, cc * 128:(cc + 1) * 128], Yv.ap[:, k, :], k == 0, k == kc - 1,
                       ws.regs + Yv.r(k), PR[pb])
                tt(DVE, X.ap[:, c, :], X.ap[:, c, :], PS[pb], ALU.add, X.r(c) + PR[pb], X.r(c))

    def dump(nm, view_ap, regs):
        if nm in dbg_d:
            kb.dbgt.append(newtrk())
            dma(POOL, kb.dbgt[-1], dbg_d[nm], view_ap, regs, [])

    def mem_kv():
        kb.off = pers_off
        MIN = alloc([2, 1024])
        MX = alloc([8, MEM])
        MH = alloc([8, MEM], BF16)
        MSQ = alloc([8, MEM], BF16)
        dma(SP, newtrk(), MIN.ap, mem_d.rearrange("(s p) f -> p s f", p=128), [], MIN.r())
        for c in range(8):
            pb = bank()
            for s in range(2):
                tr(PS[pb][:, s * 128:(s + 1) * 128], MIN.ap[:, s, c * 128:(c + 1) * 128], IDF, MIN.r(s) + CST.r(), PR[pb])
            cp(ACT, MX.ap[:, c, :], PS[pb][:, 0:MEM], PR[pb], MX.r(c))
        for l in range(2):
            norm_fm(MX, MEM, "gmkv", l * 8, MH, MSQ)

            def kcons(ci, pb, l=l):
                cp(ACT, KF[l].ap[:, ci, :], PS[pb][:, :MEM], PR[pb], KF[l].r(ci))
            proj_fm(f"xk{l}", 0, 8, MH, MEM, kcons)
            for sl in range(2):
                ws = wload(f"xv{l}", 0, 8, sl * 512, 512)
                for mh in range(2):
                    pb = bank()
                    for k in range(8):
                        mm(PS[pb], MH.ap[:, k, mh * 128:(mh + 1) * 128], ws.ap[:, k, :], k == 0, k == 7,
                           ws.regs + MH.r(k), PR[pb])
                    cp(DVE, VM[l].ap[:, mh, sl * 512:(sl + 1) * 512], PS[pb], PR[pb], VM[l].rb(mh * 1024 + sl * 512, mh * 1024 + sl * 512 + 512))


    def load_x(t):
        kb.off = pers_off
        IO = alloc([4, 1024])
        dma(SP, T_XIN, IO.ap, x_d[t * TT:(t + 1) * TT, :].rearrange("(s p) f -> p s f", p=128), [], IO.r())
        for c in range(8):
            pb = bank()
            for s in range(4):
                tr(PS[pb][:, s * 128:(s + 1) * 128], IO.ap[:, s, c * 128:(c + 1) * 128], IDF, IO.r(s) + CST.r(), PR[pb])
            cp(ACT if c % 2 else DVE, X.ap[:, c, :], PS[pb], PR[pb], X.r(c))

    def conv_chunk(pb, XAv, CARv, j, wname, wstride, bname, XCv_ap, XC_regs, TMPv=None):
        cp(POOL, XAv.ap[:, 0:3], CARv.ap[:, j, 0:3], CARv.r(j), XAv.r())
        cp(ACT, XAv.ap[:, 3:515], PS[pb], PR[pb], XAv.r())
        cp(POOL, CARv.ap[:, j, 0:3], XAv.ap[:, 512:515], XAv.r(), CARv.r(j))
        act(XCv_ap, PS[pb], AF.Identity, PR[pb] + PC.r(), XC_regs, bias=pc(bname, j), scale=pc(wname, 3 * wstride + j))
        for k in range(3):
            stt(XCv_ap, XAv.ap[:, k:k + 512], pc(wname, k * wstride + j), XCv_ap, ALU.mult, ALU.add,
                XAv.r() + XC_regs + PC.r(), XC_regs)

    def l0_mixer(t):
        kb.off = pers_off
        SQ = alloc([8, TT], BF16)
        norm_fm(X, TT, "gmix", 0, H, SQ)
        kb.off = pers_off
        FA = [[alloc([TT]) for _ in range(6)] for _ in range(2)]
        GATE = alloc([4, TT], BF16)
        GATE2 = alloc([4, TT], BF16)
        QS = alloc([4, TT], BF16)
        QT = alloc([4, TT], BF16)
        KT = alloc([4, TT], BF16)
        KDT = alloc([4, TT], BF16)
        VT = alloc([4, 512], BF16)
        XA = [alloc([516]) for _ in range(2)]
        XCBs = [alloc([TT], BF16) for _ in range(2)]
        KDs = [alloc([TT], BF16) for _ in range(2)]
        PTs = [alloc([4, 128], BF16) for _ in range(2)]
        SB = [alloc([8, 128], BF16) for _ in range(2)]
        EL = alloc([4, 8])
        OSQs = [alloc([TT], BF16) for _ in range(2)]
        RS2 = [alloc([TT]) for _ in range(2)]
        W = "e_w_in"

        for hf in range(2):
            def c_ga(ci, pb):
                act(GATE.ap[:, ci, :], PS[pb], AF.Gelu, PR[pb], GATE.r(ci))
            proj_fm(W, 1024 + hf * 512, 4, H, TT, c_ga)

            def c_xa(ci, pb, hf=hf):
                j = hf * 4 + ci
                XAb = XA[j % 2]
                XC, RR, II, AA, A2, HS = FA[j % 2]
                XCB = XCBs[j % 2]
                conv_chunk(pb, XAb, CAR0, j, "acw", 8, "acb", XC.ap, XC.r())
                cp(POOL, XCB.ap, XC.ap, XC.r(), XCB.r())
                yield
                p1 = bank()
                mm(PS[p1], GR.ap[:, j, :], XCB.ap, True, True, GR.r() + XCB.r(), PR[p1])
                act(RR.ap, PS[p1], AF.Exp, PR[p1] + NB.r(), RR.r(), bias=NB.ap[:, j:j + 1], scale=-1.0)
                act(RR.ap, RR.ap, AF.Ln, RR.r(), RR.r(), bias=1.0)
                act(RR.ap, RR.ap, AF.Exp, RR.r(), RR.r(), scale=-1.0)
                p2 = bank()
                mm(PS[p2], GI.ap[:, j, :], XCB.ap, True, True, GI.r() + XCB.r(), PR[p2])
                act(II.ap, PS[p2], AF.Exp, PR[p2] + NB.r(), II.r(), bias=NB.ap[:, 8 + j:9 + j], scale=-1.0)
                ts(DVE, II.ap, II.ap, 1.0, None, ALU.add, None, II.r(), II.r())
                recip(II.ap, II.ap, II.r(), II.r())
                act(AA.ap, RR.ap, AF.Exp, RR.r() + CL.r(), AA.r(), scale=CL.ap[:, j:j + 1])
                act(A2.ap, RR.ap, AF.Exp, RR.r() + CL2.r(), A2.r(), scale=CL2.ap[:, j:j + 1])
                act(A2.ap, A2.ap, AF.Ln, A2.r(), A2.r(), bias=1.0, scale=-1.0)
                act(A2.ap, A2.ap, AF.Exp, A2.r(), A2.r(), scale=0.5)
                tt(DVE, II.ap, II.ap, XC.ap, ALU.mult, II.r() + XC.r(), II.r())
                tt(DVE, II.ap, II.ap, A2.ap, ALU.mult, II.r() + A2.r(), II.r())
                scan(HS.ap, AA.ap, II.ap, HST.ap[:, j:j + 1], AA.r() + II.r() + HST.r(), HS.r())
                cp(POOL, HST.ap[:, j:j + 1], HS.ap[:, 511:512], HS.r(), HST.r())
                tt(POOL, Y.ap[:, j, :], HS.ap, GATE.ap[:, ci, :], ALU.mult, HS.r() + GATE.r(ci), Y.r(j))
            proj_fm(W, hf * 512, 4, H, TT, c_xa)
        dump("ya", Y.ap[:, 0:8, :], Y.r(0, 8))
        if stop_after == "rglru":
            return

        for hf in range(2):
            def c_g(ci, pb):
                act(GATE2.ap[:, ci, :], PS[pb], AF.Silu, PR[pb], GATE2.r(ci))
            proj_fm(W, 5120 + hf * 512, 4, H, TT, c_g)

            def c_q(ci, pb):
                act(QS.ap[:, ci, :], PS[pb], AF.Silu, PR[pb], QS.r(ci))
            proj_fm(W, 2048 + hf * 512, 4, H, TT, c_q)
            ws = wload(W, 0, 8, 4096 + hf * 512, 512)
            for s in range(4):
                pb = bank()
                for k in range(8):
                    mm(PS[pb], H.ap[:, k, s * 128:(s + 1) * 128], ws.ap[:, k, :], k == 0, k == 7, ws.regs + H.r(k), PR[pb])
                cp(ACT if s % 2 else DVE, VT.ap[:, s, :], PS[pb], PR[pb], VT.r(s))
            if stop_after == "hgA":
                return

            def c_f(ci, pb, hf=hf):
                j = hf * 4 + ci
                FF, LF, BB, E1, E2, _u = FA[j % 2]
                KD = KDs[j % 2]
                act(FF.ap, PS[pb], AF.Exp, PR[pb], FF.r(), scale=-1.0)
                act(FF.ap, FF.ap, AF.Ln, FF.r(), FF.r(), bias=1.0)
                act(FF.ap, FF.ap, AF.Exp, FF.r(), FF.r(), scale=-1.0)
                ts(DVE, FF.ap, FF.ap, OML.ap[:, j:j + 1], LB.ap[:, j:j + 1], ALU.mult, ALU.add, FF.r() + OML.r() + LB.r(), FF.r())
                act(LF.ap, FF.ap, AF.Ln, FF.r(), LF.r())
                scan(BB.ap, R64, LF.ap, 0.0, LF.r() + CST.r(), BB.r())
                act(E1.ap, BB.ap, AF.Exp, BB.r(), E1.r())
                act(E2.ap, BB.ap, AF.Exp, BB.r(), E2.r(), scale=-1.0)
                cp(POOL, EL.ap[:, ci, :], E1.ap.rearrange("p (c u) -> p c u", u=64)[:, :, 63], E1.r(), EL.r(ci))
                tt(DVE, QT.ap[:, ci, :], QS.ap[:, ci, :], E1.ap, ALU.mult, QS.r(ci) + E1.r(), QT.r(ci))
                ts(DVE, FF.ap, FF.ap, -1.0, 1.0, ALU.mult, ALU.add, FF.r(), FF.r())
                tt(DVE, KT.ap[:, ci, :], FF.ap, E2.ap, ALU.mult, FF.r() + E2.r(), KT.r(ci))
                tt(POOL, KD.ap.rearrange("p (c u) -> p c u", u=64), KT.ap[:, ci, :].rearrange("p (c u) -> p c u", u=64),
                   EL.ap[:, ci, :].unsqueeze(2).to_broadcast([128, 8, 64]), ALU.mult, KT.r(ci) + EL.r(ci), KD.r())
                yield
                pbt = bank()
                for s in range(4):
                    tr(PSB[pbt][:, s * 128:(s + 1) * 128], KD.ap[:, s * 128:(s + 1) * 128], IDB.ap, KD.r() + IDB.r(), PR[pbt])
                cp(ACT, KDT.ap[:, ci, :], PSB[pbt][:, 0:512], PR[pbt], KDT.r(ci))
            proj_fm(W, 3072 + hf * 512, 4, H, TT, c_f)
            if stop_after == "hgB":
                return

            def head_gen(jj, hf=hf):
                j = hf * 4 + jj
                OF = FA[j % 2][5]
                PT, OSQ, RSh = PTs[j % 2], OSQs[j % 2], RS2[j % 2]
                pbs = bank()
                for s in range(4):
                    sl_ = slice(s * 128, (s + 1) * 128)
                    mm(PS[pbs][:, sl_], KT.ap[:, jj, sl_], QT.ap[:, jj, sl_], s == 0, s == 3, KT.r(jj) + QT.r(jj), PR[pbs])
                tt(DVE, PT.ap, PS[pbs].rearrange("p (s t) -> p s t", s=4), MASKH.unsqueeze(1).to_broadcast([128, 4, 128]),
                   ALU.mult, PR[pbs] + CST.r(), PT.r())
                pd = [bank(), bank()]
                for c in range(8):
                    s, hh = c // 2, c % 2
                    mm(PS[pd[hh]][:, s * 128:(s + 1) * 128],
                       KDT.ap[hh * 64:(hh + 1) * 64, jj, s * 128:(s + 1) * 128],
                       VT.ap[hh * 64:(hh + 1) * 64, s, jj * 128:(jj + 1) * 128], s == 0, s == 3,
                       KDT.r(jj) + VT.r(s), PR[pd[hh]])
                SBj = SB[j % 2]
                cp(POOL, SBj.ap[:, 0, :], ST0.ap[:, j, :], ST0.r(j), SBj.r(0))
                yield
                po = bank()
                for s in range(4):
                    mm(PS[po][:, s * 128:(s + 1) * 128], VT.ap[:, s, jj * 128:(jj + 1) * 128], PT.ap[:, s, :], s == 0, False,
                       VT.r(s) + PT.r(), PR[po])
                for c in range(8):
                    mm(PS[po][:, c * 64:(c + 1) * 64], SBj.ap[:, c, :], QT.ap[:, jj, c * 64:(c + 1) * 64], False, c == 7,
                       SBj.r(c) + QT.r(jj), PR[po])
                    stt(ST0.ap[:, j, :], ST0.ap[:, j, :], EL.ap[:, jj, c:c + 1], PS[pd[c % 2]][:, (c // 2) * 128:(c // 2 + 1) * 128],
                        ALU.mult, ALU.add, ST0.r(j) + EL.r(jj) + PR[pd[c % 2]], ST0.r(j))
                    if c < 7:
                        cp(POOL, SBj.ap[:, c + 1, :], ST0.ap[:, j, :], ST0.r(j), SBj.r(c + 1))
                    yield
                act(OF.ap, PS[po], AF.Copy, PR[po], OF.r())
                act(OSQ.ap, OF.ap, AF.Square, OF.r(), OSQ.r())
                pn = bank()
                mm(PS[pn], ONES.ap, OSQ.ap, True, True, ONES.r() + OSQ.r(), PR[pn])
                yield
                act(RSh.ap, PS[pn], AF.Ln, PR[pn] + EPSC.r(), RSh.r(), bias=EPSC.ap[:, 0:1], scale=1.0 / 128)
                act(RSh.ap, RSh.ap, AF.Exp, RSh.r(), RSh.r(), scale=-0.5)
                stt(OF.ap, OF.ap, pc("bng", j), RSh.ap, ALU.mult, ALU.mult, OF.r() + RSh.r() + PC.r(), OF.r())
                tt(POOL, Y.ap[:, 8 + j, :], OF.ap, GATE2.ap[:, jj, :], ALU.mult, OF.r() + GATE2.r(jj), Y.r(8 + j))
            interleave([head_gen(0), head_gen(1)])
            interleave([head_gen(2), head_gen(3)])
        dump("yb", Y.ap[:, 8:16, :], Y.r(8, 16))
        outproj_add("e_w_out", Y, 16)

    def xattn(l):
        kb.off = pers_off
        QA = alloc([8, TT], BF16)
        EX = [alloc([2, TT], BF16) for _ in range(2)]
        RC = [alloc([TT]) for _ in range(2)]
        SQ = alloc([8, TT], BF16)
        norm_fm(X, TT, "gmq", l * 8, H, SQ)

        def c_q(ci, pb):
            act(QA.ap[:, ci, :], PS[pb], AF.Copy, PR[pb], QA.r(ci), scale=1.0 / 16.0)
        proj_fm(f"xq{l}", 0, 8, H, TT, c_q)
        for hd in range(4):
            EXh, RCh = EX[hd % 2], RC[hd % 2]
            for mh in range(2):
                pb = bank()
                for dc in range(2):
                    mm(PS[pb], KF[l].ap[:, 2 * hd + dc, mh * 128:(mh + 1) * 128], QA.ap[:, 2 * hd + dc, :], dc == 0, dc == 1,
                       KF[l].r(2 * hd + dc) + QA.r(2 * hd + dc), PR[pb])
                act(EXh.ap[:, mh, :], PS[pb], AF.Exp, PR[pb], EXh.r(mh))
            pb = bank()
            for mh in range(2):
                mm(PS[pb], ONES.ap, EXh.ap[:, mh, :], mh == 0, mh == 1, ONES.r() + EXh.r(mh), PR[pb])
            recip(RCh.ap, PS[pb], PR[pb], RCh.r())
            for dc in range(2):
                pb = bank()
                c = 2 * hd + dc
                for mh in range(2):
                    mm(PS[pb], VM[l].ap[:, mh, c * 128:(c + 1) * 128], EXh.ap[:, mh, :], mh == 0, mh == 1,
                       VM[l].r(mh) + EXh.r(mh), PR[pb])
                tt(DVE, Y.ap[:, c, :], PS[pb], RCh.ap, ALU.mult, PR[pb] + RCh.r(), Y.r(c))
        outproj_add(f"xo{l}", Y, 8)

    def ffn(l):
        kb.off = pers_off
        UP = alloc([32, TT], BF16)
        RL = [alloc([TT]) for _ in range(2)]
        SQ = alloc([8, TT], BF16)
        norm_fm(X, TT, "gffn", l * 8, H, SQ)

        def c_up(ci, pb):
            R_ = RL[ci % 2]
            act(R_.ap, PS[pb], AF.Relu, PR[pb], R_.r())
            tt(DVE, UP.ap[:, ci, :], R_.ap, PS[pb], ALU.mult, R_.r() + PR[pb], UP.r(ci))
        proj_fm(f"w1_{l}", 0, 32, H, TT, c_up)
        for c in range(8):
            ws = wload(f"w2_{l}", 0, 32, c * 128, 128)
            pb = bank()
            for k in range(32):
                mm(PS[pb], ws.ap[:, k, :], UP.ap[:, k, :], k == 0, k == 31, ws.regs + UP.r(k), PR[pb])
            tt(DVE, X.ap[:, c, :], X.ap[:, c, :], PS[pb], ALU.add, X.r(c) + PR[pb], X.r(c))

    def l1_mixer(t):
        kb.off = pers_off
        DT = alloc([TT])
        ACU = alloc([TT])
        DTT = alloc([4, 32])
        NAT = alloc([4, 32])
        m1 = kb.off
        SQ = alloc([8, TT], BF16)
        norm_fm(X, TT, "gmix", 8, H, SQ)
        kb.off = m1
        DA = alloc([TT])
        XTM = alloc([4, 1024])
        ZS = alloc([4, 1024], BF16)
        BF_ = alloc([4, TT], BF16)
        CF_ = alloc([4, TT], BF16)
        BT = alloc([4, 512], BF16)
        m2 = kb.off
        XA = [alloc([516]) for _ in range(3)]
        XC = [alloc([TT]) for _ in range(3)]
        kb.off = m2
        WS_ = alloc([16])
        ALB = alloc([16])
        EAL = alloc([16])
        SS = alloc([4])
        LL = [alloc([4, 128]) for _ in range(2)]
        LL2 = [alloc([4, 128]) for _ in range(2)]
        CBM = [alloc([128]) for _ in range(2)]
        MT = [alloc([4, 128], BF16) for _ in range(2)]
        CT = [alloc([4, 128], BF16) for _ in range(2)]
        XDT = alloc([1024], BF16)
        XDW = alloc([1024], BF16)
        STB = alloc([1024], BF16)
        YA = alloc([1024])
        YB = alloc([1024])
        YN = alloc([1024], BF16)
        W = "o_w_in"
        ws = wload(W, 0, 8, 6144, 32)
        pb = bank()
        for k in range(8):
            mm(PS[pb][0:32, :], ws.ap[:, k, :], H.ap[:, k, :], k == 0, k == 7, ws.regs + H.r(k), PR[pb])
        act(DT.ap[0:32, :], PS[pb][0:32, :], AF.Exp, PR[pb] + PC.r(), DT.r(), bias=PC.ap[0:32, PCI["dtb"]:PCI["dtb"] + 1])
        act(DT.ap[0:32, :], DT.ap[0:32, :], AF.Ln, DT.r(), DT.r(), bias=1.0)
        ts(DVE, DA.ap[0:32, :], DT.ap[0:32, :], ANEG.ap[0:32, 0:1], None, ALU.mult, None, DT.r() + ANEG.r(), DA.r())
        scan(ACU.ap[0:32, :], R128[0:32, :], DA.ap[0:32, :], 0.0, DA.r() + CST.r(), ACU.r())
        pb = bank()
        for s in range(4):
            tr(PS[pb][:, s * 32:(s + 1) * 32], DT.ap[0:32, s * 128:(s + 1) * 128], IDF[0:32, 0:32], DT.r() + CST.r(), PR[pb])
        cp(DVE, DTT.ap, PS[pb][:, 0:128].rearrange("p (s h) -> p s h", s=4), PR[pb], DTT.r())
        pb = bank()
        for s in range(4):
            tr(PS[pb][:, s * 32:(s + 1) * 32], ACU.ap[0:32, s * 128:(s + 1) * 128], IDF[0:32, 0:32], ACU.r() + CST.r(), PR[pb])
        ts(DVE, NAT.ap, PS[pb][:, 0:128].rearrange("p (s h) -> p s h", s=4), -1.0, None, ALU.mult, None, PR[pb], NAT.r())

        YBANKS = (0, 1)
        OTH = (2, 3, 4, 5, 6, 7)
        for hf in range(2):
            for sl in range(2):
                ws = wload(W, 0, 8, hf * 1024 + sl * 512, 512)
                for s in range(4):
                    pb = bank()
                    for k in range(8):
                        mm(PS[pb], H.ap[:, k, s * 128:(s + 1) * 128], ws.ap[:, k, :], k == 0, k == 7, ws.regs + H.r(k), PR[pb])
                    act(ZS.ap[:, s, sl * 512:(sl + 1) * 512], PS[pb], AF.Silu, PR[pb],
                        ZS.rb(s * 1024 + sl * 512, s * 1024 + sl * 512 + 512))

            def c_x(ci, pb, hf=hf):
                ch = hf * 8 + ci
                XAb, XCb = XA[ci % 3], XC[ci % 3]
                conv_chunk(pb, XAb, CAR1, ch, "mcw", 32, "mcb", XCb.ap, XCb.r())
                yield
                act(XCb.ap, XCb.ap, AF.Silu, XCb.r(), XCb.r())
                yield
                pbt = bank()
                for s in range(4):
                    tr(PS[pbt][:, s * 128:(s + 1) * 128], XCb.ap[:, s * 128:(s + 1) * 128], IDF, XCb.r() + CST.r(), PR[pbt])
                cp(DVE, XTM.ap[:, :, ci * 128:(ci + 1) * 128], PS[pbt].rearrange("p (s f) -> p s f", s=4), PR[pbt], XTM.r())
            proj_fm(W, 2048 + hf * 1024, 8, H, TT, c_x)

            def c_b(gi, pb, hf=hf):
                ch = 16 + hf * 4 + gi
                XAb, XCb = XA[gi % 3], XC[gi % 3]
                conv_chunk(pb, XAb, CAR1, ch, "mcw", 32, "mcb", XCb.ap, XCb.r())
                yield
                act(BF_.ap[:, gi, :], XCb.ap, AF.Silu, XCb.r(), BF_.r(gi))
                yield
                pbt = bank()
                for s in range(4):
                    tr(PSB[pbt][:, s * 128:(s + 1) * 128], BF_.ap[:, gi, s * 128:(s + 1) * 128], IDB.ap, BF_.r(gi) + IDB.r(), PR[pbt])
                cp(DVE, BT.ap[:, :, gi * 128:(gi + 1) * 128], PSB[pbt][:, 0:512].rearrange("p (s f) -> p s f", s=4), PR[pbt], BT.r())
            proj_fm(W, 4096 + hf * 512, 4, H, TT, c_b)

            def c_c(gi, pb, hf=hf):
                ch = 24 + hf * 4 + gi
                XAb, XCb = XA[gi % 3], XC[gi % 3]
                conv_chunk(pb, XAb, CAR1, ch, "mcw", 32, "mcb", XCb.ap, XCb.r())
                yield
                act(CF_.ap[:, gi, :], XCb.ap, AF.Silu, XCb.r(), CF_.r(gi))
            proj_fm(W, 5120 + hf * 512, 4, H, TT, c_c)

            H0 = hf * 16
            prevB = None
            for s in range(4):
                cs = slice(s * 128, (s + 1) * 128)
                tt(DVE, XDT.ap.rearrange("p (h q) -> p h q", q=64), XTM.ap[:, s, :].rearrange("p (h q) -> p h q", q=64),
                   DTT.ap[:, s, H0:H0 + 16].unsqueeze(2).to_broadcast([128, 16, 64]), ALU.mult, XTM.r(s) + DTT.r(), XDT.r())
                cp(ACT, STB.ap, ST1.ap[:, hf * 1024:(hf + 1) * 1024], ST1.rb(hf * 1024, hf * 1024 + 1024), STB.r())
                def grp_gen(gl, s=s, cs=cs, H0=H0):
                    i2 = gl % 2
                    pcb = bank(OTH)
                    mm(PS[pcb][:, 0:128], BF_.ap[:, gl, cs], CF_.ap[:, gl, cs], True, True, BF_.r(gl) + CF_.r(gl), PR[pcb])
                    tt(DVE, CBM[i2].ap, PS[pcb][:, 0:128], MASKC, ALU.mult, PR[pcb] + CST.r(), CBM[i2].r())
                    pa = bank(OTH)
                    for hh in range(4):
                        h = H0 + 4 * gl + hh
                        mm(PS[pa][:, hh * 128:(hh + 1) * 128], IDF[0:32, h:h + 1].to_broadcast([32, 128]), ACU.ap[0:32, cs],
                           hh == 0, hh == 3, CST.r() + ACU.r(), PR[pa])
                    for hh in range(4):
                        h = H0 + 4 * gl + hh
                        act(LL[i2].ap[:, hh, :], PS[pa][:, hh * 128:(hh + 1) * 128], AF.Exp, PR[pa] + NAT.r(), LL[i2].r(hh),
                            bias=NAT.ap[:, s, h:h + 1])
                    stt(MT[i2].ap, LL[i2].ap, 1.0, CBM[i2].ap.unsqueeze(1).to_broadcast([128, 4, 128]), ALU.min, ALU.mult,
                        LL[i2].r() + CBM[i2].r(), MT[i2].r())
                    pav = PS[pa].rearrange("p (h t) -> p h t", h=4)
                    cp(DVE, ALB.ap[:, 4 * gl:4 * gl + 4], pav[:, :, 127], PR[pa], ALB.r())
                    act(LL2[i2].ap, pav, AF.Exp, PR[pa], LL2[i2].r())
                    tt(DVE, CT[i2].ap, LL2[i2].ap, CF_.ap[:, gl, cs].unsqueeze(1).to_broadcast([128, 4, 128]), ALU.mult,
                       LL2[i2].r() + CF_.r(gl), CT[i2].r())
                    yield
                    for hh in range(4):
                        hl = 4 * gl + hh
                        yb = YBANKS[hl // 8]
                        oc = slice((hl % 8) * 64, (hl % 8 + 1) * 64)
                        mm(PS[yb][:, oc], MT[i2].ap[:, hh, :], XDT.ap[:, hl * 64:(hl + 1) * 64], (hl % 8 == 0), False,
                           MT[i2].r(hh) + XDT.r(), PR[yb])
                        mm(PS[yb][:, oc], CT[i2].ap[:, hh, :], STB.ap[:, hl * 64:(hl + 1) * 64], False, (hl % 8 == 7),
                           CT[i2].r(hh) + STB.r(), PR[yb])
                pend_ = [prevB] if prevB is not None else []
                for gl in range(4):
                    g_ = grp_gen(gl)
                    next(g_)
                    step_all(pend_)
                    pend_.append(g_)
                drain(pend_)
                STh = ST1.ap[:, hf * 1024:(hf + 1) * 1024]
                STr = ST1.rb(hf * 1024, hf * 1024 + 1024)
                act(EAL.ap, ALB.ap, AF.Exp, ALB.r(), EAL.r())
                tt(DVE, WS_.ap, ALB.ap, NAT.ap[:, s, H0:H0 + 16], ALU.add, ALB.r() + NAT.r(), WS_.r())
                act(WS_.ap, WS_.ap, AF.Exp, WS_.r(), WS_.r())
                tt(DVE, WS_.ap, WS_.ap, DTT.ap[:, s, H0:H0 + 16], ALU.mult, WS_.r() + DTT.r(), WS_.r())
                tt(DVE, XDW.ap.rearrange("p (h q) -> p h q", q=64), XTM.ap[:, s, :].rearrange("p (h q) -> p h q", q=64),
                   WS_.ap.unsqueeze(2).to_broadcast([128, 16, 64]), ALU.mult, XTM.r(s) + WS_.r(), XDW.r())
                tt(POOL, STh.rearrange("p (h q) -> p h q", q=64), STh.rearrange("p (h q) -> p h q", q=64),
                   EAL.ap.unsqueeze(2).to_broadcast([128, 16, 64]), ALU.mult, STr + EAL.r(), STr)
                for half in range(2):
                    pd = bank(OTH)
                    for gg in range(2):
                        gl = half * 2 + gg
                        mm(PS[pd][:, gg * 256:(gg + 1) * 256], BT.ap[:, s, gl * 128:(gl + 1) * 128], XDW.ap[:, gl * 256:(gl + 1) * 256],
                           gg == 0, gg == 1, BT.r(s) + XDW.r(), PR[pd])
                    sl_ = slice(half * 512, (half + 1) * 512)
                    tt(DVE, STh[:, sl_], STh[:, sl_], PS[pd], ALU.add, STr + PR[pd], STr)
                tt(POOL, YA.ap.rearrange("p (h q) -> p h q", q=64), XTM.ap[:, s, :].rearrange("p (h q) -> p h q", q=64),
                   DBC.ap[:, H0:H0 + 16].unsqueeze(2).to_broadcast([128, 16, 64]), ALU.mult, XTM.r(s) + DBC.r(), YA.r())
                for q in range(2):
                    sl_ = slice(q * 512, (q + 1) * 512)
                    tt(DVE, YA.ap[:, sl_], YA.ap[:, sl_], PS[YBANKS[q]], ALU.add, YA.r() + PR[YBANKS[q]], YA.r())

                def fin_gen(s=s, cs=cs, hf=hf):
                    tt(POOL, YA.ap, YA.ap, ZS.ap[:, s, :], ALU.mult, YA.r() + ZS.r(s), YA.r())
                    yield
                    for g4 in range(4):
                        op(ACT, lambda g4=g4: nc.scalar.activation(out=YB.ap[:, g4 * 256:(g4 + 1) * 256],
                                                                    in_=YA.ap[:, g4 * 256:(g4 + 1) * 256], func=AF.Square,
                                                                    accum_out=SS.ap[:, g4:g4 + 1]),
                           YA.r(), YB.r() + SS.r())
                    act(SS.ap, SS.ap, AF.Ln, SS.r() + EPSC.r(), SS.r(), bias=EPSC.ap[:, 0:1], scale=1.0 / 256)
                    act(SS.ap, SS.ap, AF.Exp, SS.r(), SS.r(), scale=-0.5)
                    yield
                    for g4 in range(4):
                        act(YA.ap[:, g4 * 256:(g4 + 1) * 256], YA.ap[:, g4 * 256:(g4 + 1) * 256], AF.Copy, YA.r() + SS.r(), YA.r(),
                            scale=SS.ap[:, g4:g4 + 1])
                    tt(POOL, YN.ap, YA.ap, NG.ap[:, hf * 1024:(hf + 1) * 1024], ALU.mult, YA.r() + NG.r(), YN.r())
                    yield
                    for q in range(2):
                        pbt = bank(OTH)
                        for cc in range(4):
                            c = q * 4 + cc
                            tr(PSB[pbt][:, cc * 128:(cc + 1) * 128], YN.ap[:, c * 128:(c + 1) * 128], IDB.ap, YN.r() + IDB.r(), PR[pbt])
                        c0 = hf * 8 + q * 4
                        cp(ACT, Y.ap[:, c0:c0 + 4, cs], PSB[pbt][:, 0:512].rearrange("p (c t) -> p c t", c=4), PR[pbt],
                           Y.r(c0, c0 + 4))
                        yield
                prevB = fin_gen()
            drain([prevB])
        dump("ym", Y.ap, Y.r())
        outproj_add("o_w_out", Y, 16)

    def final(t):
        kb.off = pers_off
        HF = alloc([8, TT])
        IO = alloc([4, 1024])
        SQ = alloc([8, TT], BF16)
        norm_fm(X, TT, "gfin", 0, HF, SQ)
        for s in range(4):
            for hf in range(2):
                pb = bank()
                for cc in range(4):
                    c = hf * 4 + cc
                    tr(PS[pb][:, cc * 128:(cc + 1) * 128], HF.ap[:, c, s * 128:(s + 1) * 128], IDF, HF.r(c) + CST.r(), PR[pb])
                cp(ACT if hf else DVE, IO.ap[:, s, hf * 512:(hf + 1) * 512], PS[pb], PR[pb], IO.r(s))
        dma(POOL, T_OUT, out_d[t * TT:(t + 1) * TT, :].rearrange("(s p) f -> p s f", p=128), IO.ap, IO.r(), [])

    stages = ["l0", "a0", "f0", "l1", "a1", "f1"]
    for t in range(NT):
        load_x(t)
        for st in stages:
            if stop_after == "load":
                break
            if st == "l0":
                l0_mixer(t)
            elif st == "a0":
                if t == 0:
                    mem_kv()
                xattn(0)
            elif st == "f0":
                ffn(0)
            elif st == "l1":
                l1_mixer(t)
            elif st == "a1":
                xattn(1)
            elif st == "f1":
                ffn(1)
            if stop_after == st or (stop_after in ("rglru", "hgA", "hgB") and st == "l0"):
                break
        final(t)
    nc.gpsimd.wait_ge(T_OUT.sem, T_OUT.n * 16)
    for t_ in kb.dbgt:
        nc.gpsimd.wait_ge(t_.sem, t_.n * 16)
    es.close()
    kb.counts = counts
    kb.wneed = wneed
    return nc, kb


def _cols(v):
    v = np.asarray(v, np.float32).reshape(-1)
    return np.ascontiguousarray(v.reshape(v.size // 128, 128).T)


def make_shared_inputs(inp):
    pc = np.zeros((128, NPC), np.float32)

    def put(name, arr):
        a = _cols(arr)
        pc[:, PCI[name]:PCI[name] + a.shape[1]] = a
    put("gmix", inp["norm_mix_g"])
    put("gmq", inp["norm_mem_q_g"])
    put("gmkv", inp["norm_mem_kv_g"])
    put("gffn", inp["norm_ffn_g"])
    put("gfin", inp["final_norm_g"])
    put("acw", inp["a_conv_w"][0])
    put("acb", inp["a_conv_b"][0])
    put("arb", inp["a_gate_r_b"][0])
    put("aib", inp["a_gate_i_b"][0])
    put("alam", inp["a_lambda"][0])
    put("lbl", inp["b_lb_logits"])
    put("bng", inp["b_norm_g"][0])
    put("mcw", inp["m_conv_w"][0])
    put("mcb", inp["m_conv_b"][0])
    pc[:32, PCI["dtb"]] = np.asarray(inp["m_dt_bias"][0], np.float32)
    pc[:32, PCI["alog"]] = np.asarray(inp["m_a_log"][0], np.float32)
    cst = np.zeros((128, NCST), np.float32)
    cst[:, C_ID:C_ID + 128] = np.eye(128, dtype=np.float32)
    s_ = np.arange(128)[:, None]
    t_ = np.arange(128)[None, :]
    cst[:, C_MH:C_MH + 128] = ((s_ <= t_) & ((s_ // 64) == (t_ // 64))).astype(np.float32)
    cst[:, C_MC:C_MC + 128] = (s_ <= t_).astype(np.float32)
    cst[:, C_R64:C_R64 + 512] = (np.arange(512) % 64 != 0).astype(np.float32)[None, :]
    cst[:, C_R128:C_R128 + 512] = (np.arange(512) % 128 != 0).astype(np.float32)[None, :]
    sh = {"pc": pc, "cst": cst}
    sh["gr"] = np.ascontiguousarray(np.asarray(inp["a_gate_r_w"][0], np.float32).transpose(1, 0, 2).reshape(128, 1024))
    sh["gi"] = np.ascontiguousarray(np.asarray(inp["a_gate_i_w"][0], np.float32).transpose(1, 0, 2).reshape(128, 1024))
    sh["mng"] = np.asarray(inp["m_norm_g"], np.float32).reshape(1, 2048)
    sh["md"] = np.repeat(np.asarray(inp["m_d"], np.float32).reshape(1, 32), 1, axis=0)
    sh["e_w_in"] = np.asarray(inp["e_w_in"][0], np.float32)
    sh["e_w_out"] = np.asarray(inp["e_w_out"][0], np.float32)
    sh["o_w_in"] = np.asarray(inp["o_w_in"][0], np.float32)
    sh["o_w_out"] = np.asarray(inp["o_w_out"][0], np.float32)
    for l in range(2):
        sh[f"xq{l}"] = np.asarray(inp["xq_w"][l], np.float32)
        sh[f"xk{l}"] = np.asarray(inp["xk_w"][l], np.float32)
        sh[f"xv{l}"] = np.asarray(inp["xv_w"][l], np.float32)
        sh[f"xo{l}"] = np.asarray(inp["xo_w"][l], np.float32)
        sh[f"w1_{l}"] = np.asarray(inp["ffn_w1"][l], np.float32)
        sh[f"w2_{l}"] = np.asarray(inp["ffn_w2"][l], np.float32)
    return sh


_CACHE = {}


def kernel(**inputs):
    x = np.asarray(inputs["x"], np.float32)
    mem = np.asarray(inputs["mem"], np.float32)
    B, S, _ = x.shape
    if S not in _CACHE:
        _CACHE[S] = build(S)[0]
    nc = _CACHE[S]
    sh = make_shared_inputs(inputs)
    in_maps = []
    for b in range(B):
        m = dict(sh)
        m["x"] = np.ascontiguousarray(x[b])
        m["mem"] = np.ascontiguousarray(mem[b])
        in_maps.append(m)
    res = run_bass_kernel_spmd(nc, in_maps, core_ids=list(range(B)))
    return np.stack([np.asarray(r["out"], np.float32) for r in res.results], axis=0)
```

```python
import numpy as np
from contextlib import ExitStack
import concourse.bass as bass
import concourse.mybir as mybir
from concourse.bass_utils import run_bass_kernel_spmd

F32 = mybir.dt.float32
BF16 = mybir.dt.bfloat16
AF = mybir.ActivationFunctionType
ALU = mybir.AluOpType
D = 1024
TT = 512
EPS = 1e-6
MEM = 256

PCI = {}
_n = 0
for _nm, _c in [("gmix", 16), ("gmq", 16), ("gmkv", 16), ("gffn", 16), ("gfin", 8), ("acw", 32), ("acb", 8),
                ("arb", 8), ("aib", 8), ("alam", 8), ("lbl", 24), ("bng", 8), ("mcw", 128), ("mcb", 32),
                ("dtb", 1), ("alog", 1)]:
    PCI[_nm] = _n
    _n += _c
NPC = _n
C_ID, C_MH, C_MC, C_R64, C_R128 = 0, 128, 256, 384, 896
NCST = 1408

WSHAPES = {"e_w_in": (1024, 6144), "e_w_out": (2048, 1024), "o_w_in": (1024, 6176), "o_w_out": (2048, 1024),
           "xq0": (1024, 1024), "xk0": (1024, 1024), "xv0": (1024, 1024), "xo0": (1024, 1024),
           "xq1": (1024, 1024), "xk1": (1024, 1024), "xv1": (1024, 1024), "xo1": (1024, 1024),
           "w1_0": (1024, 4096), "w2_0": (4096, 1024), "w1_1": (1024, 4096), "w2_1": (4096, 1024)}
WORDER = ["e_w_in", "e_w_out", "xk0", "xv0", "xk1", "xv1", "xq0", "xo0", "w1_0", "w2_0",
          "o_w_in", "o_w_out", "xq1", "xo1", "w1_1", "w2_1"]


class Reg:
    __slots__ = ("w", "r")

    def __init__(self):
        self.w = None
        self.r = {}


class Trk:
    def __init__(self, sem, inc):
        self.sem = sem
        self.inc = inc
        self.n = 0


class Eng:
    def __init__(self, eng, trk, is_pe=False):
        self.eng = eng
        self.trk = trk
        self.seen = {}
        self.is_pe = is_pe


class View:
    def __init__(self, kb, b0, dtype, shape):
        self.kb = kb
        self.b0 = b0
        self.es = 4 if dtype == F32 else 2
        self.shape = tuple(shape)
        n = int(np.prod(shape))
        self.nbytes = n * self.es
        base = kb.arena[:, b0 // 4:(b0 + self.nbytes) // 4]
        ap = base if dtype == F32 else base.bitcast(BF16)
        if len(shape) == 2:
            ap = ap.rearrange("p (a b) -> p a b", a=shape[0])
        elif len(shape) == 3:
            ap = ap.rearrange("p (a b c) -> p a b c", a=shape[0], b=shape[1])
        self.ap = ap
        self.slot = self.nbytes // shape[0] if len(shape) > 1 else self.nbytes

    def r(self, i=None, j=None):
        G = self.kb.G
        if i is None:
            lo, hi = self.b0, self.b0 + self.nbytes
        else:
            if j is None:
                j = i + 1
            lo, hi = self.b0 + i * self.slot, self.b0 + j * self.slot
        return self.kb.regs[lo // G:(hi + G - 1) // G]

    def rb(self, lo_el, hi_el):
        G = self.kb.G
        lo, hi = self.b0 + lo_el * self.es, self.b0 + hi_el * self.es
        return self.kb.regs[lo // G:(hi + G - 1) // G]


def build(S, stop_after=None, dbg=None):
    NT = S // TT
    nc = bass.Bass("TRN2", target_bir_lowering=False)
    kb = type("KB", (), {})()
    es = ExitStack()
    E_ = es.enter_context

    def din(name, shape, dt=F32):
        return nc.dram_tensor(name, list(shape), dt, kind="ExternalInput").ap()

    x_d = din("x", [S, D])
    mem_d = din("mem", [MEM, D])
    pc_d = din("pc", [128, NPC])
    cst_d = din("cst", [128, NCST])
    gr_d = din("gr", [128, 1024])
    gi_d = din("gi", [128, 1024])
    mng_d = din("mng", [1, 2048])
    md_d = din("md", [1, 32])
    _stage_w = {"load": [], "rglru": ["e_w_in"], "hgA": [], "hgB": [], "l0": ["e_w_in", "e_w_out"], "a0": ["xq0", "xo0"], "f0": ["w1_0", "w2_0"],
                "l1": ["o_w_in", "o_w_out"], "a1": ["xq1", "xo1"], "f1": ["w1_1", "w2_1"]}
    wneed = ["xk0", "xv0", "xk1", "xv1"]
    for _st in ["load", "rglru", "hgA", "hgB", "l0", "a0", "f0", "l1", "a1", "f1"]:
        wneed += _stage_w[_st]
        if stop_after == _st:
            break
    wneed = [k for k in WORDER if k in set(wneed)]
    wf = {k: din(k, WSHAPES[k]) for k in wneed}
    wb = {k: nc.dram_tensor(k + "_b", list(WSHAPES[k]), BF16, kind="Internal").ap() for k in wneed}
    out_d = nc.dram_tensor("out", [S, D], F32, kind="ExternalOutput").ap()
    dbg_d = {}
    if dbg:
        for nm, shp in dbg.items():
            dbg_d[nm] = nc.dram_tensor("dbg_" + nm, list(shp), F32, kind="ExternalOutput").ap()

    ARENA_BYTES = 206 * 1024
    kb.G = 512
    kb.arena = E_(nc.sbuf_tensor("arena", [128, ARENA_BYTES // 4], F32))[:]
    kb.regs = [Reg() for _ in range(ARENA_BYTES // kb.G)]
    kb.off = 0
    kb.dbgt = []

    def alloc(shape, dtype=F32):
        v = View(kb, kb.off, dtype, shape)
        kb.off += (v.nbytes + kb.G - 1) // kb.G * kb.G
        assert kb.off <= ARENA_BYTES, f"SBUF arena overflow {kb.off}"
        return v

    def mksem(name):
        return E_(nc.semaphore(name))

    PE = Eng(nc.tensor, Trk(mksem("s_pe"), 1), is_pe=True)
    ACT = Eng(nc.scalar, Trk(mksem("s_act"), 1))
    DVE = Eng(nc.vector, Trk(mksem("s_dve"), 1))
    POOL = Eng(nc.gpsimd, Trk(mksem("s_pool"), 1))
    SP = Eng(nc.sync, None)
    kb.nsem = 0

    def newtrk():
        kb.nsem += 1
        return Trk(mksem(f"s_dma{kb.nsem}"), 16)
    NCAST = 6
    T_CAST = [newtrk() for _ in range(NCAST)]
    T_XIN = newtrk()
    T_OUT = newtrk()
    counts = {"wait": 0, "ins": 0}

    def op(E, fn, r=(), w=(), trk=None):
        trk = trk or E.trk
        need = {}
        for g in r:
            if g.w is not None:
                t, c = g.w
                if need.get(t, 0) < c:
                    need[t] = c
        for g in w:
            if g.w is not None:
                t, c = g.w
                if need.get(t, 0) < c:
                    need[t] = c
            for t, c in g.r.items():
                if need.get(t, 0) < c:
                    need[t] = c
        for t, c in need.items():
            if E.is_pe and t is E.trk:
                continue
            if E.seen.get(t, 0) < c:
                E.eng.wait_ge(t.sem, c * t.inc)
                E.seen[t] = c
                counts["wait"] += 1
        ins = fn()
        trk.n += 1
        ins.then_inc(trk.sem, trk.inc)
        counts["ins"] += 1
        n = trk.n
        for g in r:
            if g.r.get(trk, 0) < n:
                g.r[trk] = n
        for g in w:
            g.w = (trk, n)
            g.r = {}

    PSt = [E_(nc.psum_tensor(f"ps{i}", [128, 512], F32)) for i in range(8)]
    PS = [t[:] for t in PSt]
    PSB = [t[:].bitcast(BF16) for t in PSt]
    PR = [[Reg()] for _ in range(8)]
    kb.pb = 0

    def bank(pool=(0, 1, 2, 3, 4, 5, 6, 7)):
        kb.pb = (kb.pb + 1) % len(pool)
        return pool[kb.pb]

    def mm(out, lhsT, rhs, start, stop, r, w):
        op(PE, lambda: nc.tensor.matmul(out, lhsT=lhsT, rhs=rhs, start=start, stop=stop), r, w)

    def tr(out, in_, ident, r, w):
        op(PE, lambda: nc.tensor.transpose(out=out, in_=in_, identity=ident), r, w)

    def act(out, in_, func, r, w, bias=None, scale=None):
        kw = {}
        if bias is not None:
            kw["bias"] = bias
        if scale is not None:
            kw["scale"] = scale
        op(ACT, lambda: nc.scalar.activation(out=out, in_=in_, func=func, **kw), r, w)

    def ts(E, out, in0, s1, s2, op0, op1, r, w):
        if s2 is None:
            op(E, lambda: E.eng.tensor_scalar(out=out, in0=in0, scalar1=s1, scalar2=None, op0=op0), r, w)
        else:
            op(E, lambda: E.eng.tensor_scalar(out=out, in0=in0, scalar1=s1, scalar2=s2, op0=op0, op1=op1), r, w)

    def tt(E, out, in0, in1, o, r, w):
        op(E, lambda: E.eng.tensor_tensor(out=out, in0=in0, in1=in1, op=o), r, w)

    def stt(out, in0, scalar, in1, op0, op1, r, w):
        op(DVE, lambda: nc.vector.scalar_tensor_tensor(out=out, in0=in0, scalar=scalar, in1=in1, op0=op0, op1=op1), r, w)

    def scan(out, d0, d1, initial, r, w):
        op(DVE, lambda: nc.vector.tensor_tensor_scan(out=out, data0=d0, data1=d1, initial=initial,
                                                     op0=ALU.mult, op1=ALU.add), r, w)

    def cp(E, out, in_, r, w):
        if E is ACT:
            op(ACT, lambda: nc.scalar.activation(out=out, in_=in_, func=AF.Copy), r, w)
        else:
            op(E, lambda: E.eng.tensor_copy(out=out, in_=in_), r, w)

    def recip(out, in_, r, w):
        op(DVE, lambda: nc.vector.reciprocal(out=out, in_=in_), r, w)

    def dma(E, trk, out, in_, r, w):
        op(E, lambda: E.eng.dma_start(out=out, in_=in_), r, w, trk=trk)

    WBR = {k: [Reg() for _ in range(v[0] // 128)] for k, v in WSHAPES.items()}

    PC = alloc([NPC])
    CST = alloc([NCST])
    IDB = alloc([128], BF16)
    ONES = alloc([128], BF16)
    EPSC = alloc([1])
    NG = alloc([2048])
    DBC = alloc([32])
    GR = alloc([8, 128], BF16)
    GI = alloc([8, 128], BF16)
    CL = alloc([8])
    CL2 = alloc([8])
    LB = alloc([8])
    OML = alloc([8])
    ANEG = alloc([1])
    NB = alloc([16])
    KF = [alloc([8, MEM], BF16) for _ in range(2)]
    VM = [alloc([2, 1024], BF16) for _ in range(2)]
    X = alloc([8, TT])
    H = alloc([8, TT], BF16)
    Y = alloc([16, TT], BF16)
    RS = alloc([TT])
    CAR0 = alloc([8, 4])
    HST = alloc([8])
    ST0 = alloc([8, 128])
    CAR1 = alloc([32, 4])
    ST1 = alloc([2048])
    NRING = 4
    RING = [alloc([8 * 512], BF16) for _ in range(NRING)]
    T_RING = [newtrk() for _ in range(NRING)]
    kb.ring = 0
    pers_off = kb.off

    def pc(name, i):
        c = PCI[name] + i
        return PC.ap[:, c:c + 1]

    IDF = CST.ap[:, C_ID:C_ID + 128]
    MASKH = CST.ap[:, C_MH:C_MH + 128]
    MASKC = CST.ap[:, C_MC:C_MC + 128]
    R64 = CST.ap[:, C_R64:C_R64 + 512]
    R128 = CST.ap[:, C_R128:C_R128 + 512]

    class Slab:
        pass

    def wload(name, k0, kc, c0, ncols):
        assert kc * ncols * 2 <= 8192
        v = RING[kb.ring]
        trk_ = T_RING[kb.ring]
        kb.ring = (kb.ring + 1) % NRING
        s = Slab()
        base = v.ap[:, 0:kc * ncols]
        s.ap = base.rearrange("p (k n) -> p k n", k=kc)
        s.regs = v.rb(0, kc * ncols)
        src = wb[name][k0 * 128:(k0 + kc) * 128, c0:c0 + ncols].rearrange("(k p) n -> p k n", p=128)
        dma(SP, trk_, s.ap, src, r=WBR[name][k0:k0 + kc], w=s.regs)
        return s

    dma(SP, newtrk(), PC.ap, pc_d[:, :], [], PC.r())
    dma(SP, newtrk(), CST.ap, cst_d[:, :], [], CST.r())
    dma(SP, newtrk(), NG.ap, mng_d.partition_broadcast(128), [], NG.r())
    dma(SP, newtrk(), DBC.ap, md_d.partition_broadcast(128), [], DBC.r())
    kb.ncast = 0

    def cast_weights(names, nfl=NCAST):
        for name in names:
            if name not in wneed:
                continue
            K_, N_ = WSHAPES[name]
            nb_ = 1
            for kb_ in range(0, K_ // 128, nb_):
                tc_ = T_CAST[kb.ncast % nfl]
                kb.ncast += 1
                if tc_.n > 0:
                    nc.gpsimd.wait_ge(tc_.sem, tc_.n * 16)
                dst_ = wb[name][kb_ * 128:(kb_ + nb_) * 128, :]
                src_ = wf[name][kb_ * 128:(kb_ + nb_) * 128, :]
                if N_ < 6144:
                    rr_ = 8192 // N_
                    dst_ = dst_.rearrange("(a b) n -> a (b n)", b=rr_)
                    src_ = src_.rearrange("(a b) n -> a (b n)", b=rr_)
                dma(POOL, tc_, dst_, src_, [], WBR[name][kb_:kb_ + nb_])
    cast_weights(["e_w_in", "e_w_out", "xk0", "xv0", "xk1", "xv1", "xq0", "xo0"])

    m0 = kb.off
    TMPF = alloc([1024])
    dma(SP, newtrk(), TMPF.ap, gr_d[:, :], [], TMPF.r())
    cp(DVE, GR.ap, TMPF.ap.rearrange("p (a b) -> p a b", a=8), TMPF.r(), GR.r())
    TMPG = alloc([1024])
    dma(SP, newtrk(), TMPG.ap, gi_d[:, :], [], TMPG.r())
    cp(DVE, GI.ap, TMPG.ap.rearrange("p (a b) -> p a b", a=8), TMPG.r(), GI.r())
    cp(DVE, IDB.ap, IDF, CST.r(), IDB.r())
    op(DVE, lambda: nc.vector.memset(ONES.ap, 1.0), [], ONES.r())
    op(DVE, lambda: nc.vector.memset(EPSC.ap, EPS), [], EPSC.r())
    for v in (CAR0, HST, ST0, CAR1, ST1):
        op(DVE, lambda v=v: nc.vector.memset(v.ap, 0.0), [], v.r())
    T8 = alloc([24])
    act(T8.ap[:, 0:8], PC.ap[:, PCI["alam"]:PCI["alam"] + 8], AF.Exp, PC.r(), T8.r(), scale=-1.0)
    act(T8.ap[:, 0:8], T8.ap[:, 0:8], AF.Ln, T8.r(), T8.r(), bias=1.0)
    ts(DVE, CL.ap, T8.ap[:, 0:8], -8.0, None, ALU.mult, None, T8.r(), CL.r())
    ts(DVE, CL2.ap, T8.ap[:, 0:8], -16.0, None, ALU.mult, None, T8.r(), CL2.r())
    act(T8.ap, PC.ap[:, PCI["lbl"]:PCI["lbl"] + 24], AF.Exp, PC.r(), T8.r())
    tt(DVE, LB.ap, T8.ap[:, 0:8], T8.ap[:, 8:16], ALU.add, T8.r(), LB.r())
    tt(DVE, LB.ap, LB.ap, T8.ap[:, 16:24], ALU.add, T8.r() + LB.r(), LB.r())
    recip(LB.ap, LB.ap, LB.r(), LB.r())
    tt(DVE, LB.ap, LB.ap, T8.ap[:, 0:8], ALU.mult, LB.r() + T8.r(), LB.r())
    ts(DVE, OML.ap, LB.ap, -1.0, 1.0, ALU.mult, ALU.add, LB.r(), OML.r())
    ts(DVE, NB.ap, PC.ap[:, PCI["arb"]:PCI["arb"] + 16], -1.0, None, ALU.mult, None, PC.r(), NB.r())
    act(ANEG.ap, pc("alog", 0), AF.Exp, PC.r(), ANEG.r())
    ts(DVE, ANEG.ap, ANEG.ap, -1.0, None, ALU.mult, None, ANEG.r(), ANEG.r())

    def norm_fm(Xv, n, gname, goff, out, tmp_sq):
        for c in range(8):
            act(tmp_sq.ap[:, c, :n], Xv.ap[:, c, :n], AF.Square, Xv.r(c), tmp_sq.r(c))
        pb = bank()
        for c in range(8):
            mm(PS[pb][:, :n], ONES.ap, tmp_sq.ap[:, c, :n], c == 0, c == 7, tmp_sq.r(c) + ONES.r(), PR[pb])
        act(RS.ap[:, :n], PS[pb][:, :n], AF.Ln, PR[pb] + EPSC.r(), RS.r(), bias=EPSC.ap[:, 0:1], scale=1.0 / D)
        act(RS.ap[:, :n], RS.ap[:, :n], AF.Exp, RS.r(), RS.r(), scale=-0.5)
        for c in range(8):
            stt(out.ap[:, c, :n], Xv.ap[:, c, :n], pc(gname, goff + c), RS.ap[:, :n], ALU.mult, ALU.mult,
                Xv.r(c) + RS.r() + PC.r(), out.r(c))

    def step_all(pend):
        for g_ in list(pend):
            try:
                next(g_)
            except StopIteration:
                pend.remove(g_)

    def drain(pend):
        while pend:
            step_all(pend)

    def interleave(gens):
        pend = []
        for g_ in gens:
            pend.append(g_)
        drain(pend)

    def proj_fm(name, c0, nchunks, Hv, n, consumer, kc=8, k0=0, pend=None):
        own = pend is None
        if own:
            pend = []
        done = 0
        while done < nchunks:
            g = min(4, nchunks - done)
            ws = wload(name, k0, kc, c0 + done * 128, g * 128)
            for cc in range(g):
                pb = bank()
                for k in range(kc):
                    mm(PS[pb][:, :n], ws.ap[:, k, cc * 128:(cc + 1) * 128], Hv.ap[:, k, :n], k == 0, k == kc - 1,
                       ws.regs + Hv.r(k), PR[pb])
                gen_ = consumer(done + cc, pb)
                started = None
                if gen_ is not None:
                    try:
                        next(gen_)
                        started = gen_
                    except StopIteration:
                        pass
                step_all(pend)
                if started is not None:
                    pend.append(started)
            done += g
        if own:
            drain(pend)

    def outproj_add(name, Yv, kc):
        ncol = 8192 // (kc * 2)
        per = ncol // 128
        for g0 in range(0, 8, per):
            ws = wload(name, 0, kc, g0 * 128, ncol)
            for cc in range(per):
                c = g0 + cc
                pb = bank()
                for k in range(kc):
                    mm(PS[pb], ws.ap[:, k, cc * 128:(cc + 1) * 128], Yv.ap[:, k, :], k == 0, k == kc - 1,
                       ws.regs + Yv.r(k), PR[pb])
                tt(DVE, X.ap[:, c, :], X.ap[:, c, :], PS[pb], ALU.add, X.r(c) + PR[pb], X.r(c))

    def dump(nm, view_ap, regs):
        if nm in dbg_d:
            kb.dbgt.append(newtrk())
            dma(POOL, kb.dbgt[-1], dbg_d[nm], view_ap, regs, [])

    def mem_kv():
        kb.off = pers_off
        MIN = alloc([2, 1024])
        MX = alloc([8, MEM])
        MH = alloc([8, MEM], BF16)
        MSQ = alloc([8, MEM], BF16)
        dma(SP, newtrk(), MIN.ap, mem_d.rearrange("(s p) f -> p s f", p=128), [], MIN.r())
        for c in range(8):
            pb = bank()
            for s in range(2):
                tr(PS[pb][:, s * 128:(s + 1) * 128], MIN.ap[:, s, c * 128:(c + 1) * 128], IDF, MIN.r(s) + CST.r(), PR[pb])
            cp(ACT, MX.ap[:, c, :], PS[pb][:, 0:MEM], PR[pb], MX.r(c))
        for l in range(2):
            norm_fm(MX, MEM, "gmkv", l * 8, MH, MSQ)

            def kcons(ci, pb, l=l):
                cp(ACT, KF[l].ap[:, ci, :], PS[pb][:, :MEM], PR[pb], KF[l].r(ci))
            proj_fm(f"xk{l}", 0, 8, MH, MEM, kcons)
            for sl in range(2):
                ws = wload(f"xv{l}", 0, 8, sl * 512, 512)
                for mh in range(2):
                    pb = bank()
                    for k in range(8):
                        mm(PS[pb], MH.ap[:, k, mh * 128:(mh + 1) * 128], ws.ap[:, k, :], k == 0, k == 7,
                           ws.regs + MH.r(k), PR[pb])
                    cp(DVE, VM[l].ap[:, mh, sl * 512:(sl + 1) * 512], PS[pb], PR[pb], VM[l].rb(mh * 1024 + sl * 512, mh * 1024 + sl * 512 + 512))


    def load_x(t):
        kb.off = pers_off
        IO = alloc([4, 1024])
        dma(SP, T_XIN, IO.ap, x_d[t * TT:(t + 1) * TT, :].rearrange("(s p) f -> p s f", p=128), [], IO.r())
        for c in range(8):
            pb = bank()
            for s in range(4):
                tr(PS[pb][:, s * 128:(s + 1) * 128], IO.ap[:, s, c * 128:(c + 1) * 128], IDF, IO.r(s) + CST.r(), PR[pb])
            cp(ACT if c % 2 else DVE, X.ap[:, c, :], PS[pb], PR[pb], X.r(c))

    def conv_chunk(pb, XAv, CARv, j, wname, wstride, bname, XCv_ap, XC_regs, TMPv=None):
        cp(POOL, XAv.ap[:, 0:3], CARv.ap[:, j, 0:3], CARv.r(j), XAv.r())
        cp(ACT, XAv.ap[:, 3:515], PS[pb], PR[pb], XAv.r())
        cp(POOL, CARv.ap[:, j, 0:3], XAv.ap[:, 512:515], XAv.r(), CARv.r(j))
        act(XCv_ap, PS[pb], AF.Identity, PR[pb] + PC.r(), XC_regs, bias=pc(bname, j), scale=pc(wname, 3 * wstride + j))
        for k in range(3):
            stt(XCv_ap, XAv.ap[:, k:k + 512], pc(wname, k * wstride + j), XCv_ap, ALU.mult, ALU.add,
                XAv.r() + XC_regs + PC.r(), XC_regs)

    def l0_mixer(t):
        kb.off = pers_off
        SQ = alloc([8, TT], BF16)
        norm_fm(X, TT, "gmix", 0, H, SQ)
        kb.off = pers_off
        FA = [[alloc([TT]) for _ in range(6)] for _ in range(2)]
        GATE = alloc([4, TT], BF16)
        GATE2 = alloc([4, TT], BF16)
        QS = alloc([4, TT], BF16)
        QT = alloc([4, TT], BF16)
        KT = alloc([4, TT], BF16)
        KDT = alloc([4, TT], BF16)
        VT = alloc([4, 512], BF16)
        XA = [alloc([516]) for _ in range(2)]
        XCBs = [alloc([TT], BF16) for _ in range(2)]
        KDs = [alloc([TT], BF16) for _ in range(2)]
        PTs = [alloc([4, 128], BF16) for _ in range(2)]
        SB = [alloc([8, 128], BF16) for _ in range(2)]
        EL = alloc([4, 8])
        OSQs = [alloc([TT], BF16) for _ in range(2)]
        RS2 = [alloc([TT]) for _ in range(2)]
        W = "e_w_in"

        for hf in range(2):
            def c_ga(ci, pb):
                act(GATE.ap[:, ci, :], PS[pb], AF.Gelu, PR[pb], GATE.r(ci))
            proj_fm(W, 1024 + hf * 512, 4, H, TT, c_ga)

            def c_xa(ci, pb, hf=hf):
                j = hf * 4 + ci
                XAb = XA[j % 2]
                XC, RR, II, AA, A2, HS = FA[j % 2]
                XCB = XCBs[j % 2]
                conv_chunk(pb, XAb, CAR0, j, "acw", 8, "acb", XC.ap, XC.r())
                cp(POOL, XCB.ap, XC.ap, XC.r(), XCB.r())
                yield
                p1 = bank()
                mm(PS[p1], GR.ap[:, j, :], XCB.ap, True, True, GR.r() + XCB.r(), PR[p1])
                act(RR.ap, PS[p1], AF.Exp, PR[p1] + NB.r(), RR.r(), bias=NB.ap[:, j:j + 1], scale=-1.0)
                act(RR.ap, RR.ap, AF.Ln, RR.r(), RR.r(), bias=1.0)
                act(RR.ap, RR.ap, AF.Exp, RR.r(), RR.r(), scale=-1.0)
                p2 = bank()
                mm(PS[p2], GI.ap[:, j, :], XCB.ap, True, True, GI.r() + XCB.r(), PR[p2])
                act(II.ap, PS[p2], AF.Exp, PR[p2] + NB.r(), II.r(), bias=NB.ap[:, 8 + j:9 + j], scale=-1.0)
                ts(DVE, II.ap, II.ap, 1.0, None, ALU.add, None, II.r(), II.r())
                recip(II.ap, II.ap, II.r(), II.r())
                act(AA.ap, RR.ap, AF.Exp, RR.r() + CL.r(), AA.r(), scale=CL.ap[:, j:j + 1])
                act(A2.ap, RR.ap, AF.Exp, RR.r() + CL2.r(), A2.r(), scale=CL2.ap[:, j:j + 1])
                act(A2.ap, A2.ap, AF.Ln, A2.r(), A2.r(), bias=1.0, scale=-1.0)
                act(A2.ap, A2.ap, AF.Exp, A2.r(), A2.r(), scale=0.5)
                tt(DVE, II.ap, II.ap, XC.ap, ALU.mult, II.r() + XC.r(), II.r())
                tt(DVE, II.ap, II.ap, A2.ap, ALU.mult, II.r() + A2.r(), II.r())
                scan(HS.ap, AA.ap, II.ap, HST.ap[:, j:j + 1], AA.r() + II.r() + HST.r(), HS.r())
                cp(POOL, HST.ap[:, j:j + 1], HS.ap[:, 511:512], HS.r(), HST.r())
                tt(POOL, Y.ap[:, j, :], HS.ap, GATE.ap[:, ci, :], ALU.mult, HS.r() + GATE.r(ci), Y.r(j))
            proj_fm(W, hf * 512, 4, H, TT, c_xa)
        dump("ya", Y.ap[:, 0:8, :], Y.r(0, 8))
        if stop_after == "rglru":
            return

        for hf in range(2):
            def c_g(ci, pb):
                act(GATE2.ap[:, ci, :], PS[pb], AF.Silu, PR[pb], GATE2.r(ci))
            proj_fm(W, 5120 + hf * 512, 4, H, TT, c_g)

            def c_q(ci, pb):
                act(QS.ap[:, ci, :], PS[pb], AF.Silu, PR[pb], QS.r(ci))
            proj_fm(W, 2048 + hf * 512, 4, H, TT, c_q)
            ws = wload(W, 0, 8, 4096 + hf * 512, 512)
            for s in range(4):
                pb = bank()
                for k in range(8):
                    mm(PS[pb], H.ap[:, k, s * 128:(s + 1) * 128], ws.ap[:, k, :], k == 0, k == 7, ws.regs + H.r(k), PR[pb])
                cp(ACT if s % 2 else DVE, VT.ap[:, s, :], PS[pb], PR[pb], VT.r(s))
            if stop_after == "hgA":
                return

            def c_f(ci, pb, hf=hf):
                j = hf * 4 + ci
                FF, LF, BB, E1, E2, _u = FA[j % 2]
                KD = KDs[j % 2]
                act(FF.ap, PS[pb], AF.Exp, PR[pb], FF.r(), scale=-1.0)
                act(FF.ap, FF.ap, AF.Ln, FF.r(), FF.r(), bias=1.0)
                act(FF.ap, FF.ap, AF.Exp, FF.r(), FF.r(), scale=-1.0)
                ts(DVE, FF.ap, FF.ap, OML.ap[:, j:j + 1], LB.ap[:, j:j + 1], ALU.mult, ALU.add, FF.r() + OML.r() + LB.r(), FF.r())
                act(LF.ap, FF.ap, AF.Ln, FF.r(), LF.r())
                scan(BB.ap, R64, LF.ap, 0.0, LF.r() + CST.r(), BB.r())
                act(E1.ap, BB.ap, AF.Exp, BB.r(), E1.r())
                act(E2.ap, BB.ap, AF.Exp, BB.r(), E2.r(), scale=-1.0)
                cp(POOL, EL.ap[:, ci, :], E1.ap.rearrange("p (c u) -> p c u", u=64)[:, :, 63], E1.r(), EL.r(ci))
                tt(DVE, QT.ap[:, ci, :], QS.ap[:, ci, :], E1.ap, ALU.mult, QS.r(ci) + E1.r(), QT.r(ci))
                ts(DVE, FF.ap, FF.ap, -1.0, 1.0, ALU.mult, ALU.add, FF.r(), FF.r())
                tt(DVE, KT.ap[:, ci, :], FF.ap, E2.ap, ALU.mult, FF.r() + E2.r(), KT.r(ci))
                tt(POOL, KD.ap.rearrange("p (c u) -> p c u", u=64), KT.ap[:, ci, :].rearrange("p (c u) -> p c u", u=64),
                   EL.ap[:, ci, :].unsqueeze(2).to_broadcast([128, 8, 64]), ALU.mult, KT.r(ci) + EL.r(ci), KD.r())
                yield
                pbt = bank()
                for s in range(4):
                    tr(PSB[pbt][:, s * 128:(s + 1) * 128], KD.ap[:, s * 128:(s + 1) * 128], IDB.ap, KD.r() + IDB.r(), PR[pbt])
                cp(ACT, KDT.ap[:, ci, :], PSB[pbt][:, 0:512], PR[pbt], KDT.r(ci))
            proj_fm(W, 3072 + hf * 512, 4, H, TT, c_f)
            if stop_after == "hgB":
                return

            def head_gen(jj, hf=hf):
                j = hf * 4 + jj
                OF = FA[j % 2][5]
                PT, OSQ, RSh = PTs[j % 2], OSQs[j % 2], RS2[j % 2]
                pbs = bank()
                for s in range(4):
                    sl_ = slice(s * 128, (s + 1) * 128)
                    mm(PS[pbs][:, sl_], KT.ap[:, jj, sl_], QT.ap[:, jj, sl_], s == 0, s == 3, KT.r(jj) + QT.r(jj), PR[pbs])
                tt(DVE, PT.ap, PS[pbs].rearrange("p (s t) -> p s t", s=4), MASKH.unsqueeze(1).to_broadcast([128, 4, 128]),
                   ALU.mult, PR[pbs] + CST.r(), PT.r())
                pd = [bank(), bank()]
                for c in range(8):
                    s, hh = c // 2, c % 2
                    mm(PS[pd[hh]][:, s * 128:(s + 1) * 128],
                       KDT.ap[hh * 64:(hh + 1) * 64, jj, s * 128:(s + 1) * 128],
                       VT.ap[hh * 64:(hh + 1) * 64, s, jj * 128:(jj + 1) * 128], s == 0, s == 3,
                       KDT.r(jj) + VT.r(s), PR[pd[hh]])
                SBj = SB[j % 2]
                cp(POOL, SBj.ap[:, 0, :], ST0.ap[:, j, :], ST0.r(j), SBj.r(0))
                yield
                po = bank()
                for s in range(4):
                    mm(PS[po][:, s * 128:(s + 1) * 128], VT.ap[:, s, jj * 128:(jj + 1) * 128], PT.ap[:, s, :], s == 0, False,
                       VT.r(s) + PT.r(), PR[po])
                for c in range(8):
                    mm(PS[po][:, c * 64:(c + 1) * 64], SBj.ap[:, c, :], QT.ap[:, jj, c * 64:(c + 1) * 64], False, c == 7,
                       SBj.r(c) + QT.r(jj), PR[po])
                    stt(ST0.ap[:, j, :], ST0.ap[:, j, :], EL.ap[:, jj, c:c + 1], PS[pd[c % 2]][:, (c // 2) * 128:(c // 2 + 1) * 128],
                        ALU.mult, ALU.add, ST0.r(j) + EL.r(jj) + PR[pd[c % 2]], ST0.r(j))
                    if c < 7:
                        cp(ACT, SBj.ap[:, c + 1, :], ST0.ap[:, j, :], ST0.r(j), SBj.r(c + 1))
                    yield
                act(OF.ap, PS[po], AF.Copy, PR[po], OF.r())
                act(OSQ.ap, OF.ap, AF.Square, OF.r(), OSQ.r())
                pn = bank()
                mm(PS[pn], ONES.ap, OSQ.ap, True, True, ONES.r() + OSQ.r(), PR[pn])
                yield
                act(RSh.ap, PS[pn], AF.Ln, PR[pn] + EPSC.r(), RSh.r(), bias=EPSC.ap[:, 0:1], scale=1.0 / 128)
                act(RSh.ap, RSh.ap, AF.Exp, RSh.r(), RSh.r(), scale=-0.5)
                stt(OF.ap, OF.ap, pc("bng", j), RSh.ap, ALU.mult, ALU.mult, OF.r() + RSh.r() + PC.r(), OF.r())
                tt(POOL, Y.ap[:, 8 + j, :], OF.ap, GATE2.ap[:, jj, :], ALU.mult, OF.r() + GATE2.r(jj), Y.r(8 + j))
            interleave([head_gen(0), head_gen(1)])
            interleave([head_gen(2), head_gen(3)])
        dump("yb", Y.ap[:, 8:16, :], Y.r(8, 16))
        outproj_add("e_w_out", Y, 16)

    def xattn(l):
        kb.off = pers_off
        QA = alloc([8, TT], BF16)
        EX = [alloc([2, TT], BF16) for _ in range(2)]
        RC = [alloc([TT]) for _ in range(2)]
        SQ = alloc([8, TT], BF16)
        norm_fm(X, TT, "gmq", l * 8, H, SQ)

        def c_q(ci, pb):
            act(QA.ap[:, ci, :], PS[pb], AF.Copy, PR[pb], QA.r(ci), scale=1.0 / 16.0)
        proj_fm(f"xq{l}", 0, 8, H, TT, c_q)
        for hd in range(4):
            EXh, RCh = EX[hd % 2], RC[hd % 2]
            for mh in range(2):
                pb = bank()
                for dc in range(2):
                    mm(PS[pb], KF[l].ap[:, 2 * hd + dc, mh * 128:(mh + 1) * 128], QA.ap[:, 2 * hd + dc, :], dc == 0, dc == 1,
                       KF[l].r(2 * hd + dc) + QA.r(2 * hd + dc), PR[pb])
                act(EXh.ap[:, mh, :], PS[pb], AF.Exp, PR[pb], EXh.r(mh))
            pb = bank()
            for mh in range(2):
                mm(PS[pb], ONES.ap, EXh.ap[:, mh, :], mh == 0, mh == 1, ONES.r() + EXh.r(mh), PR[pb])
            recip(RCh.ap, PS[pb], PR[pb], RCh.r())
            for dc in range(2):
                pb = bank()
                c = 2 * hd + dc
                for mh in range(2):
                    mm(PS[pb], VM[l].ap[:, mh, c * 128:(c + 1) * 128], EXh.ap[:, mh, :], mh == 0, mh == 1,
                       VM[l].r(mh) + EXh.r(mh), PR[pb])
                tt(DVE, Y.ap[:, c, :], PS[pb], RCh.ap, ALU.mult, PR[pb] + RCh.r(), Y.r(c))
        outproj_add(f"xo{l}", Y, 8)

    def ffn(l):
        kb.off = pers_off
        UP = alloc([32, TT], BF16)
        RL = [alloc([TT]) for _ in range(2)]
        SQ = alloc([8, TT], BF16)
        norm_fm(X, TT, "gffn", l * 8, H, SQ)

        def c_up(ci, pb):
            R_ = RL[ci % 2]
            act(R_.ap, PS[pb], AF.Relu, PR[pb], R_.r())
            tt(DVE, UP.ap[:, ci, :], R_.ap, PS[pb], ALU.mult, R_.r() + PR[pb], UP.r(ci))
        proj_fm(f"w1_{l}", 0, 32, H, TT, c_up)
        for c in range(8):
            ws = wload(f"w2_{l}", 0, 32, c * 128, 128)
            pb = bank()
            for k in range(32):
                mm(PS[pb], ws.ap[:, k, :], UP.ap[:, k, :], k == 0, k == 31, ws.regs + UP.r(k), PR[pb])
            tt(DVE, X.ap[:, c, :], X.ap[:, c, :], PS[pb], ALU.add, X.r(c) + PR[pb], X.r(c))

    def l1_mixer(t):
        kb.off = pers_off
        DT = alloc([TT])
        ACU = alloc([TT])
        DTT = alloc([4, 32])
        NAT = alloc([4, 32])
        m1 = kb.off
        SQ = alloc([8, TT], BF16)
        norm_fm(X, TT, "gmix", 8, H, SQ)
        kb.off = m1
        DA = alloc([TT])
        XTM = alloc([4, 1024])
        ZS = alloc([4, 1024], BF16)
        BF_ = alloc([4, TT], BF16)
        CF_ = alloc([4, TT], BF16)
        BT = alloc([4, 512], BF16)
        m2 = kb.off
        XA = [alloc([516]) for _ in range(3)]
        XC = [alloc([TT]) for _ in range(3)]
        kb.off = m2
        WS_ = alloc([16])
        ALB = alloc([16])
        EAL = alloc([16])
        SS = alloc([4])
        LL = [alloc([4, 128]) for _ in range(2)]
        LL2 = [alloc([4, 128]) for _ in range(2)]
        CBM = [alloc([128]) for _ in range(2)]
        MT = [alloc([4, 128], BF16) for _ in range(2)]
        CT = [alloc([4, 128], BF16) for _ in range(2)]
        XDT = alloc([1024], BF16)
        XDW = alloc([1024], BF16)
        STB = alloc([1024], BF16)
        YA = alloc([1024])
        YB = alloc([1024])
        YN = alloc([1024], BF16)
        W = "o_w_in"
        ws = wload(W, 0, 8, 6144, 32)
        pb = bank()
        for k in range(8):
            mm(PS[pb][0:32, :], ws.ap[:, k, :], H.ap[:, k, :], k == 0, k == 7, ws.regs + H.r(k), PR[pb])
        act(DT.ap[0:32, :], PS[pb][0:32, :], AF.Exp, PR[pb] + PC.r(), DT.r(), bias=PC.ap[0:32, PCI["dtb"]:PCI["dtb"] + 1])
        act(DT.ap[0:32, :], DT.ap[0:32, :], AF.Ln, DT.r(), DT.r(), bias=1.0)
        ts(DVE, DA.ap[0:32, :], DT.ap[0:32, :], ANEG.ap[0:32, 0:1], None, ALU.mult, None, DT.r() + ANEG.r(), DA.r())
        scan(ACU.ap[0:32, :], R128[0:32, :], DA.ap[0:32, :], 0.0, DA.r() + CST.r(), ACU.r())
        pb = bank()
        for s in range(4):
            tr(PS[pb][:, s * 32:(s + 1) * 32], DT.ap[0:32, s * 128:(s + 1) * 128], IDF[0:32, 0:32], DT.r() + CST.r(), PR[pb])
        cp(DVE, DTT.ap, PS[pb][:, 0:128].rearrange("p (s h) -> p s h", s=4), PR[pb], DTT.r())
        pb = bank()
        for s in range(4):
            tr(PS[pb][:, s * 32:(s + 1) * 32], ACU.ap[0:32, s * 128:(s + 1) * 128], IDF[0:32, 0:32], ACU.r() + CST.r(), PR[pb])
        ts(DVE, NAT.ap, PS[pb][:, 0:128].rearrange("p (s h) -> p s h", s=4), -1.0, None, ALU.mult, None, PR[pb], NAT.r())

        YBANKS = (0, 1)
        OTH = (2, 3, 4, 5, 6, 7)
        for hf in range(2):
            for sl in range(2):
                ws = wload(W, 0, 8, hf * 1024 + sl * 512, 512)
                for s in range(4):
                    pb = bank()
                    for k in range(8):
                        mm(PS[pb], H.ap[:, k, s * 128:(s + 1) * 128], ws.ap[:, k, :], k == 0, k == 7, ws.regs + H.r(k), PR[pb])
                    act(ZS.ap[:, s, sl * 512:(sl + 1) * 512], PS[pb], AF.Silu, PR[pb],
                        ZS.rb(s * 1024 + sl * 512, s * 1024 + sl * 512 + 512))

            def c_x(ci, pb, hf=hf):
                ch = hf * 8 + ci
                XAb, XCb = XA[ci % 3], XC[ci % 3]
                conv_chunk(pb, XAb, CAR1, ch, "mcw", 32, "mcb", XCb.ap, XCb.r())
                yield
                act(XCb.ap, XCb.ap, AF.Silu, XCb.r(), XCb.r())
                yield
                pbt = bank()
                for s in range(4):
                    tr(PS[pbt][:, s * 128:(s + 1) * 128], XCb.ap[:, s * 128:(s + 1) * 128], IDF, XCb.r() + CST.r(), PR[pbt])
                cp(ACT, XTM.ap[:, :, ci * 128:(ci + 1) * 128], PS[pbt].rearrange("p (s f) -> p s f", s=4), PR[pbt], XTM.r())
            proj_fm(W, 2048 + hf * 1024, 8, H, TT, c_x)

            def c_b(gi, pb, hf=hf):
                ch = 16 + hf * 4 + gi
                XAb, XCb = XA[gi % 3], XC[gi % 3]
                conv_chunk(pb, XAb, CAR1, ch, "mcw", 32, "mcb", XCb.ap, XCb.r())
                yield
                act(BF_.ap[:, gi, :], XCb.ap, AF.Silu, XCb.r(), BF_.r(gi))
                yield
                pbt = bank()
                for s in range(4):
                    tr(PSB[pbt][:, s * 128:(s + 1) * 128], BF_.ap[:, gi, s * 128:(s + 1) * 128], IDB.ap, BF_.r(gi) + IDB.r(), PR[pbt])
                cp(ACT, BT.ap[:, :, gi * 128:(gi + 1) * 128], PSB[pbt][:, 0:512].rearrange("p (s f) -> p s f", s=4), PR[pbt], BT.r())
            proj_fm(W, 4096 + hf * 512, 4, H, TT, c_b)

            def c_c(gi, pb, hf=hf):
                ch = 24 + hf * 4 + gi
                XAb, XCb = XA[gi % 3], XC[gi % 3]
                conv_chunk(pb, XAb, CAR1, ch, "mcw", 32, "mcb", XCb.ap, XCb.r())
                yield
                act(CF_.ap[:, gi, :], XCb.ap, AF.Silu, XCb.r(), CF_.r(gi))
            proj_fm(W, 5120 + hf * 512, 4, H, TT, c_c)

            H0 = hf * 16
            prevB = None
            for s in range(4):
                cs = slice(s * 128, (s + 1) * 128)
                tt(DVE, XDT.ap.rearrange("p (h q) -> p h q", q=64), XTM.ap[:, s, :].rearrange("p (h q) -> p h q", q=64),
                   DTT.ap[:, s, H0:H0 + 16].unsqueeze(2).to_broadcast([128, 16, 64]), ALU.mult, XTM.r(s) + DTT.r(), XDT.r())
                cp(ACT, STB.ap, ST1.ap[:, hf * 1024:(hf + 1) * 1024], ST1.rb(hf * 1024, hf * 1024 + 1024), STB.r())
                def grp_gen(gl, s=s, cs=cs, H0=H0):
                    i2 = gl % 2
                    pcb = bank(OTH)
                    mm(PS[pcb][:, 0:128], BF_.ap[:, gl, cs], CF_.ap[:, gl, cs], True, True, BF_.r(gl) + CF_.r(gl), PR[pcb])
                    tt(DVE, CBM[i2].ap, PS[pcb][:, 0:128], MASKC, ALU.mult, PR[pcb] + CST.r(), CBM[i2].r())
                    pa = bank(OTH)
                    for hh in range(4):
                        h = H0 + 4 * gl + hh
                        mm(PS[pa][:, hh * 128:(hh + 1) * 128], IDF[0:32, h:h + 1].to_broadcast([32, 128]), ACU.ap[0:32, cs],
                           hh == 0, hh == 3, CST.r() + ACU.r(), PR[pa])
                    for hh in range(4):
                        h = H0 + 4 * gl + hh
                        act(LL[i2].ap[:, hh, :], PS[pa][:, hh * 128:(hh + 1) * 128], AF.Exp, PR[pa] + NAT.r(), LL[i2].r(hh),
                            bias=NAT.ap[:, s, h:h + 1])
                    stt(MT[i2].ap, LL[i2].ap, 1.0, CBM[i2].ap.unsqueeze(1).to_broadcast([128, 4, 128]), ALU.min, ALU.mult,
                        LL[i2].r() + CBM[i2].r(), MT[i2].r())
                    pav = PS[pa].rearrange("p (h t) -> p h t", h=4)
                    cp(DVE, ALB.ap[:, 4 * gl:4 * gl + 4], pav[:, :, 127], PR[pa], ALB.r())
                    act(LL2[i2].ap, pav, AF.Exp, PR[pa], LL2[i2].r())
                    tt(DVE, CT[i2].ap, LL2[i2].ap, CF_.ap[:, gl, cs].unsqueeze(1).to_broadcast([128, 4, 128]), ALU.mult,
                       LL2[i2].r() + CF_.r(gl), CT[i2].r())
                    yield
                    for hh in range(4):
                        hl = 4 * gl + hh
                        yb = YBANKS[hl // 8]
                        oc = slice((hl % 8) * 64, (hl % 8 + 1) * 64)
                        mm(PS[yb][:, oc], MT[i2].ap[:, hh, :], XDT.ap[:, hl * 64:(hl + 1) * 64], (hl % 8 == 0), False,
                           MT[i2].r(hh) + XDT.r(), PR[yb])
                        mm(PS[yb][:, oc], CT[i2].ap[:, hh, :], STB.ap[:, hl * 64:(hl + 1) * 64], False, (hl % 8 == 7),
                           CT[i2].r(hh) + STB.r(), PR[yb])
                pend_ = [prevB] if prevB is not None else []
                for gl in range(4):
                    g_ = grp_gen(gl)
                    next(g_)
                    step_all(pend_)
                    pend_.append(g_)
                drain(pend_)
                STh = ST1.ap[:, hf * 1024:(hf + 1) * 1024]
                STr = ST1.rb(hf * 1024, hf * 1024 + 1024)
                act(EAL.ap, ALB.ap, AF.Exp, ALB.r(), EAL.r())
                tt(DVE, WS_.ap, ALB.ap, NAT.ap[:, s, H0:H0 + 16], ALU.add, ALB.r() + NAT.r(), WS_.r())
                act(WS_.ap, WS_.ap, AF.Exp, WS_.r(), WS_.r())
                tt(DVE, WS_.ap, WS_.ap, DTT.ap[:, s, H0:H0 + 16], ALU.mult, WS_.r() + DTT.r(), WS_.r())
                tt(DVE, XDW.ap.rearrange("p (h q) -> p h q", q=64), XTM.ap[:, s, :].rearrange("p (h q) -> p h q", q=64),
                   WS_.ap.unsqueeze(2).to_broadcast([128, 16, 64]), ALU.mult, XTM.r(s) + WS_.r(), XDW.r())
                tt(POOL, STh.rearrange("p (h q) -> p h q", q=64), STh.rearrange("p (h q) -> p h q", q=64),
                   EAL.ap.unsqueeze(2).to_broadcast([128, 16, 64]), ALU.mult, STr + EAL.r(), STr)
                for half in range(2):
                    pd = bank(OTH)
                    for gg in range(2):
                        gl = half * 2 + gg
                        mm(PS[pd][:, gg * 256:(gg + 1) * 256], BT.ap[:, s, gl * 128:(gl + 1) * 128], XDW.ap[:, gl * 256:(gl + 1) * 256],
                           gg == 0, gg == 1, BT.r(s) + XDW.r(), PR[pd])
                    sl_ = slice(half * 512, (half + 1) * 512)
                    tt(DVE, STh[:, sl_], STh[:, sl_], PS[pd], ALU.add, STr + PR[pd], STr)
                tt(POOL, YA.ap.rearrange("p (h q) -> p h q", q=64), XTM.ap[:, s, :].rearrange("p (h q) -> p h q", q=64),
                   DBC.ap[:, H0:H0 + 16].unsqueeze(2).to_broadcast([128, 16, 64]), ALU.mult, XTM.r(s) + DBC.r(), YA.r())
                for q in range(2):
                    sl_ = slice(q * 512, (q + 1) * 512)
                    tt(DVE, YA.ap[:, sl_], YA.ap[:, sl_], PS[YBANKS[q]], ALU.add, YA.r() + PR[YBANKS[q]], YA.r())

                def fin_gen(s=s, cs=cs, hf=hf):
                    tt(POOL, YA.ap, YA.ap, ZS.ap[:, s, :], ALU.mult, YA.r() + ZS.r(s), YA.r())
                    yield
                    for g4 in range(4):
                        op(ACT, lambda g4=g4: nc.scalar.activation(out=YB.ap[:, g4 * 256:(g4 + 1) * 256],
                                                                    in_=YA.ap[:, g4 * 256:(g4 + 1) * 256], func=AF.Square,
                                                                    accum_out=SS.ap[:, g4:g4 + 1]),
                           YA.r(), YB.r() + SS.r())
                    act(SS.ap, SS.ap, AF.Ln, SS.r() + EPSC.r(), SS.r(), bias=EPSC.ap[:, 0:1], scale=1.0 / 256)
                    act(SS.ap, SS.ap, AF.Exp, SS.r(), SS.r(), scale=-0.5)
                    yield
                    for g4 in range(4):
                        act(YA.ap[:, g4 * 256:(g4 + 1) * 256], YA.ap[:, g4 * 256:(g4 + 1) * 256], AF.Copy, YA.r() + SS.r(), YA.r(),
                            scale=SS.ap[:, g4:g4 + 1])
                    tt(POOL, YN.ap, YA.ap, NG.ap[:, hf * 1024:(hf + 1) * 1024], ALU.mult, YA.r() + NG.r(), YN.r())
                    yield
                    for q in range(2):
                        pbt = bank(OTH)
                        for cc in range(4):
                            c = q * 4 + cc
                            tr(PSB[pbt][:, cc * 128:(cc + 1) * 128], YN.ap[:, c * 128:(c + 1) * 128], IDB.ap, YN.r() + IDB.r(), PR[pbt])
                        c0 = hf * 8 + q * 4
                        cp(ACT, Y.ap[:, c0:c0 + 4, cs], PSB[pbt][:, 0:512].rearrange("p (c t) -> p c t", c=4), PR[pbt],
                           Y.r(c0, c0 + 4))
                        yield
                prevB = fin_gen()
            drain([prevB])
        dump("ym", Y.ap, Y.r())
        outproj_add("o_w_out", Y, 16)

    def final(t):
        kb.off = pers_off
        HF = alloc([8, TT])
        IO = alloc([4, 1024])
        SQ = alloc([8, TT], BF16)
        norm_fm(X, TT, "gfin", 0, HF, SQ)
        for s in range(4):
            for hf in range(2):
                pb = bank()
                for cc in range(4):
                    c = hf * 4 + cc
                    tr(PS[pb][:, cc * 128:(cc + 1) * 128], HF.ap[:, c, s * 128:(s + 1) * 128], IDF, HF.r(c) + CST.r(), PR[pb])
                cp(ACT if hf else DVE, IO.ap[:, s, hf * 512:(hf + 1) * 512], PS[pb], PR[pb], IO.r(s))
        dma(POOL, T_OUT, out_d[t * TT:(t + 1) * TT, :].rearrange("(s p) f -> p s f", p=128), IO.ap, IO.r(), [])

    stages = ["l0", "a0", "f0", "l1", "a1", "f1"]
    for t in range(NT):
        load_x(t)
        for st in stages:
            if stop_after == "load":
                break
            if st == "l0":
                l0_mixer(t)
                if t == 0:
                    cast_weights(["w1_0", "w2_0", "o_w_in", "o_w_out"], 2)
            elif st == "a0":
                if t == 0:
                    mem_kv()
                xattn(0)
            elif st == "f0":
                ffn(0)
            elif st == "l1":
                l1_mixer(t)
                if t == 0:
                    cast_weights(["xq1", "xo1", "w1_1", "w2_1"], 2)
            elif st == "a1":
                xattn(1)
            elif st == "f1":
                ffn(1)
            if stop_after == st or (stop_after in ("rglru", "hgA", "hgB") and st == "l0"):
                break
        final(t)
    nc.gpsimd.wait_ge(T_OUT.sem, T_OUT.n * 16)
    for t_ in kb.dbgt:
        nc.gpsimd.wait_ge(t_.sem, t_.n * 16)
    es.close()
    kb.counts = counts
    kb.wneed = wneed
    return nc, kb


def _cols(v):
    v = np.asarray(v, np.float32).reshape(-1)
    return np.ascontiguousarray(v.reshape(v.size // 128, 128).T)


def make_shared_inputs(inp):
    pc = np.zeros((128, NPC), np.float32)

    def put(name, arr):
        a = _cols(arr)
        pc[:, PCI[name]:PCI[name] + a.shape[1]] = a
    put("gmix", inp["norm_mix_g"])
    put("gmq", inp["norm_mem_q_g"])
    put("gmkv", inp["norm_mem_kv_g"])
    put("gffn", inp["norm_ffn_g"])
    put("gfin", inp["final_norm_g"])
    put("acw", inp["a_conv_w"][0])
    put("acb", inp["a_conv_b"][0])
    put("arb", inp["a_gate_r_b"][0])
    put("aib", inp["a_gate_i_b"][0])
    put("alam", inp["a_lambda"][0])
    put("lbl", inp["b_lb_logits"])
    put("bng", inp["b_norm_g"][0])
    put("mcw", inp["m_conv_w"][0])
    put("mcb", inp["m_conv_b"][0])
    pc[:32, PCI["dtb"]] = np.asarray(inp["m_dt_bias"][0], np.float32)
    pc[:32, PCI["alog"]] = np.asarray(inp["m_a_log"][0], np.float32)
    cst = np.zeros((128, NCST), np.float32)
    cst[:, C_ID:C_ID + 128] = np.eye(128, dtype=np.float32)
    s_ = np.arange(128)[:, None]
    t_ = np.arange(128)[None, :]
    cst[:, C_MH:C_MH + 128] = ((s_ <= t_) & ((s_ // 64) == (t_ // 64))).astype(np.float32)
    cst[:, C_MC:C_MC + 128] = (s_ <= t_).astype(np.float32)
    cst[:, C_R64:C_R64 + 512] = (np.arange(512) % 64 != 0).astype(np.float32)[None, :]
    cst[:, C_R128:C_R128 + 512] = (np.arange(512) % 128 != 0).astype(np.float32)[None, :]
    sh = {"pc": pc, "cst": cst}
    sh["gr"] = np.ascontiguousarray(np.asarray(inp["a_gate_r_w"][0], np.float32).transpose(1, 0, 2).reshape(128, 1024))
    sh["gi"] = np.ascontiguousarray(np.asarray(inp["a_gate_i_w"][0], np.float32).transpose(1, 0, 2).reshape(128, 1024))
    sh["mng"] = np.asarray(inp["m_norm_g"], np.float32).reshape(1, 2048)
    sh["md"] = np.repeat(np.asarray(inp["m_d"], np.float32).reshape(1, 32), 1, axis=0)
    sh["e_w_in"] = np.asarray(inp["e_w_in"][0], np.float32)
    sh["e_w_out"] = np.asarray(inp["e_w_out"][0], np.float32)
    sh["o_w_in"] = np.asarray(inp["o_w_in"][0], np.float32)
    sh["o_w_out"] = np.asarray(inp["o_w_out"][0], np.float32)
    for l in range(2):
        sh[f"xq{l}"] = np.asarray(inp["xq_w"][l], np.float32)
        sh[f"xk{l}"] = np.asarray(inp["xk_w"][l], np.float32)
        sh[f"xv{l}"] = np.asarray(inp["xv_w"][l], np.float32)
        sh[f"xo{l}"] = np.asarray(inp["xo_w"][l], np.float32)
        sh[f"w1_{l}"] = np.asarray(inp["ffn_w1"][l], np.float32)
        sh[f"w2_{l}"] = np.asarray(inp["ffn_w2"][l], np.float32)
    return sh


_CACHE = {}


def kernel(**inputs):
    x = np.asarray(inputs["x"], np.float32)
    mem = np.asarray(inputs["mem"], np.float32)
    B, S, _ = x.shape
    if S not in _CACHE:
        _CACHE[S] = build(S)[0]
    nc = _CACHE[S]
    sh = make_shared_inputs(inputs)
    in_maps = []
    for b in range(B):
        m = dict(sh)
        m["x"] = np.ascontiguousarray(x[b])
        m["mem"] = np.ascontiguousarray(mem[b])
        in_maps.append(m)
    res = run_bass_kernel_spmd(nc, in_maps, core_ids=list(range(B)))
    return np.stack([np.asarray(r["out"], np.float32) for r in res.results], axis=0)
```

```python
import numpy as np
from contextlib import ExitStack
import concourse.bass as bass
import concourse.mybir as mybir
from concourse.bass_utils import run_bass_kernel_spmd

F32 = mybir.dt.float32
BF16 = mybir.dt.bfloat16
AF = mybir.ActivationFunctionType
ALU = mybir.AluOpType
D = 1024
TT = 512
EPS = 1e-6
MEM = 256

PCI = {}
_n = 0
for _nm, _c in [("gmix", 16), ("gmq", 16), ("gmkv", 16), ("gffn", 16), ("gfin", 8), ("acw", 32), ("acb", 8),
                ("arb", 8), ("aib", 8), ("alam", 8), ("lbl", 24), ("bng", 8), ("mcw", 128), ("mcb", 32),
                ("dtb", 1), ("alog", 1)]:
    PCI[_nm] = _n
    _n += _c
NPC = _n
C_ID, C_MH, C_MC, C_R64, C_R128 = 0, 128, 256, 384, 896
NCST = 1408

WSHAPES = {"e_w_in": (1024, 6144), "e_w_out": (2048, 1024), "o_w_in": (1024, 6176), "o_w_out": (2048, 1024),
           "xq0": (1024, 1024), "xk0": (1024, 1024), "xv0": (1024, 1024), "xo0": (1024, 1024),
           "xq1": (1024, 1024), "xk1": (1024, 1024), "xv1": (1024, 1024), "xo1": (1024, 1024),
           "w1_0": (1024, 4096), "w2_0": (4096, 1024), "w1_1": (1024, 4096), "w2_1": (4096, 1024)}
WORDER = ["e_w_in", "e_w_out", "xk0", "xv0", "xk1", "xv1", "xq0", "xo0", "w1_0", "w2_0",
          "o_w_in", "o_w_out", "xq1", "xo1", "w1_1", "w2_1"]


class Reg:
    __slots__ = ("w", "r")

    def __init__(self):
        self.w = None
        self.r = {}


class Trk:
    def __init__(self, sem, inc):
        self.sem = sem
        self.inc = inc
        self.n = 0


class Eng:
    def __init__(self, eng, trk, is_pe=False):
        self.eng = eng
        self.trk = trk
        self.seen = {}
        self.is_pe = is_pe


class View:
    def __init__(self, kb, b0, dtype, shape):
        self.kb = kb
        self.b0 = b0
        self.es = 4 if dtype == F32 else 2
        self.shape = tuple(shape)
        n = int(np.prod(shape))
        self.nbytes = n * self.es
        base = kb.arena[:, b0 // 4:(b0 + self.nbytes) // 4]
        ap = base if dtype == F32 else base.bitcast(BF16)
        if len(shape) == 2:
            ap = ap.rearrange("p (a b) -> p a b", a=shape[0])
        elif len(shape) == 3:
            ap = ap.rearrange("p (a b c) -> p a b c", a=shape[0], b=shape[1])
        self.ap = ap
        self.slot = self.nbytes // shape[0] if len(shape) > 1 else self.nbytes

    def r(self, i=None, j=None):
        G = self.kb.G
        if i is None:
            lo, hi = self.b0, self.b0 + self.nbytes
        else:
            if j is None:
                j = i + 1
            lo, hi = self.b0 + i * self.slot, self.b0 + j * self.slot
        return self.kb.regs[lo // G:(hi + G - 1) // G]

    def rb(self, lo_el, hi_el):
        G = self.kb.G
        lo, hi = self.b0 + lo_el * self.es, self.b0 + hi_el * self.es
        return self.kb.regs[lo // G:(hi + G - 1) // G]


def build(S, stop_after=None, dbg=None):
    NT = S // TT
    nc = bass.Bass("TRN2", target_bir_lowering=False)
    kb = type("KB", (), {})()
    es = ExitStack()
    E_ = es.enter_context

    def din(name, shape, dt=F32):
        return nc.dram_tensor(name, list(shape), dt, kind="ExternalInput").ap()

    x_d = din("x", [S, D])
    mem_d = din("mem", [MEM, D])
    pc_d = din("pc", [128, NPC])
    cst_d = din("cst", [128, NCST])
    gr_d = din("gr", [128, 1024])
    gi_d = din("gi", [128, 1024])
    mng_d = din("mng", [1, 2048])
    md_d = din("md", [1, 32])
    _stage_w = {"load": [], "rglru": ["e_w_in"], "hgA": [], "hgB": [], "l0": ["e_w_in", "e_w_out"], "a0": ["xq0", "xo0"], "f0": ["w1_0", "w2_0"],
                "l1": ["o_w_in", "o_w_out"], "a1": ["xq1", "xo1"], "f1": ["w1_1", "w2_1"]}
    wneed = ["xk0", "xv0", "xk1", "xv1"]
    for _st in ["load", "rglru", "hgA", "hgB", "l0", "a0", "f0", "l1", "a1", "f1"]:
        wneed += _stage_w[_st]
        if stop_after == _st:
            break
    wneed = [k for k in WORDER if k in set(wneed)]
    wf = {k: din(k, WSHAPES[k]) for k in wneed}
    wb = {k: nc.dram_tensor(k + "_b", list(WSHAPES[k]), BF16, kind="Internal").ap() for k in wneed}
    out_d = nc.dram_tensor("out", [S, D], F32, kind="ExternalOutput").ap()
    dbg_d = {}
    if dbg:
        for nm, shp in dbg.items():
            dbg_d[nm] = nc.dram_tensor("dbg_" + nm, list(shp), F32, kind="ExternalOutput").ap()

    ARENA_BYTES = 206 * 1024
    kb.G = 512
    kb.arena = E_(nc.sbuf_tensor("arena", [128, ARENA_BYTES // 4], F32))[:]
    kb.regs = [Reg() for _ in range(ARENA_BYTES // kb.G)]
    kb.off = 0
    kb.dbgt = []

    def alloc(shape, dtype=F32):
        v = View(kb, kb.off, dtype, shape)
        kb.off += (v.nbytes + kb.G - 1) // kb.G * kb.G
        assert kb.off <= ARENA_BYTES, f"SBUF arena overflow {kb.off}"
        return v

    def mksem(name):
        return E_(nc.semaphore(name))

    PE = Eng(nc.tensor, Trk(mksem("s_pe"), 1), is_pe=True)
    ACT = Eng(nc.scalar, Trk(mksem("s_act"), 1))
    DVE = Eng(nc.vector, Trk(mksem("s_dve"), 1))
    POOL = Eng(nc.gpsimd, Trk(mksem("s_pool"), 1))
    SP = Eng(nc.sync, None)
    kb.nsem = 0

    def newtrk():
        kb.nsem += 1
        return Trk(mksem(f"s_dma{kb.nsem}"), 16)
    NCAST = 6
    T_CAST = [newtrk() for _ in range(NCAST)]
    T_XIN = newtrk()
    T_OUT = newtrk()
    counts = {"wait": 0, "ins": 0}

    def op(E, fn, r=(), w=(), trk=None):
        trk = trk or E.trk
        need = {}
        for g in r:
            if g.w is not None:
                t, c = g.w
                if need.get(t, 0) < c:
                    need[t] = c
        for g in w:
            if g.w is not None:
                t, c = g.w
                if need.get(t, 0) < c:
                    need[t] = c
            for t, c in g.r.items():
                if need.get(t, 0) < c:
                    need[t] = c
        for t, c in need.items():
            if E.is_pe and t is E.trk:
                continue
            if E.seen.get(t, 0) < c:
                E.eng.wait_ge(t.sem, c * t.inc)
                E.seen[t] = c
                counts["wait"] += 1
        ins = fn()
        trk.n += 1
        ins.then_inc(trk.sem, trk.inc)
        counts["ins"] += 1
        n = trk.n
        for g in r:
            if g.r.get(trk, 0) < n:
                g.r[trk] = n
        for g in w:
            g.w = (trk, n)
            g.r = {}

    PSt = [E_(nc.psum_tensor(f"ps{i}", [128, 512], F32)) for i in range(8)]
    PS = [t[:] for t in PSt]
    PSB = [t[:].bitcast(BF16) for t in PSt]
    PR = [[Reg()] for _ in range(8)]
    kb.pb = 0

    def bank(pool=(0, 1, 2, 3, 4, 5, 6, 7)):
        kb.pb = (kb.pb + 1) % len(pool)
        return pool[kb.pb]

    def mm(out, lhsT, rhs, start, stop, r, w):
        op(PE, lambda: nc.tensor.matmul(out, lhsT=lhsT, rhs=rhs, start=start, stop=stop), r, w)

    def tr(out, in_, ident, r, w):
        op(PE, lambda: nc.tensor.transpose(out=out, in_=in_, identity=ident), r, w)

    def act(out, in_, func, r, w, bias=None, scale=None):
        kw = {}
        if bias is not None:
            kw["bias"] = bias
        if scale is not None:
            kw["scale"] = scale
        op(ACT, lambda: nc.scalar.activation(out=out, in_=in_, func=func, **kw), r, w)

    def ts(E, out, in0, s1, s2, op0, op1, r, w):
        if s2 is None:
            op(E, lambda: E.eng.tensor_scalar(out=out, in0=in0, scalar1=s1, scalar2=None, op0=op0), r, w)
        else:
            op(E, lambda: E.eng.tensor_scalar(out=out, in0=in0, scalar1=s1, scalar2=s2, op0=op0, op1=op1), r, w)

    def tt(E, out, in0, in1, o, r, w):
        op(E, lambda: E.eng.tensor_tensor(out=out, in0=in0, in1=in1, op=o), r, w)

    def stt(out, in0, scalar, in1, op0, op1, r, w):
        op(DVE, lambda: nc.vector.scalar_tensor_tensor(out=out, in0=in0, scalar=scalar, in1=in1, op0=op0, op1=op1), r, w)

    def scan(out, d0, d1, initial, r, w):
        op(DVE, lambda: nc.vector.tensor_tensor_scan(out=out, data0=d0, data1=d1, initial=initial,
                                                     op0=ALU.mult, op1=ALU.add), r, w)

    def cp(E, out, in_, r, w):
        if E is ACT:
            op(ACT, lambda: nc.scalar.activation(out=out, in_=in_, func=AF.Copy), r, w)
        else:
            op(E, lambda: E.eng.tensor_copy(out=out, in_=in_), r, w)

    def recip(out, in_, r, w):
        op(DVE, lambda: nc.vector.reciprocal(out=out, in_=in_), r, w)

    def dma(E, trk, out, in_, r, w):
        op(E, lambda: E.eng.dma_start(out=out, in_=in_), r, w, trk=trk)

    WBR = {k: [Reg() for _ in range(v[0] // 128)] for k, v in WSHAPES.items()}

    PC = alloc([NPC])
    CST = alloc([NCST])
    IDB = alloc([128], BF16)
    ONES = alloc([128], BF16)
    EPSC = alloc([1])
    NG = alloc([2048])
    DBC = alloc([32])
    GR = alloc([8, 128], BF16)
    GI = alloc([8, 128], BF16)
    CL = alloc([8])
    CL2 = alloc([8])
    LB = alloc([8])
    OML = alloc([8])
    ANEG = alloc([1])
    NB = alloc([16])
    KF = [alloc([8, MEM], BF16) for _ in range(2)]
    VM = [alloc([2, 1024], BF16) for _ in range(2)]
    X = alloc([8, TT])
    H = alloc([8, TT], BF16)
    Y = alloc([16, TT], BF16)
    RS = alloc([TT])
    CAR0 = alloc([8, 4])
    HST = alloc([8])
    ST0 = alloc([8, 128])
    CAR1 = alloc([32, 4])
    ST1 = alloc([2048])
    NRING = 4
    RING = [alloc([8 * 512], BF16) for _ in range(NRING)]
    T_RING = [newtrk() for _ in range(NRING)]
    kb.ring = 0
    pers_off = kb.off

    def pc(name, i):
        c = PCI[name] + i
        return PC.ap[:, c:c + 1]

    IDF = CST.ap[:, C_ID:C_ID + 128]
    MASKH = CST.ap[:, C_MH:C_MH + 128]
    MASKC = CST.ap[:, C_MC:C_MC + 128]
    R64 = CST.ap[:, C_R64:C_R64 + 512]
    R128 = CST.ap[:, C_R128:C_R128 + 512]

    class Slab:
        pass

    def wload(name, k0, kc, c0, ncols):
        assert kc * ncols * 2 <= 8192
        v = RING[kb.ring]
        trk_ = T_RING[kb.ring]
        kb.ring = (kb.ring + 1) % NRING
        s = Slab()
        base = v.ap[:, 0:kc * ncols]
        s.ap = base.rearrange("p (k n) -> p k n", k=kc)
        s.regs = v.rb(0, kc * ncols)
        src = wb[name][k0 * 128:(k0 + kc) * 128, c0:c0 + ncols].rearrange("(k p) n -> p k n", p=128)
        dma(SP, trk_, s.ap, src, r=WBR[name][k0:k0 + kc], w=s.regs)
        return s

    dma(SP, newtrk(), PC.ap, pc_d[:, :], [], PC.r())
    dma(SP, newtrk(), CST.ap, cst_d[:, :], [], CST.r())
    dma(SP, newtrk(), NG.ap, mng_d.partition_broadcast(128), [], NG.r())
    dma(SP, newtrk(), DBC.ap, md_d.partition_broadcast(128), [], DBC.r())
    kb.ncast = 0

    def cast_weights(names, nfl=NCAST):
        for name in names:
            if name not in wneed:
                continue
            K_, N_ = WSHAPES[name]
            nb_ = 1
            for kb_ in range(0, K_ // 128, nb_):
                tc_ = T_CAST[kb.ncast % nfl]
                kb.ncast += 1
                if tc_.n > 0:
                    nc.gpsimd.wait_ge(tc_.sem, tc_.n * 16)
                dst_ = wb[name][kb_ * 128:(kb_ + nb_) * 128, :]
                src_ = wf[name][kb_ * 128:(kb_ + nb_) * 128, :]
                if N_ < 6144:
                    rr_ = 8192 // N_
                    dst_ = dst_.rearrange("(a b) n -> a (b n)", b=rr_)
                    src_ = src_.rearrange("(a b) n -> a (b n)", b=rr_)
                dma(POOL, tc_, dst_, src_, [], WBR[name][kb_:kb_ + nb_])
    cast_weights(WORDER)

    m0 = kb.off
    TMPF = alloc([1024])
    dma(SP, newtrk(), TMPF.ap, gr_d[:, :], [], TMPF.r())
    cp(DVE, GR.ap, TMPF.ap.rearrange("p (a b) -> p a b", a=8), TMPF.r(), GR.r())
    TMPG = alloc([1024])
    dma(SP, newtrk(), TMPG.ap, gi_d[:, :], [], TMPG.r())
    cp(DVE, GI.ap, TMPG.ap.rearrange("p (a b) -> p a b", a=8), TMPG.r(), GI.r())
    cp(DVE, IDB.ap, IDF, CST.r(), IDB.r())
    op(DVE, lambda: nc.vector.memset(ONES.ap, 1.0), [], ONES.r())
    op(DVE, lambda: nc.vector.memset(EPSC.ap, EPS), [], EPSC.r())
    for v in (CAR0, HST, ST0, CAR1, ST1):
        op(DVE, lambda v=v: nc.vector.memset(v.ap, 0.0), [], v.r())
    T8 = alloc([24])
    act(T8.ap[:, 0:8], PC.ap[:, PCI["alam"]:PCI["alam"] + 8], AF.Exp, PC.r(), T8.r(), scale=-1.0)
    act(T8.ap[:, 0:8], T8.ap[:, 0:8], AF.Ln, T8.r(), T8.r(), bias=1.0)
    ts(DVE, CL.ap, T8.ap[:, 0:8], -8.0, None, ALU.mult, None, T8.r(), CL.r())
    ts(DVE, CL2.ap, T8.ap[:, 0:8], -16.0, None, ALU.mult, None, T8.r(), CL2.r())
    act(T8.ap, PC.ap[:, PCI["lbl"]:PCI["lbl"] + 24], AF.Exp, PC.r(), T8.r())
    tt(DVE, LB.ap, T8.ap[:, 0:8], T8.ap[:, 8:16], ALU.add, T8.r(), LB.r())
    tt(DVE, LB.ap, LB.ap, T8.ap[:, 16:24], ALU.add, T8.r() + LB.r(), LB.r())
    recip(LB.ap, LB.ap, LB.r(), LB.r())
    tt(DVE, LB.ap, LB.ap, T8.ap[:, 0:8], ALU.mult, LB.r() + T8.r(), LB.r())
    ts(DVE, OML.ap, LB.ap, -1.0, 1.0, ALU.mult, ALU.add, LB.r(), OML.r())
    ts(DVE, NB.ap, PC.ap[:, PCI["arb"]:PCI["arb"] + 16], -1.0, None, ALU.mult, None, PC.r(), NB.r())
    act(ANEG.ap, pc("alog", 0), AF.Exp, PC.r(), ANEG.r())
    ts(DVE, ANEG.ap, ANEG.ap, -1.0, None, ALU.mult, None, ANEG.r(), ANEG.r())

    def norm_fm(Xv, n, gname, goff, out, tmp_sq):
        for c in range(8):
            act(tmp_sq.ap[:, c, :n], Xv.ap[:, c, :n], AF.Square, Xv.r(c), tmp_sq.r(c))
        pb = bank()
        for c in range(8):
            mm(PS[pb][:, :n], ONES.ap, tmp_sq.ap[:, c, :n], c == 0, c == 7, tmp_sq.r(c) + ONES.r(), PR[pb])
        act(RS.ap[:, :n], PS[pb][:, :n], AF.Ln, PR[pb] + EPSC.r(), RS.r(), bias=EPSC.ap[:, 0:1], scale=1.0 / D)
        act(RS.ap[:, :n], RS.ap[:, :n], AF.Exp, RS.r(), RS.r(), scale=-0.5)
        for c in range(8):
            stt(out.ap[:, c, :n], Xv.ap[:, c, :n], pc(gname, goff + c), RS.ap[:, :n], ALU.mult, ALU.mult,
                Xv.r(c) + RS.r() + PC.r(), out.r(c))

    def step_all(pend):
        for g_ in list(pend):
            try:
                next(g_)
            except StopIteration:
                pend.remove(g_)

    def drain(pend):
        while pend:
            step_all(pend)

    def interleave(gens):
        pend = []
        for g_ in gens:
            pend.append(g_)
        drain(pend)

    def proj_fm(name, c0, nchunks, Hv, n, consumer, kc=8, k0=0, pend=None):
        own = pend is None
        if own:
            pend = []
        done = 0
        while done < nchunks:
            g = min(4, nchunks - done)
            ws = wload(name, k0, kc, c0 + done * 128, g * 128)
            for cc in range(g):
                pb = bank()
                for k in range(kc):
                    mm(PS[pb][:, :n], ws.ap[:, k, cc * 128:(cc + 1) * 128], Hv.ap[:, k, :n], k == 0, k == kc - 1,
                       ws.regs + Hv.r(k), PR[pb])
                gen_ = consumer(done + cc, pb)
                started = None
                if gen_ is not None:
                    try:
                        next(gen_)
                        started = gen_
                    except StopIteration:
                        pass
                step_all(pend)
                if started is not None:
                    pend.append(started)
            done += g
        if own:
            drain(pend)

    def outproj_add(name, Yv, kc):
        ncol = 8192 // (kc * 2)
        per = ncol // 128
        for g0 in range(0, 8, per):
            ws = wload(name, 0, kc, g0 * 128, ncol)
            for cc in range(per):
                c = g0 + cc
                pb = bank()
                for k in range(kc):
                    mm(PS[pb], ws.ap[:, k, cc * 128:(cc + 1) * 128], Yv.ap[:, k, :], k == 0, k == kc - 1,
                       ws.regs + Yv.r(k), PR[pb])
                tt(DVE, X.ap[:, c, :], X.ap[:, c, :], PS[pb], ALU.add, X.r(c) + PR[pb], X.r(c))

    def dump(nm, view_ap, regs):
        if nm in dbg_d:
            kb.dbgt.append(newtrk())
            dma(POOL, kb.dbgt[-1], dbg_d[nm], view_ap, regs, [])

    def mem_kv():
        kb.off = pers_off
        MIN = alloc([2, 1024])
        MX = alloc([8, MEM])
        MH = alloc([8, MEM], BF16)
        MSQ = alloc([8, MEM], BF16)
        dma(SP, newtrk(), MIN.ap, mem_d.rearrange("(s p) f -> p s f", p=128), [], MIN.r())
        for c in range(8):
            pb = bank()
            for s in range(2):
                tr(PS[pb][:, s * 128:(s + 1) * 128], MIN.ap[:, s, c * 128:(c + 1) * 128], IDF, MIN.r(s) + CST.r(), PR[pb])
            cp(ACT, MX.ap[:, c, :], PS[pb][:, 0:MEM], PR[pb], MX.r(c))
        for l in range(2):
            norm_fm(MX, MEM, "gmkv", l * 8, MH, MSQ)

            def kcons(ci, pb, l=l):
                cp(ACT, KF[l].ap[:, ci, :], PS[pb][:, :MEM], PR[pb], KF[l].r(ci))
            proj_fm(f"xk{l}", 0, 8, MH, MEM, kcons)
            for sl in range(2):
                ws = wload(f"xv{l}", 0, 8, sl * 512, 512)
                for mh in range(2):
                    pb = bank()
                    for k in range(8):
                        mm(PS[pb], MH.ap[:, k, mh * 128:(mh + 1) * 128], ws.ap[:, k, :], k == 0, k == 7,
                           ws.regs + MH.r(k), PR[pb])
                    cp(DVE, VM[l].ap[:, mh, sl * 512:(sl + 1) * 512], PS[pb], PR[pb], VM[l].rb(mh * 1024 + sl * 512, mh * 1024 + sl * 512 + 512))


    def load_x(t):
        kb.off = ARENA_BYTES - 16 * 1024
        IO = alloc([4, 1024])
        dma(SP, T_XIN, IO.ap, x_d[t * TT:(t + 1) * TT, :].rearrange("(s p) f -> p s f", p=128), [], IO.r())
        for c in range(8):
            pb = bank()
            for s in range(4):
                tr(PS[pb][:, s * 128:(s + 1) * 128], IO.ap[:, s, c * 128:(c + 1) * 128], IDF, IO.r(s) + CST.r(), PR[pb])
            cp(ACT if c % 2 else DVE, X.ap[:, c, :], PS[pb], PR[pb], X.r(c))

    def conv_chunk(pb, XAv, CARv, j, wname, wstride, bname, XCv_ap, XC_regs, TMPv=None):
        cp(POOL, XAv.ap[:, 0:3], CARv.ap[:, j, 0:3], CARv.r(j), XAv.r())
        cp(ACT, XAv.ap[:, 3:515], PS[pb], PR[pb], XAv.r())
        cp(POOL, CARv.ap[:, j, 0:3], XAv.ap[:, 512:515], XAv.r(), CARv.r(j))
        act(XCv_ap, PS[pb], AF.Identity, PR[pb] + PC.r(), XC_regs, bias=pc(bname, j), scale=pc(wname, 3 * wstride + j))
        for k in range(3):
            stt(XCv_ap, XAv.ap[:, k:k + 512], pc(wname, k * wstride + j), XCv_ap, ALU.mult, ALU.add,
                XAv.r() + XC_regs + PC.r(), XC_regs)

    def l0_mixer(t):
        kb.off = pers_off
        SQ = alloc([8, TT], BF16)
        norm_fm(X, TT, "gmix", 0, H, SQ)
        kb.off = pers_off
        FA = [[alloc([TT]) for _ in range(6)] for _ in range(2)]
        GATE = alloc([4, TT], BF16)
        GATE2 = alloc([4, TT], BF16)
        QS = alloc([4, TT], BF16)
        QT = alloc([4, TT], BF16)
        KT = alloc([4, TT], BF16)
        KDT = alloc([4, TT], BF16)
        VT = alloc([4, 512], BF16)
        XA = [alloc([516]) for _ in range(2)]
        XCBs = [alloc([TT], BF16) for _ in range(2)]
        KDs = [alloc([TT], BF16) for _ in range(2)]
        PTs = [alloc([4, 128], BF16) for _ in range(2)]
        SB = [alloc([8, 128], BF16) for _ in range(2)]
        EL = alloc([4, 8])
        OSQs = [alloc([TT], BF16) for _ in range(2)]
        RS2 = [alloc([TT]) for _ in range(2)]
        W = "e_w_in"

        for hf in range(2):
            def c_ga(ci, pb):
                act(GATE.ap[:, ci, :], PS[pb], AF.Gelu, PR[pb], GATE.r(ci))
            proj_fm(W, 1024 + hf * 512, 4, H, TT, c_ga)

            def c_xa(ci, pb, hf=hf):
                j = hf * 4 + ci
                XAb = XA[j % 2]
                XC, RR, II, AA, A2, HS = FA[j % 2]
                XCB = XCBs[j % 2]
                conv_chunk(pb, XAb, CAR0, j, "acw", 8, "acb", XC.ap, XC.r())
                cp(POOL, XCB.ap, XC.ap, XC.r(), XCB.r())
                yield
                p1 = bank()
                mm(PS[p1], GR.ap[:, j, :], XCB.ap, True, True, GR.r() + XCB.r(), PR[p1])
                act(RR.ap, PS[p1], AF.Exp, PR[p1] + NB.r(), RR.r(), bias=NB.ap[:, j:j + 1], scale=-1.0)
                act(RR.ap, RR.ap, AF.Ln, RR.r(), RR.r(), bias=1.0)
                act(RR.ap, RR.ap, AF.Exp, RR.r(), RR.r(), scale=-1.0)
                p2 = bank()
                mm(PS[p2], GI.ap[:, j, :], XCB.ap, True, True, GI.r() + XCB.r(), PR[p2])
                act(II.ap, PS[p2], AF.Exp, PR[p2] + NB.r(), II.r(), bias=NB.ap[:, 8 + j:9 + j], scale=-1.0)
                ts(DVE, II.ap, II.ap, 1.0, None, ALU.add, None, II.r(), II.r())
                recip(II.ap, II.ap, II.r(), II.r())
                act(AA.ap, RR.ap, AF.Exp, RR.r() + CL.r(), AA.r(), scale=CL.ap[:, j:j + 1])
                act(A2.ap, RR.ap, AF.Exp, RR.r() + CL2.r(), A2.r(), scale=CL2.ap[:, j:j + 1])
                act(A2.ap, A2.ap, AF.Ln, A2.r(), A2.r(), bias=1.0, scale=-1.0)
                act(A2.ap, A2.ap, AF.Exp, A2.r(), A2.r(), scale=0.5)
                tt(DVE, II.ap, II.ap, XC.ap, ALU.mult, II.r() + XC.r(), II.r())
                tt(DVE, II.ap, II.ap, A2.ap, ALU.mult, II.r() + A2.r(), II.r())
                scan(HS.ap, AA.ap, II.ap, HST.ap[:, j:j + 1], AA.r() + II.r() + HST.r(), HS.r())
                cp(POOL, HST.ap[:, j:j + 1], HS.ap[:, 511:512], HS.r(), HST.r())
                tt(POOL, Y.ap[:, j, :], HS.ap, GATE.ap[:, ci, :], ALU.mult, HS.r() + GATE.r(ci), Y.r(j))
            proj_fm(W, hf * 512, 4, H, TT, c_xa)
        dump("ya", Y.ap[:, 0:8, :], Y.r(0, 8))
        if stop_after == "rglru":
            return

        for hf in range(2):
            def c_g(ci, pb):
                act(GATE2.ap[:, ci, :], PS[pb], AF.Silu, PR[pb], GATE2.r(ci))
            proj_fm(W, 5120 + hf * 512, 4, H, TT, c_g)

            def c_q(ci, pb):
                act(QS.ap[:, ci, :], PS[pb], AF.Silu, PR[pb], QS.r(ci))
            proj_fm(W, 2048 + hf * 512, 4, H, TT, c_q)
            ws = wload(W, 0, 8, 4096 + hf * 512, 512)
            for s in range(4):
                pb = bank()
                for k in range(8):
                    mm(PS[pb], H.ap[:, k, s * 128:(s + 1) * 128], ws.ap[:, k, :], k == 0, k == 7, ws.regs + H.r(k), PR[pb])
                cp(ACT if s % 2 else DVE, VT.ap[:, s, :], PS[pb], PR[pb], VT.r(s))
            if stop_after == "hgA":
                return

            def c_f(ci, pb, hf=hf):
                j = hf * 4 + ci
                FF, LF, BB, E1, E2, _u = FA[j % 2]
                KD = KDs[j % 2]
                act(FF.ap, PS[pb], AF.Exp, PR[pb], FF.r(), scale=-1.0)
                act(FF.ap, FF.ap, AF.Ln, FF.r(), FF.r(), bias=1.0)
                act(FF.ap, FF.ap, AF.Exp, FF.r(), FF.r(), scale=-1.0)
                ts(DVE, FF.ap, FF.ap, OML.ap[:, j:j + 1], LB.ap[:, j:j + 1], ALU.mult, ALU.add, FF.r() + OML.r() + LB.r(), FF.r())
                act(LF.ap, FF.ap, AF.Ln, FF.r(), LF.r())
                scan(BB.ap, R64, LF.ap, 0.0, LF.r() + CST.r(), BB.r())
                act(E1.ap, BB.ap, AF.Exp, BB.r(), E1.r())
                act(E2.ap, BB.ap, AF.Exp, BB.r(), E2.r(), scale=-1.0)
                cp(POOL, EL.ap[:, ci, :], E1.ap.rearrange("p (c u) -> p c u", u=64)[:, :, 63], E1.r(), EL.r(ci))
                tt(DVE, QT.ap[:, ci, :], QS.ap[:, ci, :], E1.ap, ALU.mult, QS.r(ci) + E1.r(), QT.r(ci))
                ts(DVE, FF.ap, FF.ap, -1.0, 1.0, ALU.mult, ALU.add, FF.r(), FF.r())
                tt(DVE, KT.ap[:, ci, :], FF.ap, E2.ap, ALU.mult, FF.r() + E2.r(), KT.r(ci))
                tt(POOL, KD.ap.rearrange("p (c u) -> p c u", u=64), KT.ap[:, ci, :].rearrange("p (c u) -> p c u", u=64),
                   EL.ap[:, ci, :].unsqueeze(2).to_broadcast([128, 8, 64]), ALU.mult, KT.r(ci) + EL.r(ci), KD.r())
                yield
                pbt = bank()
                for s in range(4):
                    tr(PSB[pbt][:, s * 128:(s + 1) * 128], KD.ap[:, s * 128:(s + 1) * 128], IDB.ap, KD.r() + IDB.r(), PR[pbt])
                cp(ACT, KDT.ap[:, ci, :], PSB[pbt][:, 0:512], PR[pbt], KDT.r(ci))
            proj_fm(W, 3072 + hf * 512, 4, H, TT, c_f)
            if stop_after == "hgB":
                return

            def head_gen(jj, hf=hf):
                j = hf * 4 + jj
                OF = FA[j % 2][5]
                PT, OSQ, RSh = PTs[j % 2], OSQs[j % 2], RS2[j % 2]
                pbs = bank()
                for s in range(4):
                    sl_ = slice(s * 128, (s + 1) * 128)
                    mm(PS[pbs][:, sl_], KT.ap[:, jj, sl_], QT.ap[:, jj, sl_], s == 0, s == 3, KT.r(jj) + QT.r(jj), PR[pbs])
                tt(DVE, PT.ap, PS[pbs].rearrange("p (s t) -> p s t", s=4), MASKH.unsqueeze(1).to_broadcast([128, 4, 128]),
                   ALU.mult, PR[pbs] + CST.r(), PT.r())
                pd = [bank(), bank()]
                for c in range(8):
                    s, hh = c // 2, c % 2
                    mm(PS[pd[hh]][:, s * 128:(s + 1) * 128],
                       KDT.ap[hh * 64:(hh + 1) * 64, jj, s * 128:(s + 1) * 128],
                       VT.ap[hh * 64:(hh + 1) * 64, s, jj * 128:(jj + 1) * 128], s == 0, s == 3,
                       KDT.r(jj) + VT.r(s), PR[pd[hh]])
                SBj = SB[j % 2]
                cp(POOL, SBj.ap[:, 0, :], ST0.ap[:, j, :], ST0.r(j), SBj.r(0))
                yield
                po = bank()
                for s in range(4):
                    mm(PS[po][:, s * 128:(s + 1) * 128], VT.ap[:, s, jj * 128:(jj + 1) * 128], PT.ap[:, s, :], s == 0, False,
                       VT.r(s) + PT.r(), PR[po])
                for c in range(8):
                    mm(PS[po][:, c * 64:(c + 1) * 64], SBj.ap[:, c, :], QT.ap[:, jj, c * 64:(c + 1) * 64], False, c == 7,
                       SBj.r(c) + QT.r(jj), PR[po])
                    stt(ST0.ap[:, j, :], ST0.ap[:, j, :], EL.ap[:, jj, c:c + 1], PS[pd[c % 2]][:, (c // 2) * 128:(c // 2 + 1) * 128],
                        ALU.mult, ALU.add, ST0.r(j) + EL.r(jj) + PR[pd[c % 2]], ST0.r(j))
                    if c < 7:
                        cp(ACT, SBj.ap[:, c + 1, :], ST0.ap[:, j, :], ST0.r(j), SBj.r(c + 1))
                    yield
                act(OF.ap, PS[po], AF.Copy, PR[po], OF.r())
                act(OSQ.ap, OF.ap, AF.Square, OF.r(), OSQ.r())
                pn = bank()
                mm(PS[pn], ONES.ap, OSQ.ap, True, True, ONES.r() + OSQ.r(), PR[pn])
                yield
                act(RSh.ap, PS[pn], AF.Ln, PR[pn] + EPSC.r(), RSh.r(), bias=EPSC.ap[:, 0:1], scale=1.0 / 128)
                act(RSh.ap, RSh.ap, AF.Exp, RSh.r(), RSh.r(), scale=-0.5)
                stt(OF.ap, OF.ap, pc("bng", j), RSh.ap, ALU.mult, ALU.mult, OF.r() + RSh.r() + PC.r(), OF.r())
                tt(POOL, Y.ap[:, 8 + j, :], OF.ap, GATE2.ap[:, jj, :], ALU.mult, OF.r() + GATE2.r(jj), Y.r(8 + j))
            interleave([head_gen(0), head_gen(1)])
            interleave([head_gen(2), head_gen(3)])
        dump("yb", Y.ap[:, 8:16, :], Y.r(8, 16))
        outproj_add("e_w_out", Y, 16)

    def xattn(l):
        kb.off = pers_off
        QA = alloc([8, TT], BF16)
        EX = [alloc([2, TT], BF16) for _ in range(2)]
        RC = [alloc([TT]) for _ in range(2)]
        SQ = alloc([8, TT], BF16)
        norm_fm(X, TT, "gmq", l * 8, H, SQ)

        def c_q(ci, pb):
            act(QA.ap[:, ci, :], PS[pb], AF.Copy, PR[pb], QA.r(ci), scale=1.0 / 16.0)
        proj_fm(f"xq{l}", 0, 8, H, TT, c_q)
        for hd in range(4):
            EXh, RCh = EX[hd % 2], RC[hd % 2]
            for mh in range(2):
                pb = bank()
                for dc in range(2):
                    mm(PS[pb], KF[l].ap[:, 2 * hd + dc, mh * 128:(mh + 1) * 128], QA.ap[:, 2 * hd + dc, :], dc == 0, dc == 1,
                       KF[l].r(2 * hd + dc) + QA.r(2 * hd + dc), PR[pb])
                act(EXh.ap[:, mh, :], PS[pb], AF.Exp, PR[pb], EXh.r(mh))
            pb = bank()
            for mh in range(2):
                mm(PS[pb], ONES.ap, EXh.ap[:, mh, :], mh == 0, mh == 1, ONES.r() + EXh.r(mh), PR[pb])
            recip(RCh.ap, PS[pb], PR[pb], RCh.r())
            for dc in range(2):
                pb = bank()
                c = 2 * hd + dc
                for mh in range(2):
                    mm(PS[pb], VM[l].ap[:, mh, c * 128:(c + 1) * 128], EXh.ap[:, mh, :], mh == 0, mh == 1,
                       VM[l].r(mh) + EXh.r(mh), PR[pb])
                tt(DVE, Y.ap[:, c, :], PS[pb], RCh.ap, ALU.mult, PR[pb] + RCh.r(), Y.r(c))
        outproj_add(f"xo{l}", Y, 8)

    def ffn(l):
        kb.off = pers_off
        UP = alloc([32, TT], BF16)
        RL = [alloc([TT]) for _ in range(2)]
        SQ = alloc([8, TT], BF16)
        norm_fm(X, TT, "gffn", l * 8, H, SQ)

        def c_up(ci, pb):
            R_ = RL[ci % 2]
            act(R_.ap, PS[pb], AF.Relu, PR[pb], R_.r())
            tt(DVE, UP.ap[:, ci, :], R_.ap, PS[pb], ALU.mult, R_.r() + PR[pb], UP.r(ci))
        proj_fm(f"w1_{l}", 0, 32, H, TT, c_up)
        for c in range(8):
            ws = wload(f"w2_{l}", 0, 32, c * 128, 128)
            pb = bank()
            for k in range(32):
                mm(PS[pb], ws.ap[:, k, :], UP.ap[:, k, :], k == 0, k == 31, ws.regs + UP.r(k), PR[pb])
            tt(DVE, X.ap[:, c, :], X.ap[:, c, :], PS[pb], ALU.add, X.r(c) + PR[pb], X.r(c))

    def l1_mixer(t):
        kb.off = pers_off
        DT = alloc([TT])
        ACU = alloc([TT])
        DTT = alloc([4, 32])
        NAT = alloc([4, 32])
        m1 = kb.off
        SQ = alloc([8, TT], BF16)
        norm_fm(X, TT, "gmix", 8, H, SQ)
        kb.off = m1
        DA = alloc([TT])
        XTM = alloc([4, 1024])
        ZS = alloc([4, 1024], BF16)
        BF_ = alloc([4, TT], BF16)
        CF_ = alloc([4, TT], BF16)
        BT = alloc([4, 512], BF16)
        m2 = kb.off
        XA = [alloc([516]) for _ in range(3)]
        XC = [alloc([TT]) for _ in range(3)]
        kb.off = m2
        WS_ = alloc([16])
        ALB = alloc([16])
        EAL = alloc([16])
        SS = alloc([4])
        LL = [alloc([4, 128]) for _ in range(2)]
        LL2 = [alloc([4, 128]) for _ in range(2)]
        CBM = [alloc([128]) for _ in range(2)]
        MT = [alloc([4, 128], BF16) for _ in range(2)]
        CT = [alloc([4, 128], BF16) for _ in range(2)]
        XDT = alloc([1024], BF16)
        XDW = alloc([1024], BF16)
        STB = alloc([1024], BF16)
        YA = alloc([1024])
        YB = alloc([1024])
        YN = alloc([1024], BF16)
        W = "o_w_in"
        ws = wload(W, 0, 8, 6144, 32)
        pb = bank()
        for k in range(8):
            mm(PS[pb][0:32, :], ws.ap[:, k, :], H.ap[:, k, :], k == 0, k == 7, ws.regs + H.r(k), PR[pb])
        act(DT.ap[0:32, :], PS[pb][0:32, :], AF.Exp, PR[pb] + PC.r(), DT.r(), bias=PC.ap[0:32, PCI["dtb"]:PCI["dtb"] + 1])
        act(DT.ap[0:32, :], DT.ap[0:32, :], AF.Ln, DT.r(), DT.r(), bias=1.0)
        ts(DVE, DA.ap[0:32, :], DT.ap[0:32, :], ANEG.ap[0:32, 0:1], None, ALU.mult, None, DT.r() + ANEG.r(), DA.r())
        scan(ACU.ap[0:32, :], R128[0:32, :], DA.ap[0:32, :], 0.0, DA.r() + CST.r(), ACU.r())
        pb = bank()
        for s in range(4):
            tr(PS[pb][:, s * 32:(s + 1) * 32], DT.ap[0:32, s * 128:(s + 1) * 128], IDF[0:32, 0:32], DT.r() + CST.r(), PR[pb])
        cp(DVE, DTT.ap, PS[pb][:, 0:128].rearrange("p (s h) -> p s h", s=4), PR[pb], DTT.r())
        pb = bank()
        for s in range(4):
            tr(PS[pb][:, s * 32:(s + 1) * 32], ACU.ap[0:32, s * 128:(s + 1) * 128], IDF[0:32, 0:32], ACU.r() + CST.r(), PR[pb])
        ts(DVE, NAT.ap, PS[pb][:, 0:128].rearrange("p (s h) -> p s h", s=4), -1.0, None, ALU.mult, None, PR[pb], NAT.r())

        YBANKS = (0, 1)
        OTH = (2, 3, 4, 5, 6, 7)
        for hf in range(2):
            for sl in range(2):
                ws = wload(W, 0, 8, hf * 1024 + sl * 512, 512)
                for s in range(4):
                    pb = bank()
                    for k in range(8):
                        mm(PS[pb], H.ap[:, k, s * 128:(s + 1) * 128], ws.ap[:, k, :], k == 0, k == 7, ws.regs + H.r(k), PR[pb])
                    act(ZS.ap[:, s, sl * 512:(sl + 1) * 512], PS[pb], AF.Silu, PR[pb],
                        ZS.rb(s * 1024 + sl * 512, s * 1024 + sl * 512 + 512))

            def c_x(ci, pb, hf=hf):
                ch = hf * 8 + ci
                XAb, XCb = XA[ci % 3], XC[ci % 3]
                conv_chunk(pb, XAb, CAR1, ch, "mcw", 32, "mcb", XCb.ap, XCb.r())
                yield
                act(XCb.ap, XCb.ap, AF.Silu, XCb.r(), XCb.r())
                yield
                pbt = bank()
                for s in range(4):
                    tr(PS[pbt][:, s * 128:(s + 1) * 128], XCb.ap[:, s * 128:(s + 1) * 128], IDF, XCb.r() + CST.r(), PR[pbt])
                cp(ACT, XTM.ap[:, :, ci * 128:(ci + 1) * 128], PS[pbt].rearrange("p (s f) -> p s f", s=4), PR[pbt], XTM.r())
            proj_fm(W, 2048 + hf * 1024, 8, H, TT, c_x)

            def c_b(gi, pb, hf=hf):
                ch = 16 + hf * 4 + gi
                XAb, XCb = XA[gi % 3], XC[gi % 3]
                conv_chunk(pb, XAb, CAR1, ch, "mcw", 32, "mcb", XCb.ap, XCb.r())
                yield
                act(BF_.ap[:, gi, :], XCb.ap, AF.Silu, XCb.r(), BF_.r(gi))
                yield
                pbt = bank()
                for s in range(4):
                    tr(PSB[pbt][:, s * 128:(s + 1) * 128], BF_.ap[:, gi, s * 128:(s + 1) * 128], IDB.ap, BF_.r(gi) + IDB.r(), PR[pbt])
                cp(ACT, BT.ap[:, :, gi * 128:(gi + 1) * 128], PSB[pbt][:, 0:512].rearrange("p (s f) -> p s f", s=4), PR[pbt], BT.r())
            proj_fm(W, 4096 + hf * 512, 4, H, TT, c_b)

            def c_c(gi, pb, hf=hf):
                ch = 24 + hf * 4 + gi
                XAb, XCb = XA[gi % 3], XC[gi % 3]
                conv_chunk(pb, XAb, CAR1, ch, "mcw", 32, "mcb", XCb.ap, XCb.r())
                yield
                act(CF_.ap[:, gi, :], XCb.ap, AF.Silu, XCb.r(), CF_.r(gi))
            proj_fm(W, 5120 + hf * 512, 4, H, TT, c_c)

            H0 = hf * 16
            prevB = None
            for s in range(4):
                cs = slice(s * 128, (s + 1) * 128)
                tt(DVE, XDT.ap.rearrange("p (h q) -> p h q", q=64), XTM.ap[:, s, :].rearrange("p (h q) -> p h q", q=64),
                   DTT.ap[:, s, H0:H0 + 16].unsqueeze(2).to_broadcast([128, 16, 64]), ALU.mult, XTM.r(s) + DTT.r(), XDT.r())
                cp(ACT, STB.ap, ST1.ap[:, hf * 1024:(hf + 1) * 1024], ST1.rb(hf * 1024, hf * 1024 + 1024), STB.r())
                def grp_gen(gl, s=s, cs=cs, H0=H0):
                    i2 = gl % 2
                    pcb = bank(OTH)
                    mm(PS[pcb][:, 0:128], BF_.ap[:, gl, cs], CF_.ap[:, gl, cs], True, True, BF_.r(gl) + CF_.r(gl), PR[pcb])
                    tt(DVE, CBM[i2].ap, PS[pcb][:, 0:128], MASKC, ALU.mult, PR[pcb] + CST.r(), CBM[i2].r())
                    pa = bank(OTH)
                    for hh in range(4):
                        h = H0 + 4 * gl + hh
                        mm(PS[pa][:, hh * 128:(hh + 1) * 128], IDF[0:32, h:h + 1].to_broadcast([32, 128]), ACU.ap[0:32, cs],
                           hh == 0, hh == 3, CST.r() + ACU.r(), PR[pa])
                    for hh in range(4):
                        h = H0 + 4 * gl + hh
                        act(LL[i2].ap[:, hh, :], PS[pa][:, hh * 128:(hh + 1) * 128], AF.Exp, PR[pa] + NAT.r(), LL[i2].r(hh),
                            bias=NAT.ap[:, s, h:h + 1])
                    stt(MT[i2].ap, LL[i2].ap, 1.0, CBM[i2].ap.unsqueeze(1).to_broadcast([128, 4, 128]), ALU.min, ALU.mult,
                        LL[i2].r() + CBM[i2].r(), MT[i2].r())
                    pav = PS[pa].rearrange("p (h t) -> p h t", h=4)
                    cp(DVE, ALB.ap[:, 4 * gl:4 * gl + 4], pav[:, :, 127], PR[pa], ALB.r())
                    act(LL2[i2].ap, pav, AF.Exp, PR[pa], LL2[i2].r())
                    tt(DVE, CT[i2].ap, LL2[i2].ap, CF_.ap[:, gl, cs].unsqueeze(1).to_broadcast([128, 4, 128]), ALU.mult,
                       LL2[i2].r() + CF_.r(gl), CT[i2].r())
                    yield
                    for hh in range(4):
                        hl = 4 * gl + hh
                        yb = YBANKS[hl // 8]
                        oc = slice((hl % 8) * 64, (hl % 8 + 1) * 64)
                        mm(PS[yb][:, oc], MT[i2].ap[:, hh, :], XDT.ap[:, hl * 64:(hl + 1) * 64], (hl % 8 == 0), False,
                           MT[i2].r(hh) + XDT.r(), PR[yb])
                        mm(PS[yb][:, oc], CT[i2].ap[:, hh, :], STB.ap[:, hl * 64:(hl + 1) * 64], False, (hl % 8 == 7),
                           CT[i2].r(hh) + STB.r(), PR[yb])
                pend_ = [prevB] if prevB is not None else []
                for gl in range(4):
                    g_ = grp_gen(gl)
                    next(g_)
                    step_all(pend_)
                    pend_.append(g_)
                drain(pend_)
                STh = ST1.ap[:, hf * 1024:(hf + 1) * 1024]
                STr = ST1.rb(hf * 1024, hf * 1024 + 1024)
                act(EAL.ap, ALB.ap, AF.Exp, ALB.r(), EAL.r())
                tt(DVE, WS_.ap, ALB.ap, NAT.ap[:, s, H0:H0 + 16], ALU.add, ALB.r() + NAT.r(), WS_.r())
                act(WS_.ap, WS_.ap, AF.Exp, WS_.r(), WS_.r())
                tt(DVE, WS_.ap, WS_.ap, DTT.ap[:, s, H0:H0 + 16], ALU.mult, WS_.r() + DTT.r(), WS_.r())
                tt(DVE, XDW.ap.rearrange("p (h q) -> p h q", q=64), XTM.ap[:, s, :].rearrange("p (h q) -> p h q", q=64),
                   WS_.ap.unsqueeze(2).to_broadcast([128, 16, 64]), ALU.mult, XTM.r(s) + WS_.r(), XDW.r())
                tt(POOL, STh.rearrange("p (h q) -> p h q", q=64), STh.rearrange("p (h q) -> p h q", q=64),
                   EAL.ap.unsqueeze(2).to_broadcast([128, 16, 64]), ALU.mult, STr + EAL.r(), STr)
                for half in range(2):
                    pd = bank(OTH)
                    for gg in range(2):
                        gl = half * 2 + gg
                        mm(PS[pd][:, gg * 256:(gg + 1) * 256], BT.ap[:, s, gl * 128:(gl + 1) * 128], XDW.ap[:, gl * 256:(gl + 1) * 256],
                           gg == 0, gg == 1, BT.r(s) + XDW.r(), PR[pd])
                    sl_ = slice(half * 512, (half + 1) * 512)
                    tt(DVE, STh[:, sl_], STh[:, sl_], PS[pd], ALU.add, STr + PR[pd], STr)
                tt(POOL, YA.ap.rearrange("p (h q) -> p h q", q=64), XTM.ap[:, s, :].rearrange("p (h q) -> p h q", q=64),
                   DBC.ap[:, H0:H0 + 16].unsqueeze(2).to_broadcast([128, 16, 64]), ALU.mult, XTM.r(s) + DBC.r(), YA.r())
                for q in range(2):
                    sl_ = slice(q * 512, (q + 1) * 512)
                    tt(DVE, YA.ap[:, sl_], YA.ap[:, sl_], PS[YBANKS[q]], ALU.add, YA.r() + PR[YBANKS[q]], YA.r())

                def fin_gen(s=s, cs=cs, hf=hf):
                    tt(POOL, YA.ap, YA.ap, ZS.ap[:, s, :], ALU.mult, YA.r() + ZS.r(s), YA.r())
                    yield
                    for g4 in range(4):
                        op(ACT, lambda g4=g4: nc.scalar.activation(out=YB.ap[:, g4 * 256:(g4 + 1) * 256],
                                                                    in_=YA.ap[:, g4 * 256:(g4 + 1) * 256], func=AF.Square,
                                                                    accum_out=SS.ap[:, g4:g4 + 1]),
                           YA.r(), YB.r() + SS.r())
                    act(SS.ap, SS.ap, AF.Ln, SS.r() + EPSC.r(), SS.r(), bias=EPSC.ap[:, 0:1], scale=1.0 / 256)
                    act(SS.ap, SS.ap, AF.Exp, SS.r(), SS.r(), scale=-0.5)
                    yield
                    for g4 in range(4):
                        act(YA.ap[:, g4 * 256:(g4 + 1) * 256], YA.ap[:, g4 * 256:(g4 + 1) * 256], AF.Copy, YA.r() + SS.r(), YA.r(),
                            scale=SS.ap[:, g4:g4 + 1])
                    tt(POOL, YN.ap, YA.ap, NG.ap[:, hf * 1024:(hf + 1) * 1024], ALU.mult, YA.r() + NG.r(), YN.r())
                    yield
                    for q in range(2):
                        pbt = bank(OTH)
                        for cc in range(4):
                            c = q * 4 + cc
                            tr(PSB[pbt][:, cc * 128:(cc + 1) * 128], YN.ap[:, c * 128:(c + 1) * 128], IDB.ap, YN.r() + IDB.r(), PR[pbt])
                        c0 = hf * 8 + q * 4
                        cp(ACT, Y.ap[:, c0:c0 + 4, cs], PSB[pbt][:, 0:512].rearrange("p (c t) -> p c t", c=4), PR[pbt],
                           Y.r(c0, c0 + 4))
                        yield
                prevB = fin_gen()
            drain([prevB])
        dump("ym", Y.ap, Y.r())
        outproj_add("o_w_out", Y, 16)

    def final(t):
        kb.off = pers_off
        HF = alloc([8, TT])
        IO = alloc([4, 1024])
        SQ = alloc([8, TT], BF16)
        norm_fm(X, TT, "gfin", 0, HF, SQ)
        for s in range(4):
            for hf in range(2):
                pb = bank()
                for cc in range(4):
                    c = hf * 4 + cc
                    tr(PS[pb][:, cc * 128:(cc + 1) * 128], HF.ap[:, c, s * 128:(s + 1) * 128], IDF, HF.r(c) + CST.r(), PR[pb])
                cp(ACT if hf else DVE, IO.ap[:, s, hf * 512:(hf + 1) * 512], PS[pb], PR[pb], IO.r(s))
        dma(POOL, T_OUT, out_d[t * TT:(t + 1) * TT, :].rearrange("(s p) f -> p s f", p=128), IO.ap, IO.r(), [])

    stages = ["l0", "a0", "f0", "l1", "a1", "f1"]
    for t in range(NT):
        load_x(t)
        for st in stages:
            if stop_after == "load":
                break
            if st == "l0":
                l0_mixer(t)
            elif st == "a0":
                if t == 0:
                    mem_kv()
                xattn(0)
            elif st == "f0":
                ffn(0)
            elif st == "l1":
                l1_mixer(t)
            elif st == "a1":
                xattn(1)
            elif st == "f1":
                ffn(1)
            if stop_after == st or (stop_after in ("rglru", "hgA", "hgB") and st == "l0"):
                break
        final(t)
    nc.gpsimd.wait_ge(T_OUT.sem, T_OUT.n * 16)
    for t_ in kb.dbgt:
        nc.gpsimd.wait_ge(t_.sem, t_.n * 16)
    es.close()
    kb.counts = counts
    kb.wneed = wneed
    return nc, kb


def _cols(v):
    v = np.asarray(v, np.float32).reshape(-1)
    return np.ascontiguousarray(v.reshape(v.size // 128, 128).T)


def make_shared_inputs(inp):
    pc = np.zeros((128, NPC), np.float32)

    def put(name, arr):
        a = _cols(arr)
        pc[:, PCI[name]:PCI[name] + a.shape[1]] = a
    put("gmix", inp["norm_mix_g"])
    put("gmq", inp["norm_mem_q_g"])
    put("gmkv", inp["norm_mem_kv_g"])
    put("gffn", inp["norm_ffn_g"])
    put("gfin", inp["final_norm_g"])
    put("acw", inp["a_conv_w"][0])
    put("acb", inp["a_conv_b"][0])
    put("arb", inp["a_gate_r_b"][0])
    put("aib", inp["a_gate_i_b"][0])
    put("alam", inp["a_lambda"][0])
    put("lbl", inp["b_lb_logits"])
    put("bng", inp["b_norm_g"][0])
    put("mcw", inp["m_conv_w"][0])
    put("mcb", inp["m_conv_b"][0])
    pc[:32, PCI["dtb"]] = np.asarray(inp["m_dt_bias"][0], np.float32)
    pc[:32, PCI["alog"]] = np.asarray(inp["m_a_log"][0], np.float32)
    cst = np.zeros((128, NCST), np.float32)
    cst[:, C_ID:C_ID + 128] = np.eye(128, dtype=np.float32)
    s_ = np.arange(128)[:, None]
    t_ = np.arange(128)[None, :]
    cst[:, C_MH:C_MH + 128] = ((s_ <= t_) & ((s_ // 64) == (t_ // 64))).astype(np.float32)
    cst[:, C_MC:C_MC + 128] = (s_ <= t_).astype(np.float32)
    cst[:, C_R64:C_R64 + 512] = (np.arange(512) % 64 != 0).astype(np.float32)[None, :]
    cst[:, C_R128:C_R128 + 512] = (np.arange(512) % 128 != 0).astype(np.float32)[None, :]
    sh = {"pc": pc, "cst": cst}
    sh["gr"] = np.ascontiguousarray(np.asarray(inp["a_gate_r_w"][0], np.float32).transpose(1, 0, 2).reshape(128, 1024))
    sh["gi"] = np.ascontiguousarray(np.asarray(inp["a_gate_i_w"][0], np.float32).transpose(1, 0, 2).reshape(128, 1024))
    sh["mng"] = np.asarray(inp["m_norm_g"], np.float32).reshape(1, 2048)
    sh["md"] = np.repeat(np.asarray(inp["m_d"], np.float32).reshape(1, 32), 1, axis=0)
    sh["e_w_in"] = np.asarray(inp["e_w_in"][0], np.float32)
    sh["e_w_out"] = np.asarray(inp["e_w_out"][0], np.float32)
    sh["o_w_in"] = np.asarray(inp["o_w_in"][0], np.float32)
    sh["o_w_out"] = np.asarray(inp["o_w_out"][0], np.float32)
    for l in range(2):
        sh[f"xq{l}"] = np.asarray(inp["xq_w"][l], np.float32)
        sh[f"xk{l}"] = np.asarray(inp["xk_w"][l], np.float32)
        sh[f"xv{l}"] = np.asarray(inp["xv_w"][l], np.float32)
        sh[f"xo{l}"] = np.asarray(inp["xo_w"][l], np.float32)
        sh[f"w1_{l}"] = np.asarray(inp["ffn_w1"][l], np.float32)
        sh[f"w2_{l}"] = np.asarray(inp["ffn_w2"][l], np.float32)
    return sh


_CACHE = {}


def kernel(**inputs):
    x = np.asarray(inputs["x"], np.float32)
    mem = np.asarray(inputs["mem"], np.float32)
    B, S, _ = x.shape
    if S not in _CACHE:
        _CACHE[S] = build(S)[0]
    nc = _CACHE[S]
    sh = make_shared_inputs(inputs)
    in_maps = []
    for b in range(B):
        m = dict(sh)
        m["x"] = np.ascontiguousarray(x[b])
        m["mem"] = np.ascontiguousarray(mem[b])
        in_maps.append(m)
    res = run_bass_kernel_spmd(nc, in_maps, core_ids=list(range(B)))
    return np.stack([np.asarray(r["out"], np.float32) for r in res.results], axis=0)
```
